# Optimizing a Trainium2 kernel written in Bass

```python
import math
import jax, jax.numpy as jnp
from jax import lax
import numpy as np

D_MODEL = 4096
BATCH = 4
SEQ = 2048
DEPTH = 1
DEC_BATCH = 128
DEC_SEQ = 8
PAST_LEN = 16384
PAGE_SIZE = 128

D_LRU = D_MODEL // 2
LRU_HEADS = 16
LRU_BLK = D_LRU // LRU_HEADS
CONV_W = 4
C_RG = 8.0
D_SSM = D_MODEL // 2
SSM_CG = 16
SSM_GROUPS = D_SSM // SSM_CG
SSM_P = 64
D_FF = ((8 * D_MODEL // 3) + 255) // 256 * 256
D_IN = 2 * D_LRU + D_SSM + 2 * D_MODEL
EPS = 1e-6

kernel_name = 'hawk_s5_macaron_sandwich_decode_step'


def _rms_norm(x, g):
    xf = x.astype(jnp.float32)
    y = xf * lax.rsqrt(jnp.mean(xf * xf, axis=-1, keepdims=True) + EPS)
    return (y * g.astype(jnp.float32)).astype(x.dtype)


def _half_ffn(x, pre_g, post_g, w_gate, w_up, w_down):
    h = _rms_norm(x, pre_g)
    f = (jax.nn.silu(h @ w_gate) * (h @ w_up)) @ w_down
    return x + 0.5 * _rms_norm(f, post_g)


def _lin_combine(e1, e2):
    a1, b1 = e1
    a2, b2 = e2
    return a1 * a2, a2 * b1 + b2


def _cplx_combine(e1, e2):
    ar1, ai1, br1, bi1 = e1
    ar2, ai2, br2, bi2 = e2
    return (ar1 * ar2 - ai1 * ai2,
            ar1 * ai2 + ai1 * ar2,
            ar2 * br1 - ai2 * bi1 + br2,
            ar2 * bi1 + ai2 * br1 + bi2)


def _rg_lru_branch(xa, conv_buf, h0, conv_w, conv_b, w_rg, b_rg, w_ig, b_ig, lam):
    n, t, _ = xa.shape
    xp = jnp.concatenate([conv_buf.astype(xa.dtype), xa], axis=1)
    new_buf = xp[:, -(CONV_W - 1):]
    xc = conv_b + xp[:, 0:t] * conv_w[0]
    for k in range(1, CONV_W):
        xc = xc + xp[:, k:k + t] * conv_w[k]
    xh = xc.reshape(n, t, LRU_HEADS, LRU_BLK)
    r = jax.nn.sigmoid((jnp.einsum('nthi,hij->nthj', xh, w_rg).reshape(n, t, D_LRU) + b_rg).astype(jnp.float32))
    i = jax.nn.sigmoid((jnp.einsum('nthi,hij->nthj', xh, w_ig).reshape(n, t, D_LRU) + b_ig).astype(jnp.float32))
    log_a = -C_RG * r * jax.nn.softplus(-lam.astype(jnp.float32))
    a = jnp.exp(log_a)
    mult = jnp.sqrt(-jnp.expm1(2.0 * log_a))
    b = mult * (i * xc.astype(jnp.float32))
    a_t = jnp.swapaxes(a, 0, 1)
    b_t = jnp.swapaxes(b, 0, 1)
    b_t = b_t.at[0].add(a_t[0] * h0.astype(jnp.float32))
    _, h = lax.associative_scan(_lin_combine, (a_t, b_t), axis=0)
    y = jnp.swapaxes(h, 0, 1).astype(xa.dtype)
    return y, h[-1], new_buf


def _s5_branch(u, s_re, s_im, a_re, a_im, log_dt, b_re, b_im, c_re, c_im, d_skip, w_glu, b_glu):
    f32 = jnp.float32
    n, t, _ = u.shape
    a_re = a_re.astype(f32)
    a_im = a_im.astype(f32)
    dt = jnp.exp(log_dt.astype(f32))[:, None]
    mag = jnp.exp(a_re * dt)
    abar_re = mag * jnp.cos(a_im * dt)
    abar_im = mag * jnp.sin(a_im * dt)
    den = a_re * a_re + a_im * a_im
    nr = abar_re - 1.0
    ni = abar_im
    coef_re = ((nr * a_re + ni * a_im) / den)[..., None]
    coef_im = ((ni * a_re - nr * a_im) / den)[..., None]
    b_re = b_re.astype(f32)
    b_im = b_im.astype(f32)
    bb_re = coef_re * b_re - coef_im * b_im
    bb_im = coef_re * b_im + coef_im * b_re
    uf = u.astype(f32)
    ug = uf.reshape(n, t, SSM_GROUPS, SSM_CG)
    bu_re = jnp.einsum('ntgc,gpc->tngp', ug, bb_re)
    bu_im = jnp.einsum('ntgc,gpc->tngp', ug, bb_im)
    h0r = s_re.astype(f32)
    h0i = s_im.astype(f32)
    bu_re = bu_re.at[0].add(abar_re * h0r - abar_im * h0i)
    bu_im = bu_im.at[0].add(abar_re * h0i + abar_im * h0r)
    ar_t = jnp.broadcast_to(abar_re, (t, 1, SSM_GROUPS, SSM_P))
    ai_t = jnp.broadcast_to(abar_im, (t, 1, SSM_GROUPS, SSM_P))
    _, _, h_re, h_im = lax.associative_scan(_cplx_combine, (ar_t, ai_t, bu_re, bu_im), axis=0)
    y = (jnp.einsum('tngp,gcp->ntgc', h_re, c_re.astype(f32))
         - jnp.einsum('tngp,gcp->ntgc', h_im, c_im.astype(f32)))
    y = y.reshape(n, t, D_SSM) + d_skip.astype(f32) * uf
    g = jax.nn.gelu(y)
    out = g * jax.nn.sigmoid(g @ w_glu.astype(f32) + b_glu.astype(f32))
    return out.astype(u.dtype), h_re[-1], h_im[-1]


def _decoder_layer(x, lru_h, conv_buf, ssm_re, ssm_im, p):
    x = _half_ffn(x, p['ffn1_pre_g'], p['ffn1_post_g'], p['ffn1_w_gate'], p['ffn1_w_up'], p['ffn1_w_down'])
    u = _rms_norm(x, p['mix_pre_g'])
    z = u @ p['w_in']
    xa = z[..., :D_LRU]
    ga = z[..., D_LRU:2 * D_LRU]
    xb = z[..., 2 * D_LRU:2 * D_LRU + D_SSM]
    gl = z[..., 2 * D_LRU + D_SSM:]
    ya, new_h, new_buf = _rg_lru_branch(xa, conv_buf, lru_h, p['conv_w'], p['conv_b'], p['w_rg'], p['b_rg'],
                                        p['w_ig'], p['b_ig'], p['lru_lambda'])
    ya = ya * jax.nn.gelu(ga)
    yb, new_re, new_im = _s5_branch(xb, ssm_re, ssm_im, p['ssm_a_re'], p['ssm_a_im'], p['ssm_log_dt'],
                                    p['ssm_b_re'], p['ssm_b_im'], p['ssm_c_re'], p['ssm_c_im'], p['ssm_d'],
                                    p['w_glu'], p['b_glu'])
    gates = jax.nn.sigmoid(gl.astype(jnp.float32)).astype(x.dtype)
    merged = gates[..., :D_MODEL] * (ya @ p['w_out_a']) + gates[..., D_MODEL:] * (yb @ p['w_out_b'])
    x = x + _rms_norm(merged @ p['w_o'], p['mix_post_g'])
    x = _half_ffn(x, p['ffn2_pre_g'], p['ffn2_post_g'], p['ffn2_w_gate'], p['ffn2_w_up'], p['ffn2_w_down'])
    return x, new_h, new_buf, new_re, new_im


def setup_inputs(seed: int = 0) -> dict:
    key = jax.random.key(seed)
    ks = iter(jax.random.split(key, 48))
    L = DEPTH
    f32 = jnp.float32

    def nrm(shape, scale):
        return jax.random.normal(next(ks), shape, f32) * scale

    def gain(shape):
        return 1.0 + 0.05 * jax.random.normal(next(ks), shape, f32)

    d = {}
    d['x_prompt'] = nrm((BATCH, SEQ, D_MODEL), 1.0)
    d['x_sample'] = nrm((DEC_BATCH, DEC_SEQ, D_MODEL), 1.0)
    d['state_lru_h'] = nrm((L, DEC_BATCH, D_LRU), 0.5)
    d['state_conv'] = nrm((L, DEC_BATCH, CONV_W - 1, D_LRU), 1.0)
    d['state_ssm_re'] = nrm((L, DEC_BATCH, SSM_GROUPS, SSM_P), 0.1)
    d['state_ssm_im'] = nrm((L, DEC_BATCH, SSM_GROUPS, SSM_P), 0.1)
    d['ffn1_pre_g'] = gain((L, D_MODEL))
    d['ffn1_post_g'] = gain((L, D_MODEL))
    d['ffn1_w_gate'] = nrm((L, D_MODEL, D_FF), D_MODEL ** -0.5)
    d['ffn1_w_up'] = nrm((L, D_MODEL, D_FF), D_MODEL ** -0.5)
    d['ffn1_w_down'] = nrm((L, D_FF, D_MODEL), D_FF ** -0.5)
    d['mix_pre_g'] = gain((L, D_MODEL))
    d['mix_post_g'] = gain((L, D_MODEL))
    d['w_in'] = nrm((L, D_MODEL, D_IN), D_MODEL ** -0.5)
    d['conv_w'] = nrm((L, CONV_W, D_LRU), CONV_W ** -0.5)
    d['conv_b'] = nrm((L, D_LRU), 0.02)
    d['w_rg'] = nrm((L, LRU_HEADS, LRU_BLK, LRU_BLK), LRU_BLK ** -0.5)
    d['b_rg'] = nrm((L, D_LRU), 0.02)
    d['w_ig'] = nrm((L, LRU_HEADS, LRU_BLK, LRU_BLK), LRU_BLK ** -0.5)
    d['b_ig'] = nrm((L, D_LRU), 0.02)
    a_pow = jax.random.uniform(next(ks), (L, D_LRU), f32, minval=0.9, maxval=0.999)
    s = a_pow ** (1.0 / C_RG)
    d['lru_lambda'] = jnp.log(s) - jnp.log1p(-s)
    d['ssm_a_re'] = -0.5 + 0.01 * jax.random.normal(next(ks), (L, SSM_GROUPS, SSM_P), f32)
    d['ssm_a_im'] = (math.pi * jnp.arange(SSM_P, dtype=f32))[None, None, :] + 0.01 * jax.random.normal(next(ks), (L, SSM_GROUPS, SSM_P), f32)
    d['ssm_log_dt'] = jax.random.uniform(next(ks), (L, SSM_GROUPS), f32, minval=math.log(1e-3), maxval=math.log(1e-1))
    d['ssm_b_re'] = nrm((L, SSM_GROUPS, SSM_P, SSM_CG), (2 * SSM_CG) ** -0.5)
    d['ssm_b_im'] = nrm((L, SSM_GROUPS, SSM_P, SSM_CG), (2 * SSM_CG) ** -0.5)
    d['ssm_c_re'] = nrm((L, SSM_GROUPS, SSM_CG, SSM_P), (2 * SSM_P) ** -0.5)
    d['ssm_c_im'] = nrm((L, SSM_GROUPS, SSM_CG, SSM_P), (2 * SSM_P) ** -0.5)
    d['ssm_d'] = nrm((L, D_SSM), 1.0)
    d['w_glu'] = nrm((L, D_SSM, D_SSM), D_SSM ** -0.5)
    d['b_glu'] = nrm((L, D_SSM), 0.02)
    d['w_out_a'] = nrm((L, D_LRU, D_MODEL), D_LRU ** -0.5)
    d['w_out_b'] = nrm((L, D_SSM, D_MODEL), D_SSM ** -0.5)
    d['w_o'] = nrm((L, D_MODEL, D_MODEL), D_MODEL ** -0.5)
    d['ffn2_pre_g'] = gain((L, D_MODEL))
    d['ffn2_post_g'] = gain((L, D_MODEL))
    d['ffn2_w_gate'] = nrm((L, D_MODEL, D_FF), D_MODEL ** -0.5)
    d['ffn2_w_up'] = nrm((L, D_MODEL, D_FF), D_MODEL ** -0.5)
    d['ffn2_w_down'] = nrm((L, D_FF, D_MODEL), D_FF ** -0.5)
    return d


def reference(x_prompt, x_sample, state_lru_h, state_conv, state_ssm_re, state_ssm_im,
              ffn1_pre_g, ffn1_post_g, ffn1_w_gate, ffn1_w_up, ffn1_w_down,
              mix_pre_g, mix_post_g, w_in, conv_w, conv_b, w_rg, b_rg, w_ig, b_ig, lru_lambda,
              ssm_a_re, ssm_a_im, ssm_log_dt, ssm_b_re, ssm_b_im, ssm_c_re, ssm_c_im, ssm_d,
              w_glu, b_glu, w_out_a, w_out_b, w_o,
              ffn2_pre_g, ffn2_post_g, ffn2_w_gate, ffn2_w_up, ffn2_w_down):
    n_p = x_prompt.shape[0]
    sdt = state_lru_h.dtype
    yp = x_prompt
    ys = x_sample
    p_h, p_c, p_re, p_im = [], [], [], []
    s_h, s_c, s_re, s_im = [], [], [], []
    for l in range(DEPTH):
        p = dict(ffn1_pre_g=ffn1_pre_g[l], ffn1_post_g=ffn1_post_g[l], ffn1_w_gate=ffn1_w_gate[l],
                 ffn1_w_up=ffn1_w_up[l], ffn1_w_down=ffn1_w_down[l],
                 mix_pre_g=mix_pre_g[l], mix_post_g=mix_post_g[l], w_in=w_in[l],
                 conv_w=conv_w[l], conv_b=conv_b[l], w_rg=w_rg[l], b_rg=b_rg[l], w_ig=w_ig[l], b_ig=b_ig[l],
                 lru_lambda=lru_lambda[l], ssm_a_re=ssm_a_re[l], ssm_a_im=ssm_a_im[l], ssm_log_dt=ssm_log_dt[l],
                 ssm_b_re=ssm_b_re[l], ssm_b_im=ssm_b_im[l], ssm_c_re=ssm_c_re[l], ssm_c_im=ssm_c_im[l],
                 ssm_d=ssm_d[l], w_glu=w_glu[l], b_glu=b_glu[l], w_out_a=w_out_a[l], w_out_b=w_out_b[l],
                 w_o=w_o[l], ffn2_pre_g=ffn2_pre_g[l], ffn2_post_g=ffn2_post_g[l], ffn2_w_gate=ffn2_w_gate[l],
                 ffn2_w_up=ffn2_w_up[l], ffn2_w_down=ffn2_w_down[l])
        yp, h1, c1, r1, i1 = _decoder_layer(
            yp, jnp.zeros((n_p, D_LRU), sdt), jnp.zeros((n_p, CONV_W - 1, D_LRU), sdt),
            jnp.zeros((n_p, SSM_GROUPS, SSM_P), sdt), jnp.zeros((n_p, SSM_GROUPS, SSM_P), sdt), p)
        ys, h2, c2, r2, i2 = _decoder_layer(
            ys, state_lru_h[l], state_conv[l], state_ssm_re[l], state_ssm_im[l], p)
        p_h.append(h1.astype(sdt)); p_c.append(c1.astype(sdt)); p_re.append(r1.astype(sdt)); p_im.append(i1.astype(sdt))
        s_h.append(h2.astype(sdt)); s_c.append(c2.astype(sdt)); s_re.append(r2.astype(sdt)); s_im.append(i2.astype(sdt))
    prompt_lru_h = jnp.stack(p_h, axis=0)
    prompt_conv = jnp.stack(p_c, axis=0)
    prompt_ssm_re = jnp.stack(p_re, axis=0)
    prompt_ssm_im = jnp.stack(p_im, axis=0)
    sample_lru_h = jnp.stack(s_h, axis=0)
    sample_conv = jnp.stack(s_c, axis=0)
    sample_ssm_re = jnp.stack(s_re, axis=0)
    sample_ssm_im = jnp.stack(s_im, axis=0)
    return (yp, ys, prompt_lru_h, prompt_conv, prompt_ssm_re, prompt_ssm_im,
            sample_lru_h, sample_conv, sample_ssm_re, sample_ssm_im)
```

```python
import math
import numpy as np
from contextlib import ExitStack
import concourse.bass as bass
import concourse.mybir as mybir
from concourse.bass_utils import run_bass_kernel_spmd

F32 = mybir.dt.float32
BF16 = mybir.dt.bfloat16
AF = mybir.ActivationFunctionType
ALU = mybir.AluOpType

D = 4096; DTL = 32; DFF = 11008; FFT = 86; DL = 2048; LT = 16
NT = 1152; TP = 1024; NSQ = 16; TS = 8; TB = 576; G = 128; SP_ = 64; CG = 16; NK = 144
EPS = 1e-6
WSLOT = 4096
ENGS = ("pe", "act", "dve", "pool", "sp")

C_G1PRE, C_G1POST, C_GMPRE, C_GMPOST, C_G2PRE, C_G2POST = 0, 32, 64, 96, 128, 160
C_CONVW = 192; C_CONVB = 256; C_BRG = 272; C_BIG = 288; C_LAM = 304; C_SSMD = 320; C_BGLU = 336; C_FLAG = 352
NCST = 353


class Tok:
    __slots__ = ("sem", "val", "eng")

    def __init__(self, sem, val, eng=None):
        self.sem = sem; self.val = val; self.eng = eng


class Buf:
    __slots__ = ("ap", "w", "r")

    def __init__(self, ap=None):
        self.ap = ap; self.w = None; self.r = []


class Ring:
    def __init__(self, aps):
        self.bufs = [Buf(a) for a in aps]; self.i = 0

    def get(self):
        b = self.bufs[self.i]; k = self.i
        self.i = (self.i + 1) % len(self.bufs)
        return k, b


class Prog:
    SEM_ROLL = 30000

    def __init__(self, nc, es):
        self.nc = nc; self.es = es
        self.streams = {e: [] for e in ENGS}
        self.cur_sem = {e: None for e in ENGS}
        self.cur_cnt = {e: 0 for e in ENGS}
        self.waited = {e: {} for e in ENGS}
        self.nsem = 0
        self.dma_sems = {}
        self.bar = []

    def new_sem(self, name):
        self.nsem += 1
        return self.es.enter_context(self.nc.semaphore(f"{name}_{self.nsem}"))

    def _waits(self, eng, deps):
        ws = []; w = self.waited[eng]
        for d in deps:
            if d is None:
                continue
            if isinstance(d, (list, tuple)):
                ws.extend(self._waits(eng, d)); continue
            if eng == "pe" and d.eng == "pe":
                continue
            k = id(d.sem)
            if w.get(k, -1) >= d.val:
                continue
            w[k] = d.val
            ws.append((d.sem, d.val))
        return ws

    def op(self, eng, fn, deps=(), signal=True):
        ws = self._waits(eng, deps)
        tok = None; inc = None
        if signal:
            if self.cur_sem[eng] is None or self.cur_cnt[eng] >= self.SEM_ROLL:
                self.cur_sem[eng] = self.new_sem(f"s_{eng}"); self.cur_cnt[eng] = 0
            self.cur_cnt[eng] += 1
            tok = Tok(self.cur_sem[eng], self.cur_cnt[eng], eng); inc = (self.cur_sem[eng], 1)
        self.streams[eng].append((fn, ws, inc))
        return tok

    def dma(self, eng, fn, key, deps=()):
        ws = self._waits(eng, deps)
        if key not in self.dma_sems:
            self.dma_sems[key] = [self.new_sem("d"), 0]
        ent = self.dma_sems[key]; ent[1] += 16
        self.streams[eng].append((fn, ws, (ent[0], 16)))
        return Tok(ent[0], ent[1])

    def wait_only(self, eng, deps):
        ws = self._waits(eng, deps)
        if ws:
            self.streams[eng].append((None, ws, None))

    def barrier(self, engs=("pe", "act", "dve", "sp")):
        toks = []
        for e in ENGS:
            if self.cur_sem[e] is not None:
                toks.append(Tok(self.cur_sem[e], self.cur_cnt[e]))
        for k, (s, v) in self.dma_sems.items():
            toks.append(Tok(s, v))
        for e in engs:
            self.wait_only(e, toks)
        self.bar = toks
        return toks

    def build(self, block):
        def run(e, items):
            for fn, ws, inc in items:
                for (s, v) in ws:
                    e.wait_ge(s, v)
                if fn is None:
                    continue
                ins = fn(e)
                if inc is not None:
                    ins.then_inc(inc[0], inc[1])

        @block.sync
        def _(e):
            run(e, self.streams["sp"])

        @block.scalar
        def _(e):
            run(e, self.streams["act"])

        @block.vector
        def _(e):
            run(e, self.streams["dve"])

        @block.gpsimd
        def _(e):
            run(e, self.streams["pool"])

        @block.tensor
        def _(e):
            run(e, self.streams["pe"])


class K:
    def __init__(self, nc, es):
        self.nc = nc; self.es = es; self.P = Prog(nc, es)
        self.nkey = 0

    def OP(self, eng, fn, reads=(), writes=(), extra=(), signal=True):
        deps = list(extra)
        for b in reads:
            deps.append(b.w)
        for b in writes:
            deps.append(b.w); deps.extend(b.r)
        t = self.P.op(eng, fn, deps=deps, signal=signal)
        if t is not None:
            for b in reads:
                b.r.append(t)
            for b in writes:
                b.w = t; b.r = []
        return t

    def DMA(self, q, out, in_, key, reads=(), writes=(), extra=()):
        deps = list(extra)
        for b in reads:
            deps.append(b.w)
        for b in writes:
            deps.append(b.w); deps.extend(b.r)
        t = self.P.dma(q, lambda e: e.dma_start(out=out, in_=in_), key, deps=deps)
        for b in reads:
            b.r.append(t)
        for b in writes:
            b.w = t; b.r = []
        return t

    def sb(self, st, name, shape, dt):
        self.nkey += 1
        return st.enter_context(self.nc.sbuf_tensor(f"{name}_s{self.nkey}", shape, dt))

    def ring(self, st, name, n, shape, dt):
        t = self.sb(st, name, [shape[0], n] + list(shape[1:]), dt)
        r = Ring([t[:, i] for i in range(n)])
        r.name = name
        return r

    def mm(self, out_buf, out_ap, pairs, reads=(), start=True, sgc=False, stop=True):
        n = len(pairs)
        deps = [out_buf.w] + list(out_buf.r)
        for b in reads:
            deps.append(b.w)
        tok = None
        for q, (l, r) in enumerate(pairs):
            tok = self.P.op("pe", (lambda e, l=l, r=r, q=q: e.matmul(out_ap, lhsT=l, rhs=r, start=(start and q == 0), stop=(stop and q == n - 1), skip_group_check=sgc)),
                            deps=deps if q == 0 else (), signal=(q == n - 1))
        for b in reads:
            b.r.append(tok)
        out_buf.w = tok; out_buf.r = []
        return tok


def chunks(n, m):
    k = (n + m - 1) // m
    base = n // k
    assert base * k == n
    return [(i * base, base) for i in range(k)]


def build_program():
    nc = bass.Bass("TRN2", target_bir_lowering=False)
    es = ExitStack()
    with es:
        kk = K(nc, es); P = kk.P; OP = kk.OP; DMA = kk.DMA

        def din(name, shape, dt=F32):
            return nc.dram_tensor(name, list(shape), dt, kind="ExternalInput").ap()

        def dout(name, shape, dt=F32):
            return nc.dram_tensor(name, list(shape), dt, kind="ExternalOutput").ap()

        def dscr(name, shape, dt=F32):
            return nc.dram_tensor(name, list(shape), dt, kind="Internal").ap()

        xT = din("xT", [DTL, 128, NT]); cst_d = din("cst", [128, NCST])
        wg1 = din("wg1", [FFT, 128, 4096]); wu1 = din("wu1", [FFT, 128, 4096]); wd1 = din("wd1", [DTL, 128, DFF])
        wg2 = din("wg2", [FFT, 128, 4096]); wu2 = din("wu2", [FFT, 128, 4096]); wd2 = din("wd2", [DTL, 128, DFF])
        win = din("win", [112, 128, 4096]); wrg_d = din("wrg", [128, LT * 128]); wig_d = din("wig", [128, LT * 128])
        wglu = din("wglu", [LT, 128, 2048]); woa = din("woa", [DTL, 128, 2048]); wob = din("wob", [DTL, 128, 2048])
        wo = din("wo", [DTL, 128, 4096])
        h0s_d = din("h0s", [128, LT, NSQ]); conv0s_d = din("conv0s", [128, LT, NSQ, 3])
        ssm0r_d = din("ssm0r", [SP_, G, NSQ]); ssm0i_d = din("ssm0i", [SP_, G, NSQ])
        are_d = din("are", [SP_, G]); aim_d = din("aim", [SP_, G]); ldt_d = din("ldt", [SP_, G])
        bre_d = din("bre", [SP_, G, CG]); bim_d = din("bim", [SP_, G, CG]); cre_d = din("cre", [SP_, G, CG]); cim_d = din("cim", [SP_, G, CG])
        sel_d = din("sel", [128, 64 * 128]); selT_d = din("selT", [128, 64 * 128])
        mask_d = din("mask", [128, 128]); ident_d = din("ident", [128, 128]); dcol_d = din("dcol", [128, G])

        yT = dout("yT", [DTL, 128, NT]); lruh_o = dout("lruh", [128, LT, 17]); convo_o = dout("convo", [128, LT, 17, 3])
        ssmr_o = dout("ssmr", [SP_, G, 17]); ssmi_o = dout("ssmi", [SP_, G, 17])

        X1T = dscr("X1T", [DTL, 128, NT]); X2T = dscr("X2T", [DTL, 128, NT]); FSC = dscr("FSC", [DTL, 128, TB])
        XA = dscr("XA", [LT, 128, NT]); GA = dscr("GA", [LT, 128, NT]); XB = dscr("XB", [LT, 128, NT], BF16)
        YA = dscr("YA", [LT, 128, NT], BF16); YB = dscr("YB", [LT, 128, NT], BF16)
        BCd = dscr("BCd", [G, 128, 128], BF16); DCd = dscr("DCd", [G, 128, 128], BF16)
        CCRd = dscr("CCRd", [SP_, G, 128]); CCId = dscr("CCId", [SP_, G, 128])
        EXI = dscr("EXI", [128, 208]); EXO = dscr("EXO", [256, 208])
        def dbufs(n):
            return [Buf() for _ in range(n)]
        bX1 = dbufs(DTL); bX2 = dbufs(DTL); bFS = dbufs(DTL); bXA = dbufs(LT); bGA = dbufs(LT); bXB = dbufs(LT)
        bYA = dbufs(LT); bYB = dbufs(LT); bBC = dbufs(G); bDC = dbufs(G); bCC = dbufs(8); bEXI = Buf(); bEXO = Buf()
        bXin = dbufs(DTL); bYout = dbufs(DTL)

        cst = kk.sb(es, "cst", [128, NCST], F32); bcst = Buf()
        drv = kk.sb(es, "drv", [128, 3 * 32 + 16], F32); bdrv = Buf()
        ones = kk.sb(es, "ones", [128, 128], BF16); bones = Buf()
        NWS = 5
        wring = kk.ring(es, "wr", NWS, [128, WSLOT], BF16)
        psum = es.enter_context(nc.psum_tensor("ps", [128, 8, 512], F32))
        pbank = [Buf(psum[:, i, :]) for i in range(8)]
        prr = [0]

        def pget():
            b = pbank[prr[0]]; prr[0] = (prr[0] + 1) % 6
            return b

        def wload(src, n):
            k, b = wring.get()
            DMA("pool", b.ap[:, :n], src, f"w{k}", writes=[b])
            return b

        DMA("sp", cst[:, :], cst_d, "c0", writes=[bcst])
        OP("dve", lambda e: e.memset(ones[:, :], 1.0), writes=[bones])
        OP("dve", lambda e: e.tensor_scalar(out=drv[:, 0:32], in0=cst[:, C_G1POST:C_G1POST + 32], scalar1=0.5, scalar2=None, op0=ALU.mult), reads=[bcst], writes=[bdrv])
        OP("dve", lambda e: e.tensor_scalar(out=drv[:, 32:64], in0=cst[:, C_G2POST:C_G2POST + 32], scalar1=0.5, scalar2=None, op0=ALU.mult), reads=[bcst], writes=[bdrv])
        OP("act", lambda e: e.activation(out=drv[:, 96:112], in_=cst[:, C_LAM:C_LAM + 16], func=AF.Exp, scale=-1.0), reads=[bcst], writes=[bdrv])
        OP("act", lambda e: e.activation(out=drv[:, 96:112], in_=drv[:, 96:112], func=AF.Ln, bias=1.0), writes=[bdrv])
        OP("dve", lambda e: e.tensor_scalar(out=drv[:, 96:112], in0=drv[:, 96:112], scalar1=-8.0, scalar2=None, op0=ALU.mult), writes=[bdrv])
        GP05_1 = drv[:, 0:32]; GP05_2 = drv[:, 32:64]; C8 = drv[:, 96:112]

        HALVES = chunks(TB, 512)
        THIRDS = chunks(NT, 512)

        def norm_rstd(st, src, bsrc, t0, n, cks, rstd, brstd, xring, sqring):
            ssb = [pbank[6], pbank[7], pbank[5]][:len(cks)]
            for i in range(DTL):
                k, xb_ = xring.get()
                DMA("sp", xb_.ap[:, :n], src[i][:, t0:t0 + n], f"{xring.name}{k}", reads=[bsrc[i]], writes=[xb_])
                k2, sq = sqring.get()
                OP("act", lambda e, xb_=xb_, sq=sq: e.activation(out=sq.ap[:, :n], in_=xb_.ap[:, :n], func=AF.Square), reads=[xb_], writes=[sq])
                for ci, (o, m) in enumerate(cks):
                    deps = [sq.w]
                    if i == 0:
                        deps += [ssb[ci].w] + ssb[ci].r
                    t = P.op("pe", lambda e, ci=ci, o=o, m=m, sq=sq, i=i: e.matmul(ssb[ci].ap[:, :m], lhsT=ones[:, :], rhs=sq.ap[:, o:o + m], start=(i == 0), stop=(i == DTL - 1)), deps=deps + [bones.w])
                    sq.r.append(t)
                    if i == DTL - 1:
                        ssb[ci].w = t; ssb[ci].r = []
            for ci, (o, m) in enumerate(cks):
                OP("act", lambda e, ci=ci, o=o, m=m: e.activation(out=rstd[:, o:o + m], in_=ssb[ci].ap[:, :m], func=AF.Sqrt, scale=1.0 / D, bias=EPS), reads=[ssb[ci]], writes=[brstd])
            OP("dve", lambda e: e.reciprocal(out=rstd[:, :n], in_=rstd[:, :n]), writes=[brstd])

        def apply_norm(src, bsrc, t0, n, gcol, rstd, brstd, dstT, bdst, xring):
            for i in range(DTL):
                k, xb_ = xring.get()
                DMA("sp", xb_.ap[:, :n], src[i][:, t0:t0 + n], f"{xring.name}{k}", reads=[bsrc[i]], writes=[xb_])
                OP("dve", lambda e, xb_=xb_, i=i: e.scalar_tensor_tensor(out=dstT[:, i, :n], in0=xb_.ap[:, :n], scalar=cst[:, gcol + i:gcol + i + 1], in1=rstd[:, :n], op0=ALU.mult, op1=ALU.mult),
                   reads=[xb_, brstd, bcst], writes=[bdst[i]])

        def resid(srcx, bsrcx, t0, n, gsc, rstd, brstd, dst, bdst, xring, fring):
            for j in range(DTL):
                k, fb = fring.get()
                DMA("sp", fb.ap[:, :n], FSC[j][:, :n], f"{fring.name}{k}", reads=[bFS[j]], writes=[fb])
                k2, xb_ = xring.get()
                DMA("sp", xb_.ap[:, :n], srcx[j][:, t0:t0 + n], f"{xring.name}{k2}", reads=[bsrcx[j]], writes=[xb_])
                OP("dve", lambda e, fb=fb, j=j: e.scalar_tensor_tensor(out=fb.ap[:, :n], in0=fb.ap[:, :n], scalar=gsc[:, j:j + 1], in1=rstd[:, :n], op0=ALU.mult, op1=ALU.mult),
                   reads=[brstd, bdrv, bcst], writes=[fb])
                OP("dve", lambda e, fb=fb, xb_=xb_: e.tensor_tensor(out=xb_.ap[:, :n], in0=fb.ap[:, :n], in1=xb_.ap[:, :n], op=ALU.add), reads=[fb], writes=[xb_])
                DMA("sp", dst[j][:, t0:t0 + n], xb_.ap[:, :n], f"{xring.name}s{k2}", reads=[xb_], writes=[bdst[j]])

        def proj_to_fsc(st, nk, wsrc_fn, rhsT, brhs, n, cks, sqring, fring):
            ssb = [pbank[6], pbank[7]]
            for j in range(DTL):
                pieces = wsrc_fn(j)
                k, fb = fring.get()
                pbs = [pget() for _ in cks]
                for pi, (wb, kc0, nkc) in enumerate(pieces):
                    for ci, (o, m) in enumerate(cks):
                        pb = pbs[ci]
                        kk.mm(pb, pb.ap[:, :m], [(wb.ap[:, q * 128:(q + 1) * 128], rhsT[:, kc0 + q, o:o + m]) for q in range(nkc)],
                              reads=[wb] + (list(brhs) if pi == 0 else []), start=(pi == 0), stop=(pi == len(pieces) - 1), sgc=True)
                for ci, (o, m) in enumerate(cks):
                    pb = pbs[ci]
                    OP("act", lambda e, pb=pb, fb=fb, o=o, m=m: e.activation(out=fb.ap[:, o:o + m], in_=pb.ap[:, :m], func=AF.Copy), reads=[pb], writes=[fb])
                k2, sq = sqring.get()
                OP("act", lambda e, fb=fb, sq=sq: e.activation(out=sq.ap[:, :n], in_=fb.ap[:, :n], func=AF.Square), reads=[fb], writes=[sq])
                for ci, (o, m) in enumerate(cks):
                    deps = [sq.w, bones.w]
                    if j == 0:
                        deps += [ssb[ci].w] + ssb[ci].r
                    t = P.op("pe", lambda e, ci=ci, o=o, m=m, sq=sq, j=j: e.matmul(ssb[ci].ap[:, :m], lhsT=ones[:, :], rhs=sq.ap[:, o:o + m], start=(j == 0), stop=(j == DTL - 1)), deps=deps)
                    sq.r.append(t)
                    if j == DTL - 1:
                        ssb[ci].w = t; ssb[ci].r = []
                DMA("sp", FSC[j][:, :n], fb.ap[:, :n], f"{fring.name}s{k}", reads=[fb], writes=[bFS[j]])

        def finish_rstd(cks, rstd, brstd, n):
            ssb = [pbank[6], pbank[7]]
            for ci, (o, m) in enumerate(cks):
                OP("act", lambda e, ci=ci, o=o, m=m: e.activation(out=rstd[:, o:o + m], in_=ssb[ci].ap[:, :m], func=AF.Sqrt, scale=1.0 / D, bias=EPS), reads=[ssb[ci]], writes=[brstd])
            OP("dve", lambda e: e.reciprocal(out=rstd[:, :n], in_=rstd[:, :n]), writes=[brstd])

        def ffn_stage(src, bsrc, dst, bdst, t0, gprecol, gp05, Wg, Wu, Wd, tail_fn=None):
            with ExitStack() as st:
                n = TB
                hT = kk.sb(st, "hT", [128, DTL, TB], BF16); bhT = dbufs(DTL)
                hid = kk.sb(st, "hid", [128, FFT, TB], BF16); bhid = dbufs(FFT)
                rstd = kk.sb(st, "rstd", [128, TB], F32); brstd = Buf()
                xring = kk.ring(st, "xr", 3, [128, TB], F32)
                fring = kk.ring(st, "fr", 2, [128, TB], F32)
                sqring = kk.ring(st, "sq", 2, [128, TB], BF16)
                sgring = kk.ring(st, "sg", 2, [128, 288], F32)
                norm_rstd(st, src, bsrc, t0, n, HALVES, rstd, brstd, xring, sqring)
                apply_norm(src, bsrc, t0, n, gprecol, rstd, brstd, hT, bhT, xring)
                for f in range(FFT):
                    sg = wload(Wg[f], 4096); su = wload(Wu[f], 4096)
                    for (o, m) in HALVES:
                        pg = pget(); pu = pget()
                        kk.mm(pg, pg.ap[:, :m], [(sg.ap[:, q * 128:(q + 1) * 128], hT[:, q, o:o + m]) for q in range(DTL)], reads=[sg] + bhT)
                        kk.mm(pu, pu.ap[:, :m], [(su.ap[:, q * 128:(q + 1) * 128], hT[:, q, o:o + m]) for q in range(DTL)], reads=[su] + bhT)
                        k, sb_ = sgring.get()
                        OP("act", lambda e, pg=pg, sb_=sb_, m=m: e.activation(out=sb_.ap[:, :m], in_=pg.ap[:, :m], func=AF.Silu), reads=[pg], writes=[sb_])
                        OP("dve", lambda e, pu=pu, sb_=sb_, f=f, o=o, m=m: e.tensor_tensor(out=hid[:, f, o:o + m], in0=sb_.ap[:, :m], in1=pu.ap[:, :m], op=ALU.mult),
                           reads=[sb_, pu], writes=[bhid[f]])
                def wsrc(j):
                    out = []
                    for (kc0, nkc) in [(k0, min(32, FFT - k0)) for k0 in range(0, FFT, 32)]:
                        b = wload(Wd[j][:, kc0 * 128:(kc0 + nkc) * 128], nkc * 128)
                        out.append((b, kc0, nkc))
                    return out
                proj_to_fsc(st, FFT, wsrc, hid, bhid, n, HALVES, sqring, fring)
                finish_rstd(HALVES, rstd, brstd, n)
                resid(src, bsrc, t0, n, gp05, rstd, brstd, dst, bdst, xring, fring)
                if tail_fn is not None:
                    tail_fn(st, t0, hT, bhT, rstd, brstd, xring, fring, sqring)
                P.barrier()

        def win_tail(st, t0, uT, buT, rstd, brstd, xring, fring, sqring):
            n = TB
            norm_rstd(st, X1T, bX1, t0, n, HALVES, rstd, brstd, xring, sqring)
            apply_norm(X1T, bX1, t0, n, C_GMPRE, rstd, brstd, uT, buT, xring)
            for o_ in range(48):
                wb = wload(win[o_], 4096)
                kind = o_ // 16; ti = o_ % 16
                k, fb = fring.get()
                for (o, m) in HALVES:
                    pb = pget()
                    kk.mm(pb, pb.ap[:, :m], [(wb.ap[:, q * 128:(q + 1) * 128], uT[:, q, o:o + m]) for q in range(DTL)], reads=[wb] + buT)
                    if kind == 1:
                        OP("act", lambda e, pb=pb, fb=fb, o=o, m=m: e.activation(out=fb.ap[:, o:o + m], in_=pb.ap[:, :m], func=AF.Gelu), reads=[pb], writes=[fb])
                    elif kind == 0:
                        OP("act", lambda e, pb=pb, fb=fb, o=o, m=m: e.activation(out=fb.ap[:, o:o + m], in_=pb.ap[:, :m], func=AF.Copy), reads=[pb], writes=[fb])
                    else:
                        fbb = fb.ap.bitcast(BF16)
                        OP("act", lambda e, pb=pb, fbb=fbb, o=o, m=m: e.activation(out=fbb[:, o:o + m], in_=pb.ap[:, :m], func=AF.Copy), reads=[pb], writes=[fb])
                if kind == 0:
                    DMA("sp", XA[ti][:, t0:t0 + n], fb.ap[:, :n], f"{fring.name}s{k}", reads=[fb], writes=[bXA[ti]])
                elif kind == 1:
                    DMA("sp", GA[ti][:, t0:t0 + n], fb.ap[:, :n], f"{fring.name}s{k}", reads=[fb], writes=[bGA[ti]])
                else:
                    DMA("sp", XB[ti][:, t0:t0 + n], fb.ap.bitcast(BF16)[:, :n], f"{fring.name}s{k}", reads=[fb], writes=[bXB[ti]])

        lamr = kk.sb(es, "lamr", [SP_, G], F32); lami = kk.sb(es, "lami", [SP_, G], F32); blam = Buf()
        def s5_setup():
            with ExitStack() as st:
                GB = 16
                pg = lambda nm: kk.sb(st, nm, [SP_, G], F32)
                are = pg("are"); aim = pg("aim"); ldt = pg("ldt"); t1 = pg("t1"); t2 = pg("t2"); t3 = pg("t3")
                cc_ = pg("cc_"); ss_ = pg("ss_"); coefr = pg("coefr"); coefi = pg("coefi"); invr = pg("invr"); invi = pg("invi")
                pwr = kk.sb(st, "pwr", [SP_, 16, G], F32); pwi = kk.sb(st, "pwi", [SP_, 16, G], F32)
                ident = kk.sb(st, "ident", [128, 128], F32); mask = kk.sb(st, "mask", [128, 128], F32); dcol = kk.sb(st, "dcol", [128, G], F32)
                B = Buf()
                for (t, d, key) in ((are, are_d, "s0"), (aim, aim_d, "s1"), (ldt, ldt_d, "s2"), (ident, ident_d, "s3"), (mask, mask_d, "s4"), (dcol, dcol_d, "s5")):
                    DMA("sp", t[:, :], d, key, writes=[B])
                V = lambda fn: OP("dve", fn, writes=[B])
                A = lambda fn: OP("act", fn, writes=[B])
                tt = lambda o, a, b, op: V(lambda e: e.tensor_tensor(out=o, in0=a, in1=b, op=op))
                A(lambda e: e.activation(out=ldt[:, :], in_=ldt[:, :], func=AF.Exp))
                tt(t1[:, :], are[:, :], ldt[:, :], ALU.mult)
                tt(t2[:, :], aim[:, :], ldt[:, :], ALU.mult)
                A(lambda e: e.activation(out=t1[:, :], in_=t1[:, :], func=AF.Exp))
                A(lambda e: e.activation(out=ss_[:, :], in_=t2[:, :], func=AF.Sin, scale=1.0 / 32))
                A(lambda e: e.activation(out=cc_[:, :], in_=t2[:, :], func=AF.Sin, scale=1.0 / 32, bias=math.pi / 2))
                for _ in range(5):
                    tt(t2[:, :], cc_[:, :], ss_[:, :], ALU.mult)
                    tt(cc_[:, :], cc_[:, :], cc_[:, :], ALU.mult)
                    tt(ss_[:, :], ss_[:, :], ss_[:, :], ALU.mult)
                    tt(cc_[:, :], cc_[:, :], ss_[:, :], ALU.subtract)
                    V(lambda e: e.tensor_scalar(out=ss_[:, :], in0=t2[:, :], scalar1=2.0, scalar2=None, op0=ALU.mult))
                ar = pwr[:, 8, :]; ai = pwi[:, 8, :]
                tt(ar, t1[:, :], cc_[:, :], ALU.mult); tt(ai, t1[:, :], ss_[:, :], ALU.mult)
                V(lambda e: e.memset(pwr[:, 7, :], 1.0)); V(lambda e: e.memset(pwi[:, 7, :], 0.0))

                def cmul(o_r, o_i, a_r, a_i, b_r, b_i, ta, tb, neg_im=False):
                    tt(ta, a_r, b_r, ALU.mult); tt(tb, a_i, b_i, ALU.mult); tt(o_r, ta, tb, ALU.subtract)
                    tt(ta, a_r, b_i, ALU.mult); tt(tb, a_i, b_r, ALU.mult)
                    if neg_im:
                        V(lambda e: e.scalar_tensor_tensor(out=o_i, in0=ta, scalar=-1.0, in1=tb, op0=ALU.mult, op1=ALU.subtract))
                    else:
                        tt(o_i, ta, tb, ALU.add)
                for j in range(2, 9):
                    cmul(pwr[:, j + 7, :], pwi[:, j + 7, :], pwr[:, j + 6, :], pwi[:, j + 6, :], ar, ai, t2[:, :], t3[:, :])
                tt(t2[:, :], ar, ar, ALU.mult); tt(t3[:, :], ai, ai, ALU.mult); tt(t2[:, :], t2[:, :], t3[:, :], ALU.add)
                V(lambda e: e.reciprocal(out=t2[:, :], in_=t2[:, :]))
                tt(invr[:, :], ar, t2[:, :], ALU.mult)
                V(lambda e: e.scalar_tensor_tensor(out=invi[:, :], in0=ai, scalar=-1.0, in1=t2[:, :], op0=ALU.mult, op1=ALU.mult))
                V(lambda e: e.tensor_copy(out=pwr[:, 6, :], in_=invr[:, :])); V(lambda e: e.tensor_copy(out=pwi[:, 6, :], in_=invi[:, :]))
                for j in range(2, 8):
                    cmul(pwr[:, 7 - j, :], pwi[:, 7 - j, :], pwr[:, 8 - j, :], pwi[:, 8 - j, :], invr[:, :], invi[:, :], t2[:, :], t3[:, :])
                tt(t2[:, :], are[:, :], are[:, :], ALU.mult); tt(t3[:, :], aim[:, :], aim[:, :], ALU.mult); tt(t2[:, :], t2[:, :], t3[:, :], ALU.add)
                V(lambda e: e.reciprocal(out=t2[:, :], in_=t2[:, :]))
                V(lambda e: e.tensor_scalar(out=t1[:, :], in0=ar, scalar1=-1.0, scalar2=None, op0=ALU.add))
                tt(t3[:, :], t1[:, :], are[:, :], ALU.mult); tt(cc_[:, :], ai, aim[:, :], ALU.mult); tt(t3[:, :], t3[:, :], cc_[:, :], ALU.add)
                tt(coefr[:, :], t3[:, :], t2[:, :], ALU.mult)
                tt(t3[:, :], ai, are[:, :], ALU.mult); tt(cc_[:, :], t1[:, :], aim[:, :], ALU.mult); tt(t3[:, :], t3[:, :], cc_[:, :], ALU.subtract)
                tt(coefi[:, :], t3[:, :], t2[:, :], ALU.mult)
                V(lambda e: e.tensor_copy(out=lamr[:, :], in_=pwr[:, 15, :])); V(lambda e: e.tensor_copy(out=lami[:, :], in_=pwi[:, 15, :]))
                blam.w = B.w

                gsh = [SP_, GB, CG]
                bt = lambda nm: kk.sb(st, nm, gsh, F32)
                br_ = bt("br_"); bi_ = bt("bi_"); cr_ = bt("cr_"); ci_ = bt("ci_"); bbr = bt("bbr"); bbi = bt("bbi"); ta = bt("ta"); tb = bt("tb")
                Wr = kk.sb(st, "Wr", [SP_, GB, 8, CG], F32); Wi = kk.sb(st, "Wi", [SP_, GB, 8, CG], F32)
                Xr = kk.sb(st, "Xr", [SP_, GB, 8, CG], F32); Xi = kk.sb(st, "Xi", [SP_, GB, 8, CG], F32)
                Yr = kk.sb(st, "Yr", [SP_, GB, 9, CG], F32); Yi = kk.sb(st, "Yi", [SP_, GB, 9, CG], F32)
                ostg = kk.ring(st, "ostg", 2, [128, 256], BF16)
                tmpm = kk.ring(st, "tmpm", 2, [128, 128], F32)
                for gb in range(G // GB):
                    g0 = gb * GB
                    for (t, d, key) in ((br_, bre_d, "s6"), (bi_, bim_d, "s7"), (cr_, cre_d, "s8"), (ci_, cim_d, "s9")):
                        DMA("sp", t[:, :, :], d[:, g0:g0 + GB, :], key, writes=[B])
                    bc = lambda ap2: ap2.unsqueeze(2).to_broadcast(gsh)
                    cmul(bbr[:, :, :], bbi[:, :, :], bc(coefr[:, g0:g0 + GB]), bc(coefi[:, g0:g0 + GB]), br_[:, :, :], bi_[:, :, :], ta[:, :, :], tb[:, :, :])
                    for s in range(8):
                        cmul(Wr[:, :, s, :], Wi[:, :, s, :], bc(pwr[:, 14 - s, g0:g0 + GB]), bc(pwi[:, 14 - s, g0:g0 + GB]), bbr[:, :, :], bbi[:, :, :], ta[:, :, :], tb[:, :, :])
                        cmul(Xr[:, :, s, :], Xi[:, :, s, :], bc(pwr[:, 7 - s, g0:g0 + GB]), bc(pwi[:, 7 - s, g0:g0 + GB]), bbr[:, :, :], bbi[:, :, :], ta[:, :, :], tb[:, :, :])
                    for t_ in range(9):
                        cmul(Yr[:, :, t_, :], Yi[:, :, t_, :], cr_[:, :, :], ci_[:, :, :], bc(pwr[:, 7 + t_, g0:g0 + GB]), bc(pwi[:, 7 + t_, g0:g0 + GB]), ta[:, :, :], tb[:, :, :], neg_im=True)
                    DMA("sp", CCRd[:, g0:g0 + GB, :].rearrange("p g (t c) -> p g t c", c=CG), Yr[:, :, 1:9, :], "s10", reads=[B], writes=[bCC[gb]])
                    DMA("sp", CCId[:, g0:g0 + GB, :].rearrange("p g (t c) -> p g t c", c=CG), Yi[:, :, 1:9, :], "s11", reads=[B], writes=[bCC[gb]])
                    for gl in range(GB):
                        g = g0 + gl
                        k, ob = ostg.get()
                        for half, Wsrc in enumerate((Wr, Wi)):
                            pb = pget()
                            OP("pe", lambda e, pb=pb, Wsrc=Wsrc, gl=gl: e.transpose(pb.ap[:, 0:SP_], Wsrc[:, gl, :, :].rearrange("p s c -> p (s c)"), ident[0:SP_, 0:SP_]), reads=[B], writes=[pb])
                            OP("act", lambda e, pb=pb, ob=ob, half=half: e.activation(out=ob.ap[:, half * 64:(half + 1) * 64], in_=pb.ap[:, 0:SP_], func=AF.Copy), reads=[pb], writes=[ob])
                        pb = pget()
                        kk.mm(pb, pb.ap[:, 0:128], [(Xr[:, gl, :, :].rearrange("p s c -> p (s c)"), Yr[:, gl, 0:8, :].rearrange("p t c -> p (t c)")),
                                                    (Xi[:, gl, :, :].rearrange("p s c -> p (s c)"), Yi[:, gl, 0:8, :].rearrange("p t c -> p (t c)"))], reads=[B])
                        k2, tm = tmpm.get()
                        OP("dve", lambda e, pb=pb, tm=tm: e.tensor_tensor(out=tm.ap[:, :], in0=pb.ap[:, 0:128], in1=mask[:, :], op=ALU.mult), reads=[pb, B], writes=[tm])
                        OP("dve", lambda e, tm=tm, ob=ob, g=g: e.scalar_tensor_tensor(out=ob.ap[:, 128:256], in0=ident[:, :], scalar=dcol[:, g:g + 1], in1=tm.ap[:, :], op0=ALU.mult, op1=ALU.add),
                           reads=[tm, B], writes=[ob])
                        DMA("sp", BCd[g], ob.ap[:, 0:128], f"ostg{k}a", reads=[ob], writes=[bBC[g]])
                        DMA("sp", DCd[g], ob.ap[:, 128:256], f"ostg{k}b", reads=[ob], writes=[bDC[g]])
                P.barrier()

        def stage2():
            with ExitStack() as st:
                U = kk.sb(st, "U", [128, G, NK], BF16); bU = dbufs(G)
                hsr = kk.sb(st, "hsr", [SP_, G, NSQ], F32); hsi = kk.sb(st, "hsi", [SP_, G, NSQ], F32); bhs = Buf()
                exg = kk.sb(st, "exg", [128, 208], F32); exg2 = kk.sb(st, "exg2", [SP_, G], F32); bexg = Buf()
                wrg = kk.sb(st, "wrg", [128, LT * 128], BF16); wig = kk.sb(st, "wig", [128, LT * 128], BF16); bwg = Buf()
                h0s = kk.sb(st, "h0s", [128, LT, NSQ], F32); bh0 = Buf()
                DMA("pool", wrg[:, :], wrg_d, "c1", writes=[bwg], extra=P.bar)
                DMA("pool", wig[:, :], wig_d, "c2", writes=[bwg], extra=P.bar)
                DMA("sp", hsr[:, :, :], ssm0r_d, "hs0", writes=[bhs]); DMA("sp", hsi[:, :, :], ssm0i_d, "hs1", writes=[bhs])
                DMA("sp", h0s[:, :, :], h0s_d, "h0s", writes=[bh0])
                with ExitStack() as s2:
                    sel = kk.sb(s2, "sel", [128, 64, 128], BF16); bsel = Buf()
                    DMA("pool", sel[:, :, :].rearrange("p a b -> p (a b)"), sel_d, "sel", writes=[bsel], extra=P.bar)
                    xbring = kk.ring(s2, "xbr", 2, [128, NT], BF16)
                    for T in range(LT):
                        k, xb_ = xbring.get()
                        DMA("sp", xb_.ap[:, :], XB[T], f"xbr{k}", reads=[bXB[T]], writes=[xb_])
                        xv = xb_.ap.rearrange("p (k s) -> p k s", s=8)
                        for gl in range(8):
                            g = T * 8 + gl
                            pb = pget()
                            kk.mm(pb, pb.ap[:, :NK], [(sel[:, gl * 8 + s, :], xv[:, :, s]) for s in range(8)], reads=[xb_, bsel])
                            OP("act", lambda e, pb=pb, g=g: e.activation(out=U[:, g, :], in_=pb.ap[:, :NK], func=AF.Copy), reads=[pb], writes=[bU[g]])
                    P.barrier()

                def lru_round(final):
                    with ExitStack() as s2:
                        lout = kk.sb(s2, "lout", [128, LT, 17], F32); blout = Buf()
                        sets = []
                        for si in range(2):
                            lt = lambda nm, w=NT, dt=F32: kk.sb(s2, f"{nm}{si}", [128, w], dt)
                            S = dict(xap=lt("xap", 1203), xc=lt("xc"), xcb=lt("xcb", NT, BF16), rr=lt("rr"), ii=lt("ii"), mm_=lt("mm_"),
                                     yab=lt("yab", NT, BF16), tsm=kk.sb(s2, f"tsm{si}", [128, 16], F32), bL=Buf(), si=si)
                            sets.append(S)

                        def head(h, S):
                            xap = S["xap"]; xc = S["xc"]; xcb = S["xcb"]; rr = S["rr"]; ii = S["ii"]; mm_ = S["mm_"]; yab = S["yab"]; tsm = S["tsm"]; bL = S["bL"]; si = S["si"]
                            aa = rr; hh = xc; gg = mm_
                            xaps = xap[:, 1027:1203].rearrange("p (b t) -> p b t", t=11)
                            L_ = lambda eng, fn, extra=(): OP(eng, fn, writes=[bL], extra=extra)
                            DMA("sp", xap[:, 3:1027], XA[h][:, 0:TP], f"xa0{si}", reads=[bXA[h]], writes=[bL])
                            DMA("sp", xaps[:, :, 3:11], XA[h][:, TP:NT].rearrange("p (b t) -> p b t", t=8), f"xa1{si}", reads=[bXA[h]], writes=[bL])
                            DMA("sp", xaps[:, :, 0:3], conv0s_d[:, h, :, :], f"xa2{si}", writes=[bL])
                            yield
                            if not final:
                                L_("dve", lambda e: e.memset(xap[:, 0:3], 0.0))
                            else:
                                L_("dve", lambda e: e.tensor_scalar(out=xap[:, 0:3], in0=exg[:, 4 * h:4 * h + 3], scalar1=cst[:, C_FLAG:C_FLAG + 1], scalar2=None, op0=ALU.mult), extra=[bexg.w])
                            cw = lambda k_: cst[:, C_CONVW + 4 * h + k_:C_CONVW + 4 * h + k_ + 1]
                            cb = cst[:, C_CONVB + h:C_CONVB + h + 1]
                            xcs = xc[:, TP:NT].rearrange("p (b t) -> p b t", t=8)
                            L_("dve", lambda e: e.tensor_scalar(out=xc[:, 0:TP], in0=xap[:, 3:1027], scalar1=cw(3), scalar2=cb, op0=ALU.mult, op1=ALU.add))
                            L_("dve", lambda e: e.tensor_scalar(out=xcs, in0=xaps[:, :, 3:11], scalar1=cw(3), scalar2=cb, op0=ALU.mult, op1=ALU.add))
                            yield
                            for k_ in range(3):
                                L_("dve", lambda e, k_=k_: e.scalar_tensor_tensor(out=xc[:, 0:TP], in0=xap[:, k_:k_ + TP], scalar=cw(k_), in1=xc[:, 0:TP], op0=ALU.mult, op1=ALU.add))
                                L_("dve", lambda e, k_=k_: e.scalar_tensor_tensor(out=xcs, in0=xaps[:, :, k_:k_ + 8], scalar=cw(k_), in1=xcs, op0=ALU.mult, op1=ALU.add))
                                yield
                            if final:
                                DMA("sp", convo_o[:, h, 0, :], xap[:, 1024:1027], f"co0{si}", reads=[bL])
                                DMA("sp", convo_o[:, h, 1:17, :], xaps[:, :, 8:11], f"co1{si}", reads=[bL])
                            else:
                                DMA("sp", EXI[:, 4 * h:4 * h + 3], xap[:, 1024:1027], f"ex0{si}", reads=[bL], writes=[bEXI])
                            L_("act", lambda e: e.activation(out=xcb[:, :], in_=xc[:, :], func=AF.Copy))
                            yield
                            for (o, m) in THIRDS:
                                pr = pget(); pi_ = pget()
                                kk.mm(pr, pr.ap[:, :m], [(wrg[:, h * 128:(h + 1) * 128], xcb[:, o:o + m])], reads=[bL, bwg])
                                kk.mm(pi_, pi_.ap[:, :m], [(wig[:, h * 128:(h + 1) * 128], xcb[:, o:o + m])], reads=[bL, bwg])
                                OP("act", lambda e, pr=pr, o=o, m=m: e.activation(out=rr[:, o:o + m], in_=pr.ap[:, :m], func=AF.Sigmoid, bias=cst[:, C_BRG + h:C_BRG + h + 1]), reads=[pr], writes=[bL])
                                OP("act", lambda e, pi_=pi_, o=o, m=m: e.activation(out=ii[:, o:o + m], in_=pi_.ap[:, :m], func=AF.Sigmoid, bias=cst[:, C_BIG + h:C_BIG + h + 1]), reads=[pi_], writes=[bL])
                                yield
                            L_("act", lambda e: e.activation(out=aa[:, :], in_=rr[:, :], func=AF.Exp, scale=C8[:, h:h + 1]))
                            yield
                            L_("dve", lambda e: e.tensor_tensor(out=mm_[:, :], in0=aa[:, :], in1=aa[:, :], op=ALU.mult))
                            yield
                            L_("act", lambda e: e.activation(out=mm_[:, :], in_=mm_[:, :], func=AF.Sqrt, scale=-1.0, bias=1.0))
                            L_("dve", lambda e: e.tensor_tensor(out=ii[:, :], in0=ii[:, :], in1=xc[:, :], op=ALU.mult))
                            yield
                            L_("dve", lambda e: e.tensor_tensor(out=ii[:, :], in0=ii[:, :], in1=mm_[:, :], op=ALU.mult))
                            as_ = aa[:, TP:NT].rearrange("p (b t) -> p b t", t=8)[:, :, 0]
                            bs_ = ii[:, TP:NT].rearrange("p (b t) -> p b t", t=8)[:, :, 0]
                            L_("dve", lambda e: e.tensor_tensor(out=tsm[:, :], in0=as_, in1=h0s[:, h, :], op=ALU.mult), extra=[bh0.w])
                            L_("dve", lambda e: e.tensor_tensor(out=bs_, in0=bs_, in1=tsm[:, :], op=ALU.add))
                            yield
                            if final:
                                L_("dve", lambda e: e.tensor_scalar(out=tsm[:, 0:1], in0=exg[:, 64 + h:65 + h], scalar1=cst[:, C_FLAG:C_FLAG + 1], scalar2=None, op0=ALU.mult), extra=[bexg.w])
                                L_("dve", lambda e: e.scalar_tensor_tensor(out=ii[:, 0:1], in0=aa[:, 0:1], scalar=tsm[:, 0:1], in1=ii[:, 0:1], op0=ALU.mult, op1=ALU.add))
                            L_("dve", lambda e: e.memset(aa[:, 0:1], 0.0))
                            L_("dve", lambda e: e.memset(as_, 0.0))
                            yield
                            L_("dve", lambda e: e.tensor_tensor_scan(out=hh[:, :], data0=aa[:, :], data1=ii[:, :], initial=0.0, op0=ALU.mult, op1=ALU.add))
                            yield
                            if not final:
                                OP("dve", lambda e: e.tensor_copy(out=lout[:, 0, h:h + 1], in_=hh[:, TP - 1:TP]), reads=[bL], writes=[blout])
                            else:
                                OP("dve", lambda e: e.tensor_copy(out=lout[:, h, 0:1], in_=hh[:, TP - 1:TP]), reads=[bL], writes=[blout])
                                OP("dve", lambda e: e.tensor_copy(out=lout[:, h, 1:17], in_=hh[:, TP:NT].rearrange("p (b t) -> p b t", t=8)[:, :, 7]), reads=[bL], writes=[blout])
                                DMA("sp", gg[:, :], GA[h], f"ga{si}", reads=[bGA[h]], writes=[bL])
                                yield
                                L_("dve", lambda e: e.tensor_tensor(out=yab[:, :], in0=hh[:, :], in1=gg[:, :], op=ALU.mult))
                                DMA("sp", YA[h], yab[:, :], f"ya{si}", reads=[bL], writes=[bYA[h]])
                            yield

                        pending = list(range(LT)); active = []
                        free_sets = [sets[0], sets[1]]
                        while pending or active:
                            while pending and free_sets:
                                S = free_sets.pop(0)
                                active.append((head(pending.pop(0), S), S))
                            nxt = []
                            for (g_, S) in active:
                                try:
                                    next(g_); nxt.append((g_, S))
                                except StopIteration:
                                    free_sets.append(S)
                            active = nxt
                        if final:
                            DMA("sp", lruh_o, lout[:, :, :], "lo", reads=[blout])
                        else:
                            DMA("sp", EXI[:, 64:80], lout[:, 0, 0:16], "ex1", reads=[blout], writes=[bEXI])
                        P.barrier()

                def s5_pass(final):
                    with ExitStack() as s2:
                        Ar = kk.sb(s2, "Ar", [SP_, 64, 145], F32); Ai = kk.sb(s2, "Ai", [SP_, 64, 145], F32); bA = Buf()
                        mring = kk.ring(s2, "mr", 3, [128, 128], BF16)
                        cring = kk.ring(s2, "cr", 2, [SP_, 2, 128], F32)
                        t4 = [kk.sb(s2, f"t4_{i}", [SP_, 64], F32) for i in range(4)]
                        t5 = kk.sb(s2, "t5", [SP_, 64, NSQ], F32)
                        S_ = lambda fn, extra=(): OP("dve", fn, writes=[bA], extra=extra)
                        for hf in range(2):
                            gs = [hf * 64 + q for q in range(64)]
                            lr = lamr[:, hf * 64:(hf + 1) * 64]; li = lami[:, hf * 64:(hf + 1) * 64]
                            for q, g in enumerate(gs):
                                k, mb = mring.get()
                                DMA("sp", mb.ap[:, :], BCd[g], f"mr{k}", reads=[bBC[g]], writes=[mb])
                                pr = pget(); pi_ = pget()
                                kk.mm(pr, pr.ap[0:SP_, :NK], [(mb.ap[:, 0:64], U[:, g, :])], reads=[mb, bU[g]])
                                kk.mm(pi_, pi_.ap[0:SP_, :NK], [(mb.ap[:, 64:128], U[:, g, :])], reads=[mb, bU[g]])
                                OP("act", lambda e, pr=pr, q=q: e.activation(out=Ar[:, q, 1:145], in_=pr.ap[0:SP_, :NK], func=AF.Copy), reads=[pr], writes=[bA])
                                OP("act", lambda e, pi_=pi_, q=q: e.activation(out=Ai[:, q, 1:145], in_=pi_.ap[0:SP_, :NK], func=AF.Copy), reads=[pi_], writes=[bA])
                            if not final:
                                S_(lambda e: e.memset(Ar[:, :, 0], 0.0)); S_(lambda e: e.memset(Ai[:, :, 0], 0.0))
                            else:
                                fl = cst[0:SP_, C_FLAG:C_FLAG + 1]
                                S_(lambda e, hf=hf, fl=fl: e.tensor_scalar(out=Ar[:, :, 0], in0=exg[0:SP_, 80 + hf * 64:80 + (hf + 1) * 64], scalar1=fl, scalar2=None, op0=ALU.mult), extra=[bexg.w])
                                S_(lambda e, hf=hf, fl=fl: e.tensor_scalar(out=Ai[:, :, 0], in0=exg2[:, hf * 64:(hf + 1) * 64], scalar1=fl, scalar2=None, op0=ALU.mult), extra=[bexg.w])
                            for k_ in range(128):
                                xr_ = Ar[:, :, k_]; xi_ = Ai[:, :, k_]; nr_ = Ar[:, :, k_ + 1]; ni_ = Ai[:, :, k_ + 1]
                                S_(lambda e, xr_=xr_, lr=lr: e.tensor_tensor(out=t4[0][:, :], in0=lr, in1=xr_, op=ALU.mult))
                                S_(lambda e, xi_=xi_, li=li: e.tensor_tensor(out=t4[1][:, :], in0=li, in1=xi_, op=ALU.mult))
                                S_(lambda e, xi_=xi_, lr=lr: e.tensor_tensor(out=t4[2][:, :], in0=lr, in1=xi_, op=ALU.mult))
                                S_(lambda e, xr_=xr_, li=li: e.tensor_tensor(out=t4[3][:, :], in0=li, in1=xr_, op=ALU.mult))
                                S_(lambda e, nr_=nr_: e.tensor_tensor(out=nr_, in0=nr_, in1=t4[0][:, :], op=ALU.add))
                                S_(lambda e, nr_=nr_: e.tensor_tensor(out=nr_, in0=nr_, in1=t4[1][:, :], op=ALU.subtract))
                                S_(lambda e, ni_=ni_: e.tensor_tensor(out=ni_, in0=ni_, in1=t4[2][:, :], op=ALU.add))
                                S_(lambda e, ni_=ni_: e.tensor_tensor(out=ni_, in0=ni_, in1=t4[3][:, :], op=ALU.add))
                            if not final:
                                S_(lambda e: e.tensor_copy(out=t4[0][:, :], in_=Ar[:, :, 128])); S_(lambda e: e.tensor_copy(out=t4[1][:, :], in_=Ai[:, :, 128]))
                                DMA("sp", EXI[0:SP_, 80 + hf * 64:80 + (hf + 1) * 64], t4[0][:, :], "ex2", reads=[bA], writes=[bEXI])
                                DMA("sp", EXI[SP_:128, 80 + hf * 64:80 + (hf + 1) * 64], t4[1][:, :], "ex3", reads=[bA], writes=[bEXI])
                                bA.w = bEXI.w
                                continue
                            lrb = lr.unsqueeze(2).to_broadcast([SP_, 64, NSQ]); lib = li.unsqueeze(2).to_broadcast([SP_, 64, NSQ])
                            hr_ = hsr[:, hf * 64:(hf + 1) * 64, :]; hi_ = hsi[:, hf * 64:(hf + 1) * 64, :]
                            sr_ = Ar[:, :, 129:145]; si_ = Ai[:, :, 129:145]
                            for (a_, b_, dst_, op_) in ((lrb, hr_, sr_, ALU.add), (lib, hi_, sr_, ALU.subtract), (lrb, hi_, si_, ALU.add), (lib, hr_, si_, ALU.add)):
                                S_(lambda e, a_=a_, b_=b_: e.tensor_tensor(out=t5[:, :, :], in0=a_, in1=b_, op=ALU.mult), extra=[bhs.w])
                                S_(lambda e, dst_=dst_, op_=op_: e.tensor_tensor(out=dst_, in0=dst_, in1=t5[:, :, :], op=op_))
                            DMA("sp", ssmr_o[:, hf * 64:(hf + 1) * 64, :], Ar[:, :, 128:145], "so0", reads=[bA])
                            DMA("sp", ssmi_o[:, hf * 64:(hf + 1) * 64, :], Ai[:, :, 128:145], "so1", reads=[bA])
                            for q, g in enumerate(gs):
                                k, cbuf = cring.get()
                                DMA("sp", cbuf.ap[:, 0, :], CCRd[:, g, :], f"cr{k}a", reads=[bCC[g // 16]], writes=[cbuf])
                                DMA("sp", cbuf.ap[:, 1, :], CCId[:, g, :], f"cr{k}b", reads=[bCC[g // 16]], writes=[cbuf])
                                k2, mb = mring.get()
                                DMA("sp", mb.ap[:, :], DCd[g], f"mr{k2}", reads=[bDC[g]], writes=[mb])
                                pb = pget()
                                kk.mm(pb, pb.ap[:, 0:128], [(mb.ap[:, :], U[:, g, 0:128])], reads=[mb, bU[g]])
                                kk.mm(pb, pb.ap[:, 0:128], [(cbuf.ap[:, 0, :], Ar[:, q, 0:128]), (cbuf.ap[:, 1, :], Ai[:, q, 0:128])], reads=[cbuf, bA], start=False, sgc=True)
                                pb2 = pget()
                                kk.mm(pb2, pb2.ap[:, 0:NSQ], [(mb.ap[:, :], U[:, g, 128:144])], reads=[mb, bU[g]])
                                kk.mm(pb2, pb2.ap[:, 0:NSQ], [(cbuf.ap[:, 0, :], hsr[:, g, :]), (cbuf.ap[:, 1, :], hsi[:, g, :])], reads=[cbuf, bhs], start=False, sgc=True)
                                OP("act", lambda e, pb=pb, g=g: e.activation(out=U[:, g, 0:128], in_=pb.ap[:, 0:128], func=AF.Copy), reads=[pb], writes=[bU[g]])
                                OP("act", lambda e, pb2=pb2, g=g: e.activation(out=U[:, g, 128:144], in_=pb2.ap[:, 0:NSQ], func=AF.Copy), reads=[pb2], writes=[bU[g]])
                        P.barrier()

                lru_round(False)
                s5_pass(False)
                def cc(e):
                    return e.collective_compute("AllGather", ALU.bypass, replica_groups=[[2 * i, 2 * i + 1] for i in range(NCORES // 2)], ins=[EXI], outs=[EXO])
                OP("pool", cc, reads=[bEXI], writes=[bEXO])
                DMA("sp", exg[:, :], EXO[0:128, :], "exg", reads=[bEXO], writes=[bexg])
                DMA("sp", exg2[:, :], EXO[SP_:128, 80:208], "exg2", reads=[bEXO], writes=[bexg])
                lru_round(True)
                s5_pass(True)
                with ExitStack() as s2:
                    selT = kk.sb(s2, "selT", [128, 64, 128], BF16); bselT = Buf()
                    DMA("pool", selT[:, :, :].rearrange("p a b -> p (a b)"), selT_d, "selT", writes=[bselT], extra=P.bar)
                    gT = kk.sb(s2, "gT", [128, LT, NT], BF16); bgT = dbufs(LT)
                    for T in range(LT):
                        banks = [pget(), pget(), pget()]
                        for t_ in range(8):
                            pb = banks[t_ // 3]; off = (t_ % 3) * NK
                            kk.mm(pb, pb.ap[:, off:off + NK], [(selT[:, gl * 8 + t_, :], U[:, T * 8 + gl, :]) for gl in range(8)], reads=[bselT] + [bU[T * 8 + gl] for gl in range(8)])
                        gv = gT[:, T, :].rearrange("p (k t) -> p t k", t=8)
                        for bi in range(3):
                            nt_ = 3 if bi < 2 else 2
                            pb = banks[bi]
                            OP("act", lambda e, pb=pb, bi=bi, nt_=nt_, gv=gv: e.activation(out=gv[:, bi * 3:bi * 3 + nt_, :], in_=pb.ap[:, 0:nt_ * NK].rearrange("p (t k) -> p t k", k=NK), func=AF.Gelu),
                               reads=[pb], writes=[bgT[T]])
                    sgr = kk.ring(s2, "sgr", 2, [128, 384], F32)
                    ybr = kk.ring(s2, "ybr", 2, [128, NT], BF16)
                    for o_ in range(LT):
                        wb = wload(wglu[o_], 2048)
                        k, yb_ = ybr.get()
                        for (o, m) in THIRDS:
                            pb = pget()
                            kk.mm(pb, pb.ap[:, :m], [(wb.ap[:, q * 128:(q + 1) * 128], gT[:, q, o:o + m]) for q in range(LT)], reads=[wb] + bgT)
                            k2, sb_ = sgr.get()
                            OP("act", lambda e, pb=pb, sb_=sb_, m=m, o_=o_: e.activation(out=sb_.ap[:, :m], in_=pb.ap[:, :m], func=AF.Sigmoid, bias=cst[:, C_BGLU + o_:C_BGLU + o_ + 1]), reads=[pb], writes=[sb_])
                            OP("dve", lambda e, sb_=sb_, yb_=yb_, o=o, m=m, o_=o_: e.tensor_tensor(out=yb_.ap[:, o:o + m], in0=sb_.ap[:, :m], in1=gT[:, o_, o:o + m], op=ALU.mult), reads=[sb_, bgT[o_]], writes=[yb_])
                        DMA("sp", YB[o_], yb_.ap[:, :], f"ybr{k}", reads=[yb_], writes=[bYB[o_]])
                    P.barrier()

        def stage3(t0):
            with ExitStack() as st:
                n = TB
                uT = kk.sb(st, "uT", [128, DTL, TB], BF16); buT = dbufs(DTL)
                mT = kk.sb(st, "mT", [128, DTL, TB], BF16); bmT = dbufs(DTL)
                yaT = kk.sb(st, "yaT", [128, LT, TB], BF16); ybT = kk.sb(st, "ybT", [128, LT, TB], BF16); bya = Buf(); byb = Buf()
                rstd = kk.sb(st, "rstd", [128, TB], F32); brstd = Buf()
                xring = kk.ring(st, "xr", 3, [128, TB], F32)
                fring = kk.ring(st, "fr", 2, [128, TB], F32)
                sqring = kk.ring(st, "sq", 2, [128, TB], BF16)
                sgr = kk.ring(st, "sg3", 4, [128, 288], F32)
                for T in range(LT):
                    DMA("sp", yaT[:, T, :], YA[T][:, t0:t0 + n], "ya3", reads=[bYA[T]], writes=[bya])
                    DMA("sp", ybT[:, T, :], YB[T][:, t0:t0 + n], "yb3", reads=[bYB[T]], writes=[byb])
                norm_rstd(st, X1T, bX1, t0, n, HALVES, rstd, brstd, xring, sqring)
                apply_norm(X1T, bX1, t0, n, C_GMPRE, rstd, brstd, uT, buT, xring)
                for j in range(DTL):
                    wga = wload(win[48 + j], 4096); wgb = wload(win[80 + j], 4096); wa = wload(woa[j], 2048); wb_ = wload(wob[j], 2048)
                    for (o, m) in HALVES:
                        pga = pget(); pgb = pget(); ppa = pget(); ppb = pget()
                        kk.mm(pga, pga.ap[:, :m], [(wga.ap[:, q * 128:(q + 1) * 128], uT[:, q, o:o + m]) for q in range(DTL)], reads=[wga] + buT)
                        kk.mm(pgb, pgb.ap[:, :m], [(wgb.ap[:, q * 128:(q + 1) * 128], uT[:, q, o:o + m]) for q in range(DTL)], reads=[wgb] + buT)
                        kk.mm(ppa, ppa.ap[:, :m], [(wa.ap[:, q * 128:(q + 1) * 128], yaT[:, q, o:o + m]) for q in range(LT)], reads=[wa, bya])
                        kk.mm(ppb, ppb.ap[:, :m], [(wb_.ap[:, q * 128:(q + 1) * 128], ybT[:, q, o:o + m]) for q in range(LT)], reads=[wb_, byb])
                        k1, s1 = sgr.get(); k2, s2_ = sgr.get()
                        OP("act", lambda e, pga=pga, s1=s1, m=m: e.activation(out=s1.ap[:, :m], in_=pga.ap[:, :m], func=AF.Sigmoid), reads=[pga], writes=[s1])
                        OP("act", lambda e, pgb=pgb, s2_=s2_, m=m: e.activation(out=s2_.ap[:, :m], in_=pgb.ap[:, :m], func=AF.Sigmoid), reads=[pgb], writes=[s2_])
                        OP("dve", lambda e, s1=s1, ppa=ppa, m=m: e.tensor_tensor(out=s1.ap[:, :m], in0=s1.ap[:, :m], in1=ppa.ap[:, :m], op=ALU.mult), reads=[ppa], writes=[s1])
                        OP("dve", lambda e, s2_=s2_, ppb=ppb, m=m: e.tensor_tensor(out=s2_.ap[:, :m], in0=s2_.ap[:, :m], in1=ppb.ap[:, :m], op=ALU.mult), reads=[ppb], writes=[s2_])
                        OP("dve", lambda e, s1=s1, s2_=s2_, j=j, o=o, m=m: e.tensor_tensor(out=mT[:, j, o:o + m], in0=s1.ap[:, :m], in1=s2_.ap[:, :m], op=ALU.add), reads=[s1, s2_], writes=[bmT[j]])

                def wsrc(j):
                    b = wload(wo[j], 4096)
                    return [(b, 0, 32)]
                proj_to_fsc(st, DTL, wsrc, mT, bmT, n, HALVES, sqring, fring)
                finish_rstd(HALVES, rstd, brstd, n)
                resid(X1T, bX1, t0, n, cst[:, C_GMPOST:C_GMPOST + 32], rstd, brstd, X2T, bX2, xring, fring)
                P.barrier()

        def dbg_dump():
            P.barrier()
            taps = dict(BCd=BCd, DCd=DCd, CCRd=CCRd, CCId=CCId, X1T=X1T, XA=XA, GA=GA, XB=XB, YA=YA, YB=YB, X2T=X2T, EXO=EXO)
            for nm in DBG_TAPS:
                src = taps[nm]
                o = nc.dram_tensor("dbg_" + nm, list(src.shape), src.dtype, kind="ExternalOutput").ap()
                P.dma("sp", lambda e, o=o, src=src: e.dma_start(out=o, in_=src), "dbg_" + nm)

        P.barrier()
        if UPTO >= 1:
            s5_setup()
        if UPTO >= 2:
            for blk in range(2):
                ffn_stage(xT, bXin, X1T, bX1, blk * TB, C_G1PRE, GP05_1, wg1, wu1, wd1, tail_fn=win_tail)
        if UPTO >= 3:
            stage2()
        if UPTO >= 4:
            for blk in range(2):
                stage3(blk * TB)
                ffn_stage(X2T, bX2, yT, bYout, blk * TB, C_G2PRE, GP05_2, wg2, wu2, wd2)
        if DEBUG:
            dbg_dump()
        P.barrier(engs=("sp",))
        block = es.enter_context(nc.Block())
        P.build(block)
    return nc


UPTO = 4
DEBUG = False
NCORES = 8
DBG_TAPS = []


def tile_w(W, K, N):
    return np.ascontiguousarray(W.reshape(K // 128, 128, N // 128, 128).transpose(2, 1, 0, 3)).reshape(N // 128, 128, K)


def vec_tiles(v, nt):
    return np.ascontiguousarray(np.asarray(v).reshape(nt, 128).T)


def make_shared(inp):
    f = lambda a: np.asarray(a, dtype=np.float32)
    sh = {}
    sh["wg1"] = tile_w(f(inp["ffn1_w_gate"])[0], D, DFF); sh["wu1"] = tile_w(f(inp["ffn1_w_up"])[0], D, DFF); sh["wd1"] = tile_w(f(inp["ffn1_w_down"])[0], DFF, D)
    sh["wg2"] = tile_w(f(inp["ffn2_w_gate"])[0], D, DFF); sh["wu2"] = tile_w(f(inp["ffn2_w_up"])[0], D, DFF); sh["wd2"] = tile_w(f(inp["ffn2_w_down"])[0], DFF, D)
    sh["win"] = tile_w(f(inp["w_in"])[0], D, 14336)
    sh["wrg"] = np.ascontiguousarray(f(inp["w_rg"])[0].transpose(1, 0, 2)).reshape(128, LT * 128)
    sh["wig"] = np.ascontiguousarray(f(inp["w_ig"])[0].transpose(1, 0, 2)).reshape(128, LT * 128)
    sh["wglu"] = tile_w(f(inp["w_glu"])[0], DL, DL); sh["woa"] = tile_w(f(inp["w_out_a"])[0], DL, D); sh["wob"] = tile_w(f(inp["w_out_b"])[0], DL, D)
    sh["wo"] = tile_w(f(inp["w_o"])[0], D, D)
    sh["are"] = np.ascontiguousarray(f(inp["ssm_a_re"])[0].T); sh["aim"] = np.ascontiguousarray(f(inp["ssm_a_im"])[0].T)
    sh["ldt"] = np.ascontiguousarray(np.broadcast_to(f(inp["ssm_log_dt"])[0][None, :], (SP_, G)))
    sh["bre"] = np.ascontiguousarray(f(inp["ssm_b_re"])[0].transpose(1, 0, 2)); sh["bim"] = np.ascontiguousarray(f(inp["ssm_b_im"])[0].transpose(1, 0, 2))
    sh["cre"] = np.ascontiguousarray(f(inp["ssm_c_re"])[0].transpose(2, 0, 1)); sh["cim"] = np.ascontiguousarray(f(inp["ssm_c_im"])[0].transpose(2, 0, 1))
    sel = np.zeros((128, 64, 128), np.float32)
    for gl in range(8):
        for s in range(8):
            for c in range(16):
                sel[gl * 16 + c, gl * 8 + s, s * 16 + c] = 1.0
    sh["sel"] = sel.reshape(128, 64 * 128)
    sh["selT"] = np.ascontiguousarray(sel.transpose(2, 1, 0)).reshape(128, 64 * 128)
    s_idx = np.arange(128) // 16
    sh["mask"] = (s_idx[:, None] <= s_idx[None, :]).astype(np.float32)
    sh["ident"] = np.eye(128, dtype=np.float32)
    d = f(inp["ssm_d"])[0].reshape(G, CG)
    sh["dcol"] = np.ascontiguousarray(np.tile(d.T, (8, 1)))
    cst = np.zeros((128, NCST), np.float32)
    for col, nm in ((C_G1PRE, "ffn1_pre_g"), (C_G1POST, "ffn1_post_g"), (C_GMPRE, "mix_pre_g"), (C_GMPOST, "mix_post_g"), (C_G2PRE, "ffn2_pre_g"), (C_G2POST, "ffn2_post_g")):
        cst[:, col:col + 32] = vec_tiles(f(inp[nm])[0], 32)
    cst[:, C_CONVW:C_CONVW + 64] = f(inp["conv_w"])[0].reshape(4, LT, 128).transpose(2, 1, 0).reshape(128, 64)
    for col, nm in ((C_CONVB, "conv_b"), (C_BRG, "b_rg"), (C_BIG, "b_ig"), (C_LAM, "lru_lambda"), (C_SSMD, "ssm_d"), (C_BGLU, "b_glu")):
        cst[:, col:col + 16] = vec_tiles(f(inp[nm])[0], 16)
    sh["cst"] = cst
    return sh


def make_core_inputs(inp, sh, c):
    f = lambda a: np.asarray(a, dtype=np.float32)
    s, hf = c // 2, c % 2
    xp = f(inp["x_prompt"])[s, hf * TP:(hf + 1) * TP, :]
    xs = f(inp["x_sample"])[NSQ * c:NSQ * (c + 1)].reshape(NSQ * TS, D)
    x = np.concatenate([xp, xs], axis=0)
    m = dict(sh)
    m["xT"] = np.ascontiguousarray(x.T).reshape(DTL, 128, NT)
    cst = sh["cst"].copy(); cst[:, C_FLAG] = float(hf); m["cst"] = cst
    sl = slice(NSQ * c, NSQ * (c + 1))
    m["h0s"] = np.ascontiguousarray(f(inp["state_lru_h"])[0, sl].reshape(NSQ, LT, 128).transpose(2, 1, 0))
    m["conv0s"] = np.ascontiguousarray(f(inp["state_conv"])[0, sl].reshape(NSQ, 3, LT, 128).transpose(3, 2, 0, 1))
    m["ssm0r"] = np.ascontiguousarray(f(inp["state_ssm_re"])[0, sl].transpose(2, 1, 0))
    m["ssm0i"] = np.ascontiguousarray(f(inp["state_ssm_im"])[0, sl].transpose(2, 1, 0))
    return m


_NC_CACHE = {}


def kernel(**inputs):
    sh = make_shared(inputs)
    in_maps = [make_core_inputs(inputs, sh, c) for c in range(8)]
    if "nc" not in _NC_CACHE:
        _NC_CACHE["nc"] = build_program()
    nc = _NC_CACHE["nc"]
    res = run_bass_kernel_spmd(nc, in_maps, core_ids=list(range(8)))
    R = res.results
    B_, S_ = 4, 2048
    yp = np.zeros((B_, S_, D), np.float32); ys = np.zeros((128, TS, D), np.float32)
    p_h = np.zeros((1, B_, DL), np.float32); p_c = np.zeros((1, B_, 3, DL), np.float32)
    p_re = np.zeros((1, B_, G, SP_), np.float32); p_im = np.zeros((1, B_, G, SP_), np.float32)
    s_h = np.zeros((1, 128, DL), np.float32); s_c = np.zeros((1, 128, 3, DL), np.float32)
    s_re = np.zeros((1, 128, G, SP_), np.float32); s_im = np.zeros((1, 128, G, SP_), np.float32)
    for c in range(8):
        s, hf = c // 2, c % 2
        y = R[c]["yT"].reshape(D, NT).T
        yp[s, hf * TP:(hf + 1) * TP] = y[:TP]
        ys[NSQ * c:NSQ * (c + 1)] = y[TP:].reshape(NSQ, TS, D)
        lh = R[c]["lruh"].transpose(2, 1, 0).reshape(17, DL)
        cv = R[c]["convo"].transpose(2, 3, 1, 0).reshape(17, 3, DL)
        sr = R[c]["ssmr"].transpose(2, 1, 0); si = R[c]["ssmi"].transpose(2, 1, 0)
        if hf == 1:
            p_h[0, s] = lh[0]; p_c[0, s] = cv[0]; p_re[0, s] = sr[0]; p_im[0, s] = si[0]
        s_h[0, NSQ * c:NSQ * (c + 1)] = lh[1:]; s_c[0, NSQ * c:NSQ * (c + 1)] = cv[1:]
        s_re[0, NSQ * c:NSQ * (c + 1)] = sr[1:]; s_im[0, NSQ * c:NSQ * (c + 1)] = si[1:]
    return (yp, ys, p_h, p_c, p_re, p_im, s_h, s_c, s_re, s_im)
```

```python
import math
import numpy as np
from contextlib import ExitStack
import concourse.bass as bass
import concourse.mybir as mybir
from concourse.bass_utils import run_bass_kernel_spmd

F32 = mybir.dt.float32
BF16 = mybir.dt.bfloat16
AF = mybir.ActivationFunctionType
ALU = mybir.AluOpType

D = 4096; DTL = 32; DFF = 11008; FFT = 86; DL = 2048; LT = 16
NT = 1152; TP = 1024; NSQ = 16; TS = 8; TB = 576; G = 128; SP_ = 64; CG = 16; NK = 144
EPS = 1e-6
WSLOT = 4096
ENGS = ("pe", "act", "dve", "pool", "sp")

C_G1PRE, C_G1POST, C_GMPRE, C_GMPOST, C_G2PRE, C_G2POST = 0, 32, 64, 96, 128, 160
C_CONVW = 192; C_CONVB = 256; C_BRG = 272; C_BIG = 288; C_LAM = 304; C_SSMD = 320; C_BGLU = 336; C_FLAG = 352
NCST = 353


class Tok:
    __slots__ = ("sem", "val", "eng")

    def __init__(self, sem, val, eng=None):
        self.sem = sem; self.val = val; self.eng = eng


class Buf:
    __slots__ = ("ap", "w", "r")

    def __init__(self, ap=None):
        self.ap = ap; self.w = None; self.r = []


class Ring:
    def __init__(self, aps):
        self.bufs = [Buf(a) for a in aps]; self.i = 0

    def get(self):
        b = self.bufs[self.i]; k = self.i
        self.i = (self.i + 1) % len(self.bufs)
        return k, b


class Prog:
    SEM_ROLL = 30000

    def __init__(self, nc, es):
        self.nc = nc; self.es = es
        self.streams = {e: [] for e in ENGS}
        self.cur_sem = {e: None for e in ENGS}
        self.cur_cnt = {e: 0 for e in ENGS}
        self.waited = {e: {} for e in ENGS}
        self.nsem = 0
        self.dma_sems = {}
        self.bar = []

    def new_sem(self, name):
        self.nsem += 1
        return self.es.enter_context(self.nc.semaphore(f"{name}_{self.nsem}"))

    def _waits(self, eng, deps):
        ws = []; w = self.waited[eng]
        for d in deps:
            if d is None:
                continue
            if isinstance(d, (list, tuple)):
                ws.extend(self._waits(eng, d)); continue
            if eng == "pe" and d.eng == "pe":
                continue
            k = id(d.sem)
            if w.get(k, -1) >= d.val:
                continue
            w[k] = d.val
            ws.append((d.sem, d.val))
        return ws

    def op(self, eng, fn, deps=(), signal=True):
        ws = self._waits(eng, deps)
        tok = None; inc = None
        if signal:
            if self.cur_sem[eng] is None or self.cur_cnt[eng] >= self.SEM_ROLL:
                self.cur_sem[eng] = self.new_sem(f"s_{eng}"); self.cur_cnt[eng] = 0
            self.cur_cnt[eng] += 1
            tok = Tok(self.cur_sem[eng], self.cur_cnt[eng], eng); inc = (self.cur_sem[eng], 1)
        self.streams[eng].append((fn, ws, inc))
        return tok

    def dma(self, eng, fn, key, deps=()):
        ws = self._waits(eng, deps)
        if key not in self.dma_sems:
            self.dma_sems[key] = [self.new_sem("d"), 0]
        ent = self.dma_sems[key]; ent[1] += 16
        self.streams[eng].append((fn, ws, (ent[0], 16)))
        return Tok(ent[0], ent[1])

    def wait_only(self, eng, deps):
        ws = self._waits(eng, deps)
        if ws:
            self.streams[eng].append((None, ws, None))

    def barrier(self, engs=("pe", "act", "dve", "sp")):
        toks = []
        for e in ENGS:
            if self.cur_sem[e] is not None:
                toks.append(Tok(self.cur_sem[e], self.cur_cnt[e]))
        for k, (s, v) in self.dma_sems.items():
            toks.append(Tok(s, v))
        for e in engs:
            self.wait_only(e, toks)
        self.bar = toks
        return toks

    def build(self, block):
        def run(e, items):
            for fn, ws, inc in items:
                for (s, v) in ws:
                    e.wait_ge(s, v)
                if fn is None:
                    continue
                ins = fn(e)
                if inc is not None:
                    ins.then_inc(inc[0], inc[1])

        @block.sync
        def _(e):
            run(e, self.streams["sp"])

        @block.scalar
        def _(e):
            run(e, self.streams["act"])

        @block.vector
        def _(e):
            run(e, self.streams["dve"])

        @block.gpsimd
        def _(e):
            run(e, self.streams["pool"])

        @block.tensor
        def _(e):
            run(e, self.streams["pe"])


class K:
    def __init__(self, nc, es):
        self.nc = nc; self.es = es; self.P = Prog(nc, es)
        self.nkey = 0
        self.lastdma = {}

    def OP(self, eng, fn, reads=(), writes=(), extra=(), signal=True):
        deps = list(extra)
        for b in reads:
            deps.append(b.w)
        for b in writes:
            deps.append(b.w); deps.extend(b.r)
        t = self.P.op(eng, fn, deps=deps, signal=signal)
        if t is not None:
            for b in reads:
                b.r.append(t)
            for b in writes:
                b.w = t; b.r = []
        return t

    def DMA(self, q, out, in_, key, reads=(), writes=(), extra=()):
        deps = list(extra)
        for b in reads:
            deps.append(b.w)
        for b in writes:
            deps.append(b.w); deps.extend(b.r)
        deps.append(self.lastdma.get(key))
        t = self.P.dma(q, lambda e: e.dma_start(out=out, in_=in_), key, deps=deps)
        self.lastdma[key] = t
        for b in reads:
            b.r.append(t)
        for b in writes:
            b.w = t; b.r = []
        return t

    def sb(self, st, name, shape, dt):
        self.nkey += 1
        return st.enter_context(self.nc.sbuf_tensor(f"{name}_s{self.nkey}", shape, dt))

    def ring(self, st, name, n, shape, dt):
        t = self.sb(st, name, [shape[0], n] + list(shape[1:]), dt)
        r = Ring([t[:, i] for i in range(n)])
        r.name = name
        return r

    def mm(self, out_buf, out_ap, pairs, reads=(), start=True, sgc=False, stop=True):
        n = len(pairs)
        deps = [out_buf.w] + list(out_buf.r)
        for b in reads:
            deps.append(b.w)
        tok = None
        for q, (l, r) in enumerate(pairs):
            tok = self.P.op("pe", (lambda e, l=l, r=r, q=q: e.matmul(out_ap, lhsT=l, rhs=r, start=(start and q == 0), stop=(stop and q == n - 1), skip_group_check=sgc)),
                            deps=deps if q == 0 else (), signal=(q == n - 1))
        for b in reads:
            b.r.append(tok)
        out_buf.w = tok; out_buf.r = []
        return tok


def chunks(n, m):
    k = (n + m - 1) // m
    base = n // k
    assert base * k == n
    return [(i * base, base) for i in range(k)]


def build_program():
    nc = bass.Bass("TRN2", target_bir_lowering=False)
    es = ExitStack()
    with es:
        kk = K(nc, es); P = kk.P; OP = kk.OP; DMA = kk.DMA

        def din(name, shape, dt=F32):
            return nc.dram_tensor(name, list(shape), dt, kind="ExternalInput").ap()

        def dout(name, shape, dt=F32):
            return nc.dram_tensor(name, list(shape), dt, kind="ExternalOutput").ap()

        def dscr(name, shape, dt=F32):
            return nc.dram_tensor(name, list(shape), dt, kind="Internal").ap()

        xT = din("xT", [DTL, 128, NT]); cst_d = din("cst", [128, NCST])
        wg1 = din("wg1", [FFT, 128, 4096]); wu1 = din("wu1", [FFT, 128, 4096]); wd1 = din("wd1", [DTL, 128, DFF])
        wg2 = din("wg2", [FFT, 128, 4096]); wu2 = din("wu2", [FFT, 128, 4096]); wd2 = din("wd2", [DTL, 128, DFF])
        win = din("win", [112, 128, 4096]); wrg_d = din("wrg", [128, LT * 128]); wig_d = din("wig", [128, LT * 128])
        wglu = din("wglu", [LT, 128, 2048]); woa = din("woa", [DTL, 128, 2048]); wob = din("wob", [DTL, 128, 2048])
        wo = din("wo", [DTL, 128, 4096])
        h0s_d = din("h0s", [128, LT, NSQ]); conv0s_d = din("conv0s", [128, LT, NSQ, 3])
        ssm0r_d = din("ssm0r", [SP_, G, NSQ]); ssm0i_d = din("ssm0i", [SP_, G, NSQ])
        are_d = din("are", [SP_, G]); aim_d = din("aim", [SP_, G]); ldt_d = din("ldt", [SP_, G])
        bre_d = din("bre", [SP_, G, CG]); bim_d = din("bim", [SP_, G, CG]); cre_d = din("cre", [SP_, G, CG]); cim_d = din("cim", [SP_, G, CG])
        sel_d = din("sel", [128, 64 * 128]); selT_d = din("selT", [128, 64 * 128])
        mask_d = din("mask", [128, 128]); ident_d = din("ident", [128, 128]); dcol_d = din("dcol", [128, G])

        yT = dout("yT", [DTL, 128, NT]); lruh_o = dout("lruh", [128, LT, 17]); convo_o = dout("convo", [128, LT, 17, 3])
        ssmr_o = dout("ssmr", [SP_, G, 17]); ssmi_o = dout("ssmi", [SP_, G, 17])

        X1T = dscr("X1T", [DTL, 128, NT]); X2T = dscr("X2T", [DTL, 128, NT]); FSC = dscr("FSC", [DTL, 128, TB])
        XA = dscr("XA", [LT, 128, NT]); GA = dscr("GA", [LT, 128, NT]); XB = dscr("XB", [LT, 128, NT], BF16)
        YA = dscr("YA", [LT, 128, NT], BF16); YB = dscr("YB", [LT, 128, NT], BF16)
        BCd = dscr("BCd", [G, 128, 128], BF16); DCd = dscr("DCd", [G, 128, 128], BF16)
        CCRd = dscr("CCRd", [SP_, G, 128]); CCId = dscr("CCId", [SP_, G, 128])
        EXI = dscr("EXI", [128, 208]); EXO = dscr("EXO", [256, 208])
        def dbufs(n):
            return [Buf() for _ in range(n)]
        bX1 = dbufs(DTL); bX2 = dbufs(DTL); bFS = dbufs(DTL); bXA = dbufs(LT); bGA = dbufs(LT); bXB = dbufs(LT)
        bYA = dbufs(LT); bYB = dbufs(LT); bBC = dbufs(G); bDC = dbufs(G); bCC = dbufs(8); bEXI = Buf(); bEXO = Buf()
        bXin = dbufs(DTL); bYout = dbufs(DTL)

        cst = kk.sb(es, "cst", [128, NCST], F32); bcst = Buf()
        drv = kk.sb(es, "drv", [128, 3 * 32 + 16], F32); bdrv = Buf()
        ones = kk.sb(es, "ones", [128, 128], BF16); bones = Buf()
        rstd1g = kk.sb(es, "rstd1g", [128, NT], F32); brstd1g = Buf()
        NWS = 5
        wring = kk.ring(es, "wr", NWS, [128, WSLOT], BF16)
        psum = es.enter_context(nc.psum_tensor("ps", [128, 8, 512], F32))
        pbank = [Buf(psum[:, i, :]) for i in range(8)]
        prr = [0]

        def pget():
            b = pbank[prr[0]]; prr[0] = (prr[0] + 1) % 6
            return b

        def wload(src, n):
            k, b = wring.get()
            DMA("pool", b.ap[:, :n], src, f"w{k}", writes=[b])
            return b

        DMA("sp", cst[:, :], cst_d, "c0", writes=[bcst])
        OP("dve", lambda e: e.memset(ones[:, :], 1.0), writes=[bones])
        OP("dve", lambda e: e.tensor_scalar(out=drv[:, 0:32], in0=cst[:, C_G1POST:C_G1POST + 32], scalar1=0.5, scalar2=None, op0=ALU.mult), reads=[bcst], writes=[bdrv])
        OP("dve", lambda e: e.tensor_scalar(out=drv[:, 32:64], in0=cst[:, C_G2POST:C_G2POST + 32], scalar1=0.5, scalar2=None, op0=ALU.mult), reads=[bcst], writes=[bdrv])
        OP("act", lambda e: e.activation(out=drv[:, 96:112], in_=cst[:, C_LAM:C_LAM + 16], func=AF.Exp, scale=-1.0), reads=[bcst], writes=[bdrv])
        OP("act", lambda e: e.activation(out=drv[:, 96:112], in_=drv[:, 96:112], func=AF.Ln, bias=1.0), writes=[bdrv])
        OP("dve", lambda e: e.tensor_scalar(out=drv[:, 96:112], in0=drv[:, 96:112], scalar1=-8.0, scalar2=None, op0=ALU.mult), writes=[bdrv])
        GP05_1 = drv[:, 0:32]; GP05_2 = drv[:, 32:64]; C8 = drv[:, 96:112]

        HALVES = chunks(TB, 512)
        THIRDS = chunks(NT, 512)

        def norm_rstd(st, src, bsrc, t0, n, cks, rstd, brstd, xring, sqring):
            ssb = [pbank[6], pbank[7], pbank[5]][:len(cks)]
            for i in range(DTL):
                k, xb_ = xring.get()
                DMA("sp", xb_.ap[:, :n], src[i][:, t0:t0 + n], f"{xring.name}{k}", reads=[bsrc[i]], writes=[xb_])
                k2, sq = sqring.get()
                OP("act", lambda e, xb_=xb_, sq=sq: e.activation(out=sq.ap[:, :n], in_=xb_.ap[:, :n], func=AF.Square), reads=[xb_], writes=[sq])
                for ci, (o, m) in enumerate(cks):
                    deps = [sq.w]
                    if i == 0:
                        deps += [ssb[ci].w] + ssb[ci].r
                    t = P.op("pe", lambda e, ci=ci, o=o, m=m, sq=sq, i=i: e.matmul(ssb[ci].ap[:, :m], lhsT=ones[:, :], rhs=sq.ap[:, o:o + m], start=(i == 0), stop=(i == DTL - 1)), deps=deps + [bones.w])
                    sq.r.append(t)
                    if i == DTL - 1:
                        ssb[ci].w = t; ssb[ci].r = []
            for ci, (o, m) in enumerate(cks):
                OP("act", lambda e, ci=ci, o=o, m=m: e.activation(out=rstd[:, o:o + m], in_=ssb[ci].ap[:, :m], func=AF.Sqrt, scale=1.0 / D, bias=EPS), reads=[ssb[ci]], writes=[brstd])
            OP("dve", lambda e: e.reciprocal(out=rstd[:, :n], in_=rstd[:, :n]), writes=[brstd])

        def apply_norm(src, bsrc, t0, n, gcol, rstd, brstd, dstT, bdst, xring):
            for i in range(DTL):
                k, xb_ = xring.get()
                DMA("sp", xb_.ap[:, :n], src[i][:, t0:t0 + n], f"{xring.name}{k}", reads=[bsrc[i]], writes=[xb_])
                OP("dve", lambda e, xb_=xb_, i=i: e.scalar_tensor_tensor(out=dstT[:, i, :n], in0=xb_.ap[:, :n], scalar=cst[:, gcol + i:gcol + i + 1], in1=rstd[:, :n], op0=ALU.mult, op1=ALU.mult),
                   reads=[xb_, brstd, bcst], writes=[bdst[i]])

        def resid(srcx, bsrcx, t0, n, gsc, rstd, brstd, dst, bdst, xring, fring, keep=None, cks=None):
            ssb = [pbank[6], pbank[7]]
            for j in range(DTL):
                k, fb = fring.get()
                DMA("sp", fb.ap[:, :n], FSC[j][:, :n], f"{fring.name}{k}", reads=[bFS[j]], writes=[fb])
                k2, xb_ = xring.get()
                DMA("sp", xb_.ap[:, :n], srcx[j][:, t0:t0 + n], f"{xring.name}{k2}", reads=[bsrcx[j]], writes=[xb_])
                OP("dve", lambda e, fb=fb, j=j: e.scalar_tensor_tensor(out=fb.ap[:, :n], in0=fb.ap[:, :n], scalar=gsc[:, j:j + 1], in1=rstd[:, :n], op0=ALU.mult, op1=ALU.mult),
                   reads=[brstd, bdrv, bcst], writes=[fb])
                if keep is None:
                    OP("dve", lambda e, fb=fb, xb_=xb_: e.tensor_tensor(out=xb_.ap[:, :n], in0=fb.ap[:, :n], in1=xb_.ap[:, :n], op=ALU.add), reads=[fb], writes=[xb_])
                    DMA("sp", dst[j][:, t0:t0 + n], xb_.ap[:, :n], f"{xring.name}s{k2}", reads=[xb_], writes=[bdst[j]])
                else:
                    x1buf, bx1, sqring = keep
                    OP("dve", lambda e, fb=fb, xb_=xb_, j=j: e.tensor_tensor(out=x1buf[:, j, :n], in0=fb.ap[:, :n], in1=xb_.ap[:, :n], op=ALU.add), reads=[fb, xb_], writes=[bx1[j]])
                    DMA("sp", dst[j][:, t0:t0 + n], x1buf[:, j, :n], f"x1s{j % 4}", reads=[bx1[j]], writes=[bdst[j]])
                    k3, sq = sqring.get()
                    OP("act", lambda e, sq=sq, j=j: e.activation(out=sq.ap[:, :n], in_=x1buf[:, j, :n], func=AF.Square), reads=[bx1[j]], writes=[sq])
                    for ci, (o, m) in enumerate(cks):
                        deps = [sq.w, bones.w]
                        if j == 0:
                            deps += [ssb[ci].w] + ssb[ci].r
                        t = P.op("pe", lambda e, ci=ci, o=o, m=m, sq=sq, j=j: e.matmul(ssb[ci].ap[:, :m], lhsT=ones[:, :], rhs=sq.ap[:, o:o + m], start=(j == 0), stop=(j == DTL - 1)), deps=deps)
                        sq.r.append(t)
                        if j == DTL - 1:
                            ssb[ci].w = t; ssb[ci].r = []

        def proj_to_fsc(st, nk, wsrc_fn, rhsT, brhs, n, cks, sqring, fring):
            ssb = [pbank[6], pbank[7]]
            for j in range(DTL):
                pieces = wsrc_fn(j)
                k, fb = fring.get()
                pbs = [pget() for _ in cks]
                for pi, (wb, kc0, nkc) in enumerate(pieces):
                    for ci, (o, m) in enumerate(cks):
                        pb = pbs[ci]
                        kk.mm(pb, pb.ap[:, :m], [(wb.ap[:, q * 128:(q + 1) * 128], rhsT[:, kc0 + q, o:o + m]) for q in range(nkc)],
                              reads=[wb] + (list(brhs) if pi == 0 else []), start=(pi == 0), stop=(pi == len(pieces) - 1), sgc=True)
                for ci, (o, m) in enumerate(cks):
                    pb = pbs[ci]
                    OP("act", lambda e, pb=pb, fb=fb, o=o, m=m: e.activation(out=fb.ap[:, o:o + m], in_=pb.ap[:, :m], func=AF.Copy), reads=[pb], writes=[fb])
                k2, sq = sqring.get()
                OP("act", lambda e, fb=fb, sq=sq: e.activation(out=sq.ap[:, :n], in_=fb.ap[:, :n], func=AF.Square), reads=[fb], writes=[sq])
                for ci, (o, m) in enumerate(cks):
                    deps = [sq.w, bones.w]
                    if j == 0:
                        deps += [ssb[ci].w] + ssb[ci].r
                    t = P.op("pe", lambda e, ci=ci, o=o, m=m, sq=sq, j=j: e.matmul(ssb[ci].ap[:, :m], lhsT=ones[:, :], rhs=sq.ap[:, o:o + m], start=(j == 0), stop=(j == DTL - 1)), deps=deps)
                    sq.r.append(t)
                    if j == DTL - 1:
                        ssb[ci].w = t; ssb[ci].r = []
                DMA("sp", FSC[j][:, :n], fb.ap[:, :n], f"{fring.name}s{k}", reads=[fb], writes=[bFS[j]])

        def finish_rstd(cks, rstd, brstd, n):
            ssb = [pbank[6], pbank[7]]
            for ci, (o, m) in enumerate(cks):
                OP("act", lambda e, ci=ci, o=o, m=m: e.activation(out=rstd[:, o:o + m], in_=ssb[ci].ap[:, :m], func=AF.Sqrt, scale=1.0 / D, bias=EPS), reads=[ssb[ci]], writes=[brstd])
            OP("dve", lambda e: e.reciprocal(out=rstd[:, :n], in_=rstd[:, :n]), writes=[brstd])

        def ffn_stage(src, bsrc, dst, bdst, t0, gprecol, gp05, Wg, Wu, Wd, tail_fn=None):
            with ExitStack() as st:
                n = TB
                hT = kk.sb(st, "hT", [128, DTL, TB], BF16); bhT = dbufs(DTL)
                hid = kk.sb(st, "hid", [128, max(FFT, 64), TB], BF16); bhid = dbufs(FFT)
                rstd = kk.sb(st, "rstd", [128, TB], F32); brstd = Buf()
                xring = kk.ring(st, "xr", 3, [128, TB], F32)
                fring = kk.ring(st, "fr", 2, [128, TB], F32)
                sqring = kk.ring(st, "sq", 2, [128, TB], BF16)
                sgring = kk.ring(st, "sg", 2, [128, 288], F32)
                norm_rstd(st, src, bsrc, t0, n, HALVES, rstd, brstd, xring, sqring)
                apply_norm(src, bsrc, t0, n, gprecol, rstd, brstd, hT, bhT, xring)
                for f in range(FFT):
                    sg = wload(Wg[f], 4096); su = wload(Wu[f], 4096)
                    for (o, m) in HALVES:
                        pg = pget(); pu = pget()
                        kk.mm(pg, pg.ap[:, :m], [(sg.ap[:, q * 128:(q + 1) * 128], hT[:, q, o:o + m]) for q in range(DTL)], reads=[sg] + bhT)
                        kk.mm(pu, pu.ap[:, :m], [(su.ap[:, q * 128:(q + 1) * 128], hT[:, q, o:o + m]) for q in range(DTL)], reads=[su] + bhT)
                        k, sb_ = sgring.get()
                        OP("act", lambda e, pg=pg, sb_=sb_, m=m: e.activation(out=sb_.ap[:, :m], in_=pg.ap[:, :m], func=AF.Silu), reads=[pg], writes=[sb_])
                        OP("dve", lambda e, pu=pu, sb_=sb_, f=f, o=o, m=m: e.tensor_tensor(out=hid[:, f, o:o + m], in0=sb_.ap[:, :m], in1=pu.ap[:, :m], op=ALU.mult),
                           reads=[sb_, pu], writes=[bhid[f]])
                def wsrc(j):
                    out = []
                    for (kc0, nkc) in [(k0, min(32, FFT - k0)) for k0 in range(0, FFT, 32)]:
                        b = wload(Wd[j][:, kc0 * 128:(kc0 + nkc) * 128], nkc * 128)
                        out.append((b, kc0, nkc))
                    return out
                proj_to_fsc(st, FFT, wsrc, hid, bhid, n, HALVES, sqring, fring)
                finish_rstd(HALVES, rstd, brstd, n)
                if tail_fn is None:
                    resid(src, bsrc, t0, n, gp05, rstd, brstd, dst, bdst, xring, fring)
                else:
                    x1buf = hid[:, :, :].rearrange("p f n -> p (f n)").bitcast(F32)[:, 0:DTL * TB].rearrange("p (i n) -> p i n", n=TB)
                    bx1 = dbufs(DTL)
                    resid(src, bsrc, t0, n, gp05, rstd, brstd, dst, bdst, xring, fring, keep=(x1buf, bx1, sqring), cks=HALVES)
                    tail_fn(st, t0, hT, bhT, rstd, brstd, xring, fring, sqring, x1buf, bx1)
                P.barrier()

        def win_tail(st, t0, uT, buT, rstd, brstd, xring, fring, sqring, x1buf, bx1):
            n = TB
            finish_rstd(HALVES, rstd, brstd, n)
            OP("dve", lambda e: e.tensor_copy(out=rstd1g[:, t0:t0 + n], in_=rstd[:, :n]), reads=[brstd], writes=[brstd1g])
            for i in range(DTL):
                OP("dve", lambda e, i=i: e.scalar_tensor_tensor(out=uT[:, i, :n], in0=x1buf[:, i, :n], scalar=cst[:, C_GMPRE + i:C_GMPRE + i + 1], in1=rstd[:, :n], op0=ALU.mult, op1=ALU.mult),
                   reads=[bx1[i], brstd, bcst], writes=[buT[i]])
            for o_ in range(48):
                wb = wload(win[o_], 4096)
                kind = o_ // 16; ti = o_ % 16
                k, fb = fring.get()
                for (o, m) in HALVES:
                    pb = pget()
                    kk.mm(pb, pb.ap[:, :m], [(wb.ap[:, q * 128:(q + 1) * 128], uT[:, q, o:o + m]) for q in range(DTL)], reads=[wb] + buT)
                    if kind == 1:
                        OP("act", lambda e, pb=pb, fb=fb, o=o, m=m: e.activation(out=fb.ap[:, o:o + m], in_=pb.ap[:, :m], func=AF.Gelu), reads=[pb], writes=[fb])
                    elif kind == 0:
                        OP("act", lambda e, pb=pb, fb=fb, o=o, m=m: e.activation(out=fb.ap[:, o:o + m], in_=pb.ap[:, :m], func=AF.Copy), reads=[pb], writes=[fb])
                    else:
                        fbb = fb.ap.bitcast(BF16)
                        OP("act", lambda e, pb=pb, fbb=fbb, o=o, m=m: e.activation(out=fbb[:, o:o + m], in_=pb.ap[:, :m], func=AF.Copy), reads=[pb], writes=[fb])
                if kind == 0:
                    DMA("sp", XA[ti][:, t0:t0 + n], fb.ap[:, :n], f"{fring.name}s{k}", reads=[fb], writes=[bXA[ti]])
                elif kind == 1:
                    DMA("sp", GA[ti][:, t0:t0 + n], fb.ap[:, :n], f"{fring.name}s{k}", reads=[fb], writes=[bGA[ti]])
                else:
                    DMA("sp", XB[ti][:, t0:t0 + n], fb.ap.bitcast(BF16)[:, :n], f"{fring.name}s{k}", reads=[fb], writes=[bXB[ti]])

        lamr = kk.sb(es, "lamr", [SP_, G], F32); lami = kk.sb(es, "lami", [SP_, G], F32); blam = Buf()
        def s5_setup():
            with ExitStack() as st:
                GB = 16
                pg = lambda nm: kk.sb(st, nm, [SP_, G], F32)
                are = pg("are"); aim = pg("aim"); ldt = pg("ldt"); t1 = pg("t1"); t2 = pg("t2"); t3 = pg("t3")
                cc_ = pg("cc_"); ss_ = pg("ss_"); coefr = pg("coefr"); coefi = pg("coefi"); invr = pg("invr"); invi = pg("invi")
                pwr = kk.sb(st, "pwr", [SP_, 16, G], F32); pwi = kk.sb(st, "pwi", [SP_, 16, G], F32)
                ident = kk.sb(st, "ident", [128, 128], F32); mask = kk.sb(st, "mask", [128, 128], F32); dcol = kk.sb(st, "dcol", [128, G], F32)
                B = Buf()
                for (t, d, key) in ((are, are_d, "s0"), (aim, aim_d, "s0"), (ldt, ldt_d, "s0"), (ident, ident_d, "s0"), (mask, mask_d, "s0"), (dcol, dcol_d, "s0")):
                    DMA("sp", t[:, :], d, key, writes=[B])
                V = lambda fn: OP("dve", fn, writes=[B])
                A = lambda fn: OP("act", fn, writes=[B])
                tt = lambda o, a, b, op: V(lambda e: e.tensor_tensor(out=o, in0=a, in1=b, op=op))
                A(lambda e: e.activation(out=ldt[:, :], in_=ldt[:, :], func=AF.Exp))
                tt(t1[:, :], are[:, :], ldt[:, :], ALU.mult)
                tt(t2[:, :], aim[:, :], ldt[:, :], ALU.mult)
                A(lambda e: e.activation(out=t1[:, :], in_=t1[:, :], func=AF.Exp))
                A(lambda e: e.activation(out=ss_[:, :], in_=t2[:, :], func=AF.Sin, scale=1.0 / 32))
                A(lambda e: e.activation(out=cc_[:, :], in_=t2[:, :], func=AF.Sin, scale=1.0 / 32, bias=math.pi / 2))
                for _ in range(5):
                    tt(t2[:, :], cc_[:, :], ss_[:, :], ALU.mult)
                    tt(cc_[:, :], cc_[:, :], cc_[:, :], ALU.mult)
                    tt(ss_[:, :], ss_[:, :], ss_[:, :], ALU.mult)
                    tt(cc_[:, :], cc_[:, :], ss_[:, :], ALU.subtract)
                    V(lambda e: e.tensor_scalar(out=ss_[:, :], in0=t2[:, :], scalar1=2.0, scalar2=None, op0=ALU.mult))
                ar = pwr[:, 8, :]; ai = pwi[:, 8, :]
                tt(ar, t1[:, :], cc_[:, :], ALU.mult); tt(ai, t1[:, :], ss_[:, :], ALU.mult)
                V(lambda e: e.memset(pwr[:, 7, :], 1.0)); V(lambda e: e.memset(pwi[:, 7, :], 0.0))

                def cmul(o_r, o_i, a_r, a_i, b_r, b_i, ta, tb, neg_im=False):
                    tt(ta, a_r, b_r, ALU.mult); tt(tb, a_i, b_i, ALU.mult); tt(o_r, ta, tb, ALU.subtract)
                    tt(ta, a_r, b_i, ALU.mult); tt(tb, a_i, b_r, ALU.mult)
                    if neg_im:
                        V(lambda e: e.scalar_tensor_tensor(out=o_i, in0=ta, scalar=-1.0, in1=tb, op0=ALU.mult, op1=ALU.subtract))
                    else:
                        tt(o_i, ta, tb, ALU.add)
                for j in range(2, 9):
                    cmul(pwr[:, j + 7, :], pwi[:, j + 7, :], pwr[:, j + 6, :], pwi[:, j + 6, :], ar, ai, t2[:, :], t3[:, :])
                tt(t2[:, :], ar, ar, ALU.mult); tt(t3[:, :], ai, ai, ALU.mult); tt(t2[:, :], t2[:, :], t3[:, :], ALU.add)
                V(lambda e: e.reciprocal(out=t2[:, :], in_=t2[:, :]))
                tt(invr[:, :], ar, t2[:, :], ALU.mult)
                V(lambda e: e.scalar_tensor_tensor(out=invi[:, :], in0=ai, scalar=-1.0, in1=t2[:, :], op0=ALU.mult, op1=ALU.mult))
                V(lambda e: e.tensor_copy(out=pwr[:, 6, :], in_=invr[:, :])); V(lambda e: e.tensor_copy(out=pwi[:, 6, :], in_=invi[:, :]))
                for j in range(2, 8):
                    cmul(pwr[:, 7 - j, :], pwi[:, 7 - j, :], pwr[:, 8 - j, :], pwi[:, 8 - j, :], invr[:, :], invi[:, :], t2[:, :], t3[:, :])
                tt(t2[:, :], are[:, :], are[:, :], ALU.mult); tt(t3[:, :], aim[:, :], aim[:, :], ALU.mult); tt(t2[:, :], t2[:, :], t3[:, :], ALU.add)
                V(lambda e: e.reciprocal(out=t2[:, :], in_=t2[:, :]))
                V(lambda e: e.tensor_scalar(out=t1[:, :], in0=ar, scalar1=-1.0, scalar2=None, op0=ALU.add))
                tt(t3[:, :], t1[:, :], are[:, :], ALU.mult); tt(cc_[:, :], ai, aim[:, :], ALU.mult); tt(t3[:, :], t3[:, :], cc_[:, :], ALU.add)
                tt(coefr[:, :], t3[:, :], t2[:, :], ALU.mult)
                tt(t3[:, :], ai, are[:, :], ALU.mult); tt(cc_[:, :], t1[:, :], aim[:, :], ALU.mult); tt(t3[:, :], t3[:, :], cc_[:, :], ALU.subtract)
                tt(coefi[:, :], t3[:, :], t2[:, :], ALU.mult)
                V(lambda e: e.tensor_copy(out=lamr[:, :], in_=pwr[:, 15, :])); V(lambda e: e.tensor_copy(out=lami[:, :], in_=pwi[:, 15, :]))
                blam.w = B.w

                gsh = [SP_, GB, CG]
                cur = [B]
                V = lambda fn: OP("dve", fn, reads=[B], writes=[cur[0]])
                tt = lambda o, a, b, op: V(lambda e: e.tensor_tensor(out=o, in0=a, in1=b, op=op))

                def cmul(o_r, o_i, a_r, a_i, b_r, b_i, ta, tb, neg_im=False):
                    tt(ta, a_r, b_r, ALU.mult); tt(tb, a_i, b_i, ALU.mult); tt(o_r, ta, tb, ALU.subtract)
                    tt(ta, a_r, b_i, ALU.mult); tt(tb, a_i, b_r, ALU.mult)
                    if neg_im:
                        V(lambda e: e.scalar_tensor_tensor(out=o_i, in0=ta, scalar=-1.0, in1=tb, op0=ALU.mult, op1=ALU.subtract))
                    else:
                        tt(o_i, ta, tb, ALU.add)
                bsets = []
                for si in range(2):
                    bt = lambda nm: kk.sb(st, nm, gsh, F32)
                    T_ = dict(br_=bt("br_"), bi_=bt("bi_"), cr_=bt("cr_"), ci_=bt("ci_"), bbr=bt("bbr"), bbi=bt("bbi"), ta=bt("ta"), tb=bt("tb"),
                              Wr=kk.sb(st, "Wr", [SP_, GB, 8, CG], F32), Wi=kk.sb(st, "Wi", [SP_, GB, 8, CG], F32),
                              Xr=kk.sb(st, "Xr", [SP_, GB, 8, CG], F32), Xi=kk.sb(st, "Xi", [SP_, GB, 8, CG], F32),
                              Yr=kk.sb(st, "Yr", [SP_, GB, 9, CG], F32), Yi=kk.sb(st, "Yi", [SP_, GB, 9, CG], F32), B=Buf())
                    bsets.append(T_)
                ostg = kk.ring(st, "ostg", 3, [128, 256], BF16)
                tmpm = kk.ring(st, "tmpm", 3, [128, 128], F32)
                def unpack(gb):
                    T_ = bsets[gb % 2]
                    return T_, T_["B"]

                def batch_elem(gb):
                    g0 = gb * GB
                    T_, Bk = unpack(gb)
                    br_ = T_["br_"]; bi_ = T_["bi_"]; cr_ = T_["cr_"]; ci_ = T_["ci_"]; bbr = T_["bbr"]; bbi = T_["bbi"]; ta = T_["ta"]; tb = T_["tb"]
                    Wr = T_["Wr"]; Wi = T_["Wi"]; Xr = T_["Xr"]; Xi = T_["Xi"]; Yr = T_["Yr"]; Yi = T_["Yi"]
                    for (t, d, key) in ((br_, bre_d, "s6"), (bi_, bim_d, "s6"), (cr_, cre_d, "s6"), (ci_, cim_d, "s6")):
                        DMA("sp", t[:, :, :], d[:, g0:g0 + GB, :], f"{key}_{gb % 2}", writes=[Bk])
                    bc = lambda ap2: ap2.unsqueeze(2).to_broadcast(gsh)
                    cur[0] = Bk
                    cmul(bbr[:, :, :], bbi[:, :, :], bc(coefr[:, g0:g0 + GB]), bc(coefi[:, g0:g0 + GB]), br_[:, :, :], bi_[:, :, :], ta[:, :, :], tb[:, :, :])
                    yield
                    for s_ in range(8):
                        cur[0] = Bk
                        cmul(Wr[:, :, s_, :], Wi[:, :, s_, :], bc(pwr[:, 14 - s_, g0:g0 + GB]), bc(pwi[:, 14 - s_, g0:g0 + GB]), bbr[:, :, :], bbi[:, :, :], ta[:, :, :], tb[:, :, :])
                        yield
                        cur[0] = Bk
                        cmul(Xr[:, :, s_, :], Xi[:, :, s_, :], bc(pwr[:, 7 - s_, g0:g0 + GB]), bc(pwi[:, 7 - s_, g0:g0 + GB]), bbr[:, :, :], bbi[:, :, :], ta[:, :, :], tb[:, :, :])
                        yield
                    for t_ in range(9):
                        cur[0] = Bk
                        cmul(Yr[:, :, t_, :], Yi[:, :, t_, :], cr_[:, :, :], ci_[:, :, :], bc(pwr[:, 7 + t_, g0:g0 + GB]), bc(pwi[:, 7 + t_, g0:g0 + GB]), ta[:, :, :], tb[:, :, :], neg_im=True)
                        yield
                    DMA("sp", CCRd[:, g0:g0 + GB, :].rearrange("p g (t c) -> p g t c", c=CG), Yr[:, :, 1:9, :], f"s10_{gb % 2}", reads=[Bk], writes=[bCC[gb]])
                    DMA("sp", CCId[:, g0:g0 + GB, :].rearrange("p g (t c) -> p g t c", c=CG), Yi[:, :, 1:9, :], f"s10_{gb % 2}", reads=[Bk], writes=[bCC[gb]])

                def batch_groups(gb):
                    g0 = gb * GB
                    T_, Bk = unpack(gb)
                    Wr = T_["Wr"]; Wi = T_["Wi"]; Xr = T_["Xr"]; Xi = T_["Xi"]; Yr = T_["Yr"]; Yi = T_["Yi"]
                    for gl in range(GB):
                        g = g0 + gl
                        k, ob = ostg.get()
                        for half, Wsrc in enumerate((Wr, Wi)):
                            pb = pget()
                            OP("pe", lambda e, pb=pb, Wsrc=Wsrc, gl=gl: e.transpose(pb.ap[:, 0:SP_], Wsrc[:, gl, :, :].rearrange("p s c -> p (s c)"), ident[0:SP_, 0:SP_]), reads=[Bk, B], writes=[pb])
                            OP("act", lambda e, pb=pb, ob=ob, half=half: e.activation(out=ob.ap[:, half * 64:(half + 1) * 64], in_=pb.ap[:, 0:SP_], func=AF.Copy), reads=[pb], writes=[ob])
                        pb = pget()
                        kk.mm(pb, pb.ap[:, 0:128], [(Xr[:, gl, :, :].rearrange("p s c -> p (s c)"), Yr[:, gl, 0:8, :].rearrange("p t c -> p (t c)")),
                                                    (Xi[:, gl, :, :].rearrange("p s c -> p (s c)"), Yi[:, gl, 0:8, :].rearrange("p t c -> p (t c)"))], reads=[Bk])
                        k2, tm = tmpm.get()
                        OP("dve", lambda e, pb=pb, tm=tm: e.tensor_tensor(out=tm.ap[:, :], in0=pb.ap[:, 0:128], in1=mask[:, :], op=ALU.mult), reads=[pb, B], writes=[tm])
                        OP("dve", lambda e, tm=tm, ob=ob, g=g: e.scalar_tensor_tensor(out=ob.ap[:, 128:256], in0=ident[:, :], scalar=dcol[:, g:g + 1], in1=tm.ap[:, :], op0=ALU.mult, op1=ALU.add),
                           reads=[tm, B], writes=[ob])
                        DMA("sp", BCd[g], ob.ap[:, 0:128], f"ostg{k}", reads=[ob], writes=[bBC[g]])
                        DMA("sp", DCd[g], ob.ap[:, 128:256], f"ostg{k}", reads=[ob], writes=[bDC[g]])
                        yield

                NBAT = G // GB
                for _ in batch_elem(0):
                    pass
                for gb in range(NBAT):
                    gens = [batch_groups(gb)] + ([batch_elem(gb + 1)] if gb + 1 < NBAT else [])
                    while gens:
                        for g_ in list(gens):
                            try:
                                next(g_)
                            except StopIteration:
                                gens.remove(g_)
                P.barrier()

        def stage2():
            with ExitStack() as st:
                U = kk.sb(st, "U", [128, G, NK], BF16); bU = dbufs(G)
                hsr = kk.sb(st, "hsr", [SP_, G, NSQ], F32); hsi = kk.sb(st, "hsi", [SP_, G, NSQ], F32); bhs = Buf()
                exg = kk.sb(st, "exg", [128, 208], F32); exg2 = kk.sb(st, "exg2", [SP_, G], F32); bexg = Buf()
                wrg = kk.sb(st, "wrg", [128, LT * 128], BF16); wig = kk.sb(st, "wig", [128, LT * 128], BF16); bwg = Buf()
                h0s = kk.sb(st, "h0s", [128, LT, NSQ], F32); bh0 = Buf()
                DMA("pool", wrg[:, :], wrg_d, "c1", writes=[bwg], extra=P.bar)
                DMA("pool", wig[:, :], wig_d, "c1", writes=[bwg], extra=P.bar)
                DMA("sp", hsr[:, :, :], ssm0r_d, "hs0", writes=[bhs]); DMA("sp", hsi[:, :, :], ssm0i_d, "hs0", writes=[bhs])
                DMA("sp", h0s[:, :, :], h0s_d, "hs0", writes=[bh0])
                with ExitStack() as s2:
                    sel = kk.sb(s2, "sel", [128, 64, 128], BF16); bsel = Buf()
                    DMA("pool", sel[:, :, :].rearrange("p a b -> p (a b)"), sel_d, "sel", writes=[bsel], extra=P.bar)
                    xbring = kk.ring(s2, "xbr", 2, [128, NT], BF16)
                    for T in range(LT):
                        k, xb_ = xbring.get()
                        DMA("sp", xb_.ap[:, :], XB[T], f"xbr{k}", reads=[bXB[T]], writes=[xb_])
                        xv = xb_.ap.rearrange("p (k s) -> p k s", s=8)
                        for gl in range(8):
                            g = T * 8 + gl
                            pb = pget()
                            kk.mm(pb, pb.ap[:, :NK], [(sel[:, gl * 8 + s, :], xv[:, :, s]) for s in range(8)], reads=[xb_, bsel])
                            OP("act", lambda e, pb=pb, g=g: e.activation(out=U[:, g, :], in_=pb.ap[:, :NK], func=AF.Copy), reads=[pb], writes=[bU[g]])
                    P.barrier()

                def lru_round(final):
                    with ExitStack() as s2:
                        lout = kk.sb(s2, "lout", [128, LT, 17], F32); blout = Buf()
                        sets = []
                        for si in range(3):
                            lt = lambda nm, w=NT, dt=F32, si=si: kk.sb(s2, f"{nm}{si}", [128, w], dt)
                            S = dict(xap=lt("xap", 1203), xc=lt("xc"), xcb=lt("xcb", NT, BF16), rr=lt("rr"), ii=lt("ii"), mm_=lt("mm_"),
                                     yab=lt("yab", NT, BF16), tsm=kk.sb(s2, f"tsm{si}", [128, 16], F32), bL=Buf(), si=si)
                            sets.append(S)

                        def head(h, S):
                            xap = S["xap"]; xc = S["xc"]; xcb = S["xcb"]; rr = S["rr"]; ii = S["ii"]; mm_ = S["mm_"]; yab = S["yab"]; tsm = S["tsm"]; bL = S["bL"]; si = S["si"]
                            aa = rr; hh = xc; gg = mm_
                            xaps = xap[:, 1027:1203].rearrange("p (b t) -> p b t", t=11)
                            L_ = lambda eng, fn, extra=(): OP(eng, fn, writes=[bL], extra=extra)
                            DMA("sp", xap[:, 3:1027], XA[h][:, 0:TP], f"xa0{si}", reads=[bXA[h]], writes=[bL])
                            DMA("sp", xaps[:, :, 3:11], XA[h][:, TP:NT].rearrange("p (b t) -> p b t", t=8), f"xa0{si}", reads=[bXA[h]], writes=[bL])
                            DMA("sp", xaps[:, :, 0:3], conv0s_d[:, h, :, :], f"xa0{si}", writes=[bL])
                            yield
                            if not final:
                                L_("dve", lambda e: e.memset(xap[:, 0:3], 0.0))
                            else:
                                L_("dve", lambda e: e.tensor_scalar(out=xap[:, 0:3], in0=exg[:, 4 * h:4 * h + 3], scalar1=cst[:, C_FLAG:C_FLAG + 1], scalar2=None, op0=ALU.mult), extra=[bexg.w])
                            cw = lambda k_: cst[:, C_CONVW + 4 * h + k_:C_CONVW + 4 * h + k_ + 1]
                            cb = cst[:, C_CONVB + h:C_CONVB + h + 1]
                            xcs = xc[:, TP:NT].rearrange("p (b t) -> p b t", t=8)
                            L_("dve", lambda e: e.tensor_scalar(out=xc[:, 0:TP], in0=xap[:, 3:1027], scalar1=cw(3), scalar2=cb, op0=ALU.mult, op1=ALU.add))
                            L_("dve", lambda e: e.tensor_scalar(out=xcs, in0=xaps[:, :, 3:11], scalar1=cw(3), scalar2=cb, op0=ALU.mult, op1=ALU.add))
                            yield
                            for k_ in range(3):
                                L_("dve", lambda e, k_=k_: e.scalar_tensor_tensor(out=xc[:, 0:TP], in0=xap[:, k_:k_ + TP], scalar=cw(k_), in1=xc[:, 0:TP], op0=ALU.mult, op1=ALU.add))
                                L_("dve", lambda e, k_=k_: e.scalar_tensor_tensor(out=xcs, in0=xaps[:, :, k_:k_ + 8], scalar=cw(k_), in1=xcs, op0=ALU.mult, op1=ALU.add))
                                yield
                            if final:
                                DMA("sp", convo_o[:, h, 0, :], xap[:, 1024:1027], f"co0{si}", reads=[bL])
                                DMA("sp", convo_o[:, h, 1:17, :], xaps[:, :, 8:11], f"co0{si}", reads=[bL])
                            else:
                                DMA("sp", EXI[:, 4 * h:4 * h + 3], xap[:, 1024:1027], f"co0{si}", reads=[bL], writes=[bEXI])
                            L_("act", lambda e: e.activation(out=xcb[:, :], in_=xc[:, :], func=AF.Copy))
                            yield
                            for (o, m) in THIRDS:
                                pr = pget(); pi_ = pget()
                                kk.mm(pr, pr.ap[:, :m], [(wrg[:, h * 128:(h + 1) * 128], xcb[:, o:o + m])], reads=[bL, bwg])
                                kk.mm(pi_, pi_.ap[:, :m], [(wig[:, h * 128:(h + 1) * 128], xcb[:, o:o + m])], reads=[bL, bwg])
                                OP("act", lambda e, pr=pr, o=o, m=m: e.activation(out=rr[:, o:o + m], in_=pr.ap[:, :m], func=AF.Sigmoid, bias=cst[:, C_BRG + h:C_BRG + h + 1]), reads=[pr], writes=[bL])
                                OP("act", lambda e, pi_=pi_, o=o, m=m: e.activation(out=ii[:, o:o + m], in_=pi_.ap[:, :m], func=AF.Sigmoid, bias=cst[:, C_BIG + h:C_BIG + h + 1]), reads=[pi_], writes=[bL])
                                yield
                            L_("act", lambda e: e.activation(out=aa[:, :], in_=rr[:, :], func=AF.Exp, scale=C8[:, h:h + 1]))
                            yield
                            L_("dve", lambda e: e.tensor_tensor(out=mm_[:, :], in0=aa[:, :], in1=aa[:, :], op=ALU.mult))
                            yield
                            L_("act", lambda e: e.activation(out=mm_[:, :], in_=mm_[:, :], func=AF.Sqrt, scale=-1.0, bias=1.0))
                            L_("dve", lambda e: e.tensor_tensor(out=ii[:, :], in0=ii[:, :], in1=xc[:, :], op=ALU.mult))
                            yield
                            L_("dve", lambda e: e.tensor_tensor(out=ii[:, :], in0=ii[:, :], in1=mm_[:, :], op=ALU.mult))
                            as_ = aa[:, TP:NT].rearrange("p (b t) -> p b t", t=8)[:, :, 0]
                            bs_ = ii[:, TP:NT].rearrange("p (b t) -> p b t", t=8)[:, :, 0]
                            L_("dve", lambda e: e.tensor_tensor(out=tsm[:, :], in0=as_, in1=h0s[:, h, :], op=ALU.mult), extra=[bh0.w])
                            L_("dve", lambda e: e.tensor_tensor(out=bs_, in0=bs_, in1=tsm[:, :], op=ALU.add))
                            yield
                            if final:
                                L_("dve", lambda e: e.tensor_scalar(out=tsm[:, 0:1], in0=exg[:, 64 + h:65 + h], scalar1=cst[:, C_FLAG:C_FLAG + 1], scalar2=None, op0=ALU.mult), extra=[bexg.w])
                                L_("dve", lambda e: e.scalar_tensor_tensor(out=ii[:, 0:1], in0=aa[:, 0:1], scalar=tsm[:, 0:1], in1=ii[:, 0:1], op0=ALU.mult, op1=ALU.add))
                            L_("dve", lambda e: e.memset(aa[:, 0:1], 0.0))
                            L_("dve", lambda e: e.memset(as_, 0.0))
                            yield
                            L_("dve", lambda e: e.tensor_tensor_scan(out=hh[:, :], data0=aa[:, :], data1=ii[:, :], initial=0.0, op0=ALU.mult, op1=ALU.add))
                            yield
                            if not final:
                                OP("dve", lambda e: e.tensor_copy(out=lout[:, 0, h:h + 1], in_=hh[:, TP - 1:TP]), reads=[bL], writes=[blout])
                            else:
                                OP("dve", lambda e: e.tensor_copy(out=lout[:, h, 0:1], in_=hh[:, TP - 1:TP]), reads=[bL], writes=[blout])
                                OP("dve", lambda e: e.tensor_copy(out=lout[:, h, 1:17], in_=hh[:, TP:NT].rearrange("p (b t) -> p b t", t=8)[:, :, 7]), reads=[bL], writes=[blout])
                                DMA("sp", gg[:, :], GA[h], f"ga{si}", reads=[bGA[h]], writes=[bL])
                                yield
                                L_("dve", lambda e: e.tensor_tensor(out=yab[:, :], in0=hh[:, :], in1=gg[:, :], op=ALU.mult))
                                DMA("sp", YA[h], yab[:, :], f"ya{si}", reads=[bL], writes=[bYA[h]])
                            yield

                        pending = list(range(LT)); active = []
                        free_sets = list(sets)
                        while pending or active:
                            while pending and free_sets:
                                S = free_sets.pop(0)
                                active.append((head(pending.pop(0), S), S))
                            nxt = []
                            for (g_, S) in active:
                                try:
                                    next(g_); nxt.append((g_, S))
                                except StopIteration:
                                    free_sets.append(S)
                            active = nxt
                        if final:
                            DMA("sp", lruh_o, lout[:, :, :], "lo", reads=[blout])
                        else:
                            DMA("sp", EXI[:, 64:80], lout[:, 0, 0:16], "ex1", reads=[blout], writes=[bEXI])
                        P.barrier()

                def s5_pass(final):
                    with ExitStack() as s2:
                        Ar = kk.sb(s2, "Ar", [SP_, 64, 145], F32); Ai = kk.sb(s2, "Ai", [SP_, 64, 145], F32); bA = Buf()
                        mring = kk.ring(s2, "mr", 3, [128, 4, 128], BF16)
                        crr = kk.ring(s2, "crr", 2, [SP_, 4, 128], F32); cri = kk.ring(s2, "cri", 2, [SP_, 4, 128], F32)
                        t4 = [kk.sb(s2, f"t4_{i}", [SP_, 64], F32) for i in range(4)]
                        t5 = kk.sb(s2, "t5", [SP_, 64, NSQ], F32)
                        S_ = lambda fn, extra=(): OP("dve", fn, writes=[bA], extra=extra)
                        for hf in range(2):
                            gs = [hf * 64 + q for q in range(64)]
                            lr = lamr[:, hf * 64:(hf + 1) * 64]; li = lami[:, hf * 64:(hf + 1) * 64]
                            for q, g in enumerate(gs):
                                if q % 4 == 0:
                                    k, mb = mring.get()
                                    DMA("sp", mb.ap[:, :, :], BCd[g:g + 4].rearrange("g p c -> p g c"), f"mr{k}", reads=[bBC[g + i_] for i_ in range(4)], writes=[mb])
                                pr = pget(); pi_ = pget()
                                kk.mm(pr, pr.ap[0:SP_, :NK], [(mb.ap[:, q % 4, 0:64], U[:, g, :])], reads=[mb, bU[g]])
                                kk.mm(pi_, pi_.ap[0:SP_, :NK], [(mb.ap[:, q % 4, 64:128], U[:, g, :])], reads=[mb, bU[g]])
                                OP("act", lambda e, pr=pr, q=q: e.activation(out=Ar[:, q, 1:145], in_=pr.ap[0:SP_, :NK], func=AF.Copy), reads=[pr], writes=[bA])
                                OP("act", lambda e, pi_=pi_, q=q: e.activation(out=Ai[:, q, 1:145], in_=pi_.ap[0:SP_, :NK], func=AF.Copy), reads=[pi_], writes=[bA])
                            if not final:
                                S_(lambda e: e.memset(Ar[:, :, 0], 0.0)); S_(lambda e: e.memset(Ai[:, :, 0], 0.0))
                            else:
                                fl = cst[0:SP_, C_FLAG:C_FLAG + 1]
                                S_(lambda e, hf=hf, fl=fl: e.tensor_scalar(out=Ar[:, :, 0], in0=exg[0:SP_, 80 + hf * 64:80 + (hf + 1) * 64], scalar1=fl, scalar2=None, op0=ALU.mult), extra=[bexg.w])
                                S_(lambda e, hf=hf, fl=fl: e.tensor_scalar(out=Ai[:, :, 0], in0=exg2[:, hf * 64:(hf + 1) * 64], scalar1=fl, scalar2=None, op0=ALU.mult), extra=[bexg.w])
                            for k_ in range(128):
                                xr_ = Ar[:, :, k_]; xi_ = Ai[:, :, k_]; nr_ = Ar[:, :, k_ + 1]; ni_ = Ai[:, :, k_ + 1]
                                S_(lambda e, xr_=xr_, lr=lr: e.tensor_tensor(out=t4[0][:, :], in0=lr, in1=xr_, op=ALU.mult))
                                S_(lambda e, xi_=xi_, li=li: e.tensor_tensor(out=t4[1][:, :], in0=li, in1=xi_, op=ALU.mult))
                                S_(lambda e, xi_=xi_, lr=lr: e.tensor_tensor(out=t4[2][:, :], in0=lr, in1=xi_, op=ALU.mult))
                                S_(lambda e, xr_=xr_, li=li: e.tensor_tensor(out=t4[3][:, :], in0=li, in1=xr_, op=ALU.mult))
                                S_(lambda e, nr_=nr_: e.tensor_tensor(out=nr_, in0=nr_, in1=t4[0][:, :], op=ALU.add))
                                S_(lambda e, nr_=nr_: e.tensor_tensor(out=nr_, in0=nr_, in1=t4[1][:, :], op=ALU.subtract))
                                S_(lambda e, ni_=ni_: e.tensor_tensor(out=ni_, in0=ni_, in1=t4[2][:, :], op=ALU.add))
                                S_(lambda e, ni_=ni_: e.tensor_tensor(out=ni_, in0=ni_, in1=t4[3][:, :], op=ALU.add))
                            if not final:
                                S_(lambda e: e.tensor_copy(out=t4[0][:, :], in_=Ar[:, :, 128])); S_(lambda e: e.tensor_copy(out=t4[1][:, :], in_=Ai[:, :, 128]))
                                DMA("sp", EXI[0:SP_, 80 + hf * 64:80 + (hf + 1) * 64], t4[0][:, :], "ex2", reads=[bA], writes=[bEXI])
                                DMA("sp", EXI[SP_:128, 80 + hf * 64:80 + (hf + 1) * 64], t4[1][:, :], "ex2", reads=[bA], writes=[bEXI])
                                bA.w = bEXI.w
                                continue
                            lrb = lr.unsqueeze(2).to_broadcast([SP_, 64, NSQ]); lib = li.unsqueeze(2).to_broadcast([SP_, 64, NSQ])
                            hr_ = hsr[:, hf * 64:(hf + 1) * 64, :]; hi_ = hsi[:, hf * 64:(hf + 1) * 64, :]
                            sr_ = Ar[:, :, 129:145]; si_ = Ai[:, :, 129:145]
                            for (a_, b_, dst_, op_) in ((lrb, hr_, sr_, ALU.add), (lib, hi_, sr_, ALU.subtract), (lrb, hi_, si_, ALU.add), (lib, hr_, si_, ALU.add)):
                                S_(lambda e, a_=a_, b_=b_: e.tensor_tensor(out=t5[:, :, :], in0=a_, in1=b_, op=ALU.mult), extra=[bhs.w])
                                S_(lambda e, dst_=dst_, op_=op_: e.tensor_tensor(out=dst_, in0=dst_, in1=t5[:, :, :], op=op_))
                            DMA("sp", ssmr_o[:, hf * 64:(hf + 1) * 64, :], Ar[:, :, 128:145], "so0", reads=[bA])
                            DMA("sp", ssmi_o[:, hf * 64:(hf + 1) * 64, :], Ai[:, :, 128:145], "so0", reads=[bA])
                            for q, g in enumerate(gs):
                                if q % 4 == 0:
                                    k, cr_b = crr.get(); k1, ci_b = cri.get(); k2, mb = mring.get()
                                    DMA("sp", cr_b.ap[:, :, :], CCRd[:, g:g + 4, :], f"crr{k}", reads=[bCC[g // 16]], writes=[cr_b])
                                    DMA("sp", ci_b.ap[:, :, :], CCId[:, g:g + 4, :], f"cri{k1}", reads=[bCC[g // 16]], writes=[ci_b])
                                    DMA("sp", mb.ap[:, :, :], DCd[g:g + 4].rearrange("g p c -> p g c"), f"mr{k2}", reads=[bDC[g + i_] for i_ in range(4)], writes=[mb])
                                qq = q % 4
                                pb = pget()
                                kk.mm(pb, pb.ap[:, 0:128], [(mb.ap[:, qq, :], U[:, g, 0:128])], reads=[mb, bU[g]])
                                kk.mm(pb, pb.ap[:, 0:128], [(cr_b.ap[:, qq, :], Ar[:, q, 0:128]), (ci_b.ap[:, qq, :], Ai[:, q, 0:128])], reads=[cr_b, ci_b, bA], start=False, sgc=True)
                                pb2 = pget()
                                kk.mm(pb2, pb2.ap[:, 0:NSQ], [(mb.ap[:, qq, :], U[:, g, 128:144])], reads=[mb, bU[g]])
                                kk.mm(pb2, pb2.ap[:, 0:NSQ], [(cr_b.ap[:, qq, :], hsr[:, g, :]), (ci_b.ap[:, qq, :], hsi[:, g, :])], reads=[cr_b, ci_b, bhs], start=False, sgc=True)
                                OP("act", lambda e, pb=pb, g=g: e.activation(out=U[:, g, 0:128], in_=pb.ap[:, 0:128], func=AF.Copy), reads=[pb], writes=[bU[g]])
                                OP("act", lambda e, pb2=pb2, g=g: e.activation(out=U[:, g, 128:144], in_=pb2.ap[:, 0:NSQ], func=AF.Copy), reads=[pb2], writes=[bU[g]])
                        P.barrier()

                lru_round(False)
                s5_pass(False)
                def cc(e):
                    return e.collective_compute("AllGather", ALU.bypass, replica_groups=[[2 * i, 2 * i + 1] for i in range(NCORES // 2)], ins=[EXI], outs=[EXO])
                OP("pool", cc, reads=[bEXI], writes=[bEXO])
                DMA("sp", exg[:, :], EXO[0:128, :], "exg", reads=[bEXO], writes=[bexg])
                DMA("sp", exg2[:, :], EXO[SP_:128, 80:208], "exg", reads=[bEXO], writes=[bexg])
                lru_round(True)
                s5_pass(True)
                with ExitStack() as s2:
                    selT = kk.sb(s2, "selT", [128, 64, 128], BF16); bselT = Buf()
                    DMA("pool", selT[:, :, :].rearrange("p a b -> p (a b)"), selT_d, "selT", writes=[bselT], extra=P.bar)
                    gT = kk.sb(s2, "gT", [128, LT, NT], BF16); bgT = dbufs(LT)
                    for T in range(LT):
                        banks = [pget(), pget(), pget()]
                        for t_ in range(8):
                            pb = banks[t_ // 3]; off = (t_ % 3) * NK
                            kk.mm(pb, pb.ap[:, off:off + NK], [(selT[:, gl * 8 + t_, :], U[:, T * 8 + gl, :]) for gl in range(8)], reads=[bselT] + [bU[T * 8 + gl] for gl in range(8)])
                        gv = gT[:, T, :].rearrange("p (k t) -> p t k", t=8)
                        for bi in range(3):
                            nt_ = 3 if bi < 2 else 2
                            pb = banks[bi]
                            OP("act", lambda e, pb=pb, bi=bi, nt_=nt_, gv=gv: e.activation(out=gv[:, bi * 3:bi * 3 + nt_, :], in_=pb.ap[:, 0:nt_ * NK].rearrange("p (t k) -> p t k", k=NK), func=AF.Gelu),
                               reads=[pb], writes=[bgT[T]])
                    sgr = kk.ring(s2, "sgr", 2, [128, 384], F32)
                    ybr = kk.ring(s2, "ybr", 2, [128, NT], BF16)
                    for o_ in range(LT):
                        wb = wload(wglu[o_], 2048)
                        k, yb_ = ybr.get()
                        for (o, m) in THIRDS:
                            pb = pget()
                            kk.mm(pb, pb.ap[:, :m], [(wb.ap[:, q * 128:(q + 1) * 128], gT[:, q, o:o + m]) for q in range(LT)], reads=[wb] + bgT)
                            k2, sb_ = sgr.get()
                            OP("act", lambda e, pb=pb, sb_=sb_, m=m, o_=o_: e.activation(out=sb_.ap[:, :m], in_=pb.ap[:, :m], func=AF.Sigmoid, bias=cst[:, C_BGLU + o_:C_BGLU + o_ + 1]), reads=[pb], writes=[sb_])
                            OP("dve", lambda e, sb_=sb_, yb_=yb_, o=o, m=m, o_=o_: e.tensor_tensor(out=yb_.ap[:, o:o + m], in0=sb_.ap[:, :m], in1=gT[:, o_, o:o + m], op=ALU.mult), reads=[sb_, bgT[o_]], writes=[yb_])
                        DMA("sp", YB[o_], yb_.ap[:, :], f"ybr{k}", reads=[yb_], writes=[bYB[o_]])
                    P.barrier()

        def stage3(t0):
            with ExitStack() as st:
                n = TB
                uT = kk.sb(st, "uT", [128, DTL, TB], BF16); buT = dbufs(DTL)
                mT = kk.sb(st, "mT", [128, DTL, TB], BF16); bmT = dbufs(DTL)
                yaT = kk.sb(st, "yaT", [128, LT, TB], BF16); ybT = kk.sb(st, "ybT", [128, LT, TB], BF16); bya = Buf(); byb = Buf()
                rstd = kk.sb(st, "rstd", [128, TB], F32); brstd = Buf()
                xring = kk.ring(st, "xr", 3, [128, TB], F32)
                fring = kk.ring(st, "fr", 2, [128, TB], F32)
                sqring = kk.ring(st, "sq", 2, [128, TB], BF16)
                sgr = kk.ring(st, "sg3", 4, [128, 288], F32)
                for T in range(LT):
                    DMA("sp", yaT[:, T, :], YA[T][:, t0:t0 + n], "ya3", reads=[bYA[T]], writes=[bya])
                    DMA("sp", ybT[:, T, :], YB[T][:, t0:t0 + n], "yb3", reads=[bYB[T]], writes=[byb])
                apply_norm(X1T, bX1, t0, n, C_GMPRE, rstd1g[:, t0:t0 + n], brstd1g, uT, buT, xring)
                for j in range(DTL):
                    wga = wload(win[48 + j], 4096); wgb = wload(win[80 + j], 4096); wa = wload(woa[j], 2048); wb_ = wload(wob[j], 2048)
                    for (o, m) in HALVES:
                        pga = pget(); pgb = pget(); ppa = pget(); ppb = pget()
                        kk.mm(pga, pga.ap[:, :m], [(wga.ap[:, q * 128:(q + 1) * 128], uT[:, q, o:o + m]) for q in range(DTL)], reads=[wga] + buT)
                        kk.mm(pgb, pgb.ap[:, :m], [(wgb.ap[:, q * 128:(q + 1) * 128], uT[:, q, o:o + m]) for q in range(DTL)], reads=[wgb] + buT)
                        kk.mm(ppa, ppa.ap[:, :m], [(wa.ap[:, q * 128:(q + 1) * 128], yaT[:, q, o:o + m]) for q in range(LT)], reads=[wa, bya])
                        kk.mm(ppb, ppb.ap[:, :m], [(wb_.ap[:, q * 128:(q + 1) * 128], ybT[:, q, o:o + m]) for q in range(LT)], reads=[wb_, byb])
                        k1, s1 = sgr.get(); k2, s2_ = sgr.get()
                        OP("act", lambda e, pga=pga, s1=s1, m=m: e.activation(out=s1.ap[:, :m], in_=pga.ap[:, :m], func=AF.Sigmoid), reads=[pga], writes=[s1])
                        OP("act", lambda e, pgb=pgb, s2_=s2_, m=m: e.activation(out=s2_.ap[:, :m], in_=pgb.ap[:, :m], func=AF.Sigmoid), reads=[pgb], writes=[s2_])
                        OP("dve", lambda e, s1=s1, ppa=ppa, m=m: e.tensor_tensor(out=s1.ap[:, :m], in0=s1.ap[:, :m], in1=ppa.ap[:, :m], op=ALU.mult), reads=[ppa], writes=[s1])
                        OP("dve", lambda e, s2_=s2_, ppb=ppb, m=m: e.tensor_tensor(out=s2_.ap[:, :m], in0=s2_.ap[:, :m], in1=ppb.ap[:, :m], op=ALU.mult), reads=[ppb], writes=[s2_])
                        OP("dve", lambda e, s1=s1, s2_=s2_, j=j, o=o, m=m: e.tensor_tensor(out=mT[:, j, o:o + m], in0=s1.ap[:, :m], in1=s2_.ap[:, :m], op=ALU.add), reads=[s1, s2_], writes=[bmT[j]])

                def wsrc(j):
                    b = wload(wo[j], 4096)
                    return [(b, 0, 32)]
                proj_to_fsc(st, DTL, wsrc, mT, bmT, n, HALVES, sqring, fring)
                finish_rstd(HALVES, rstd, brstd, n)
                resid(X1T, bX1, t0, n, cst[:, C_GMPOST:C_GMPOST + 32], rstd, brstd, X2T, bX2, xring, fring)
                P.barrier()

        def dbg_dump():
            P.barrier()
            taps = dict(BCd=BCd, DCd=DCd, CCRd=CCRd, CCId=CCId, X1T=X1T, XA=XA, GA=GA, XB=XB, YA=YA, YB=YB, X2T=X2T, EXO=EXO)
            for nm in DBG_TAPS:
                src = taps[nm]
                o = nc.dram_tensor("dbg_" + nm, list(src.shape), src.dtype, kind="ExternalOutput").ap()
                P.dma("sp", lambda e, o=o, src=src: e.dma_start(out=o, in_=src), "dbg")

        P.barrier()
        if UPTO >= 1:
            s5_setup()
        if UPTO >= 2:
            for blk in range(2):
                ffn_stage(xT, bXin, X1T, bX1, blk * TB, C_G1PRE, GP05_1, wg1, wu1, wd1, tail_fn=win_tail)
        if UPTO >= 3:
            stage2()
        if UPTO >= 4:
            for blk in range(2):
                stage3(blk * TB)
                ffn_stage(X2T, bX2, yT, bYout, blk * TB, C_G2PRE, GP05_2, wg2, wu2, wd2)
        if DEBUG:
            dbg_dump()
        P.barrier(engs=("sp",))
        block = es.enter_context(nc.Block())
        P.build(block)
    return nc


UPTO = 4
DEBUG = False
NCORES = 8
DBG_TAPS = []


def tile_w(W, K, N):
    return np.ascontiguousarray(W.reshape(K // 128, 128, N // 128, 128).transpose(2, 1, 0, 3)).reshape(N // 128, 128, K)


def vec_tiles(v, nt):
    return np.ascontiguousarray(np.asarray(v).reshape(nt, 128).T)


def make_shared(inp):
    f = lambda a: np.asarray(a, dtype=np.float32)
    sh = {}
    sh["wg1"] = tile_w(f(inp["ffn1_w_gate"])[0], D, DFF); sh["wu1"] = tile_w(f(inp["ffn1_w_up"])[0], D, DFF); sh["wd1"] = tile_w(f(inp["ffn1_w_down"])[0], DFF, D)
    sh["wg2"] = tile_w(f(inp["ffn2_w_gate"])[0], D, DFF); sh["wu2"] = tile_w(f(inp["ffn2_w_up"])[0], D, DFF); sh["wd2"] = tile_w(f(inp["ffn2_w_down"])[0], DFF, D)
    sh["win"] = tile_w(f(inp["w_in"])[0], D, 14336)
    sh["wrg"] = np.ascontiguousarray(f(inp["w_rg"])[0].transpose(1, 0, 2)).reshape(128, LT * 128)
    sh["wig"] = np.ascontiguousarray(f(inp["w_ig"])[0].transpose(1, 0, 2)).reshape(128, LT * 128)
    sh["wglu"] = tile_w(f(inp["w_glu"])[0], DL, DL); sh["woa"] = tile_w(f(inp["w_out_a"])[0], DL, D); sh["wob"] = tile_w(f(inp["w_out_b"])[0], DL, D)
    sh["wo"] = tile_w(f(inp["w_o"])[0], D, D)
    sh["are"] = np.ascontiguousarray(f(inp["ssm_a_re"])[0].T); sh["aim"] = np.ascontiguousarray(f(inp["ssm_a_im"])[0].T)
    sh["ldt"] = np.ascontiguousarray(np.broadcast_to(f(inp["ssm_log_dt"])[0][None, :], (SP_, G)))
    sh["bre"] = np.ascontiguousarray(f(inp["ssm_b_re"])[0].transpose(1, 0, 2)); sh["bim"] = np.ascontiguousarray(f(inp["ssm_b_im"])[0].transpose(1, 0, 2))
    sh["cre"] = np.ascontiguousarray(f(inp["ssm_c_re"])[0].transpose(2, 0, 1)); sh["cim"] = np.ascontiguousarray(f(inp["ssm_c_im"])[0].transpose(2, 0, 1))
    sel = np.zeros((128, 64, 128), np.float32)
    for gl in range(8):
        for s in range(8):
            for c in range(16):
                sel[gl * 16 + c, gl * 8 + s, s * 16 + c] = 1.0
    sh["sel"] = sel.reshape(128, 64 * 128)
    sh["selT"] = np.ascontiguousarray(sel.transpose(2, 1, 0)).reshape(128, 64 * 128)
    s_idx = np.arange(128) // 16
    sh["mask"] = (s_idx[:, None] <= s_idx[None, :]).astype(np.float32)
    sh["ident"] = np.eye(128, dtype=np.float32)
    d = f(inp["ssm_d"])[0].reshape(G, CG)
    sh["dcol"] = np.ascontiguousarray(np.tile(d.T, (8, 1)))
    cst = np.zeros((128, NCST), np.float32)
    for col, nm in ((C_G1PRE, "ffn1_pre_g"), (C_G1POST, "ffn1_post_g"), (C_GMPRE, "mix_pre_g"), (C_GMPOST, "mix_post_g"), (C_G2PRE, "ffn2_pre_g"), (C_G2POST, "ffn2_post_g")):
        cst[:, col:col + 32] = vec_tiles(f(inp[nm])[0], 32)
    cst[:, C_CONVW:C_CONVW + 64] = f(inp["conv_w"])[0].reshape(4, LT, 128).transpose(2, 1, 0).reshape(128, 64)
    for col, nm in ((C_CONVB, "conv_b"), (C_BRG, "b_rg"), (C_BIG, "b_ig"), (C_LAM, "lru_lambda"), (C_SSMD, "ssm_d"), (C_BGLU, "b_glu")):
        cst[:, col:col + 16] = vec_tiles(f(inp[nm])[0], 16)
    sh["cst"] = cst
    return sh


def make_core_inputs(inp, sh, c):
    f = lambda a: np.asarray(a, dtype=np.float32)
    s, hf = c // 2, c % 2
    xp = f(inp["x_prompt"])[s, hf * TP:(hf + 1) * TP, :]
    xs = f(inp["x_sample"])[NSQ * c:NSQ * (c + 1)].reshape(NSQ * TS, D)
    x = np.concatenate([xp, xs], axis=0)
    m = dict(sh)
    m["xT"] = np.ascontiguousarray(x.T).reshape(DTL, 128, NT)
    cst = sh["cst"].copy(); cst[:, C_FLAG] = float(hf); m["cst"] = cst
    sl = slice(NSQ * c, NSQ * (c + 1))
    m["h0s"] = np.ascontiguousarray(f(inp["state_lru_h"])[0, sl].reshape(NSQ, LT, 128).transpose(2, 1, 0))
    m["conv0s"] = np.ascontiguousarray(f(inp["state_conv"])[0, sl].reshape(NSQ, 3, LT, 128).transpose(3, 2, 0, 1))
    m["ssm0r"] = np.ascontiguousarray(f(inp["state_ssm_re"])[0, sl].transpose(2, 1, 0))
    m["ssm0i"] = np.ascontiguousarray(f(inp["state_ssm_im"])[0, sl].transpose(2, 1, 0))
    return m


_NC_CACHE = {}


def kernel(**inputs):
    sh = make_shared(inputs)
    in_maps = [make_core_inputs(inputs, sh, c) for c in range(8)]
    if "nc" not in _NC_CACHE:
        _NC_CACHE["nc"] = build_program()
    nc = _NC_CACHE["nc"]
    res = run_bass_kernel_spmd(nc, in_maps, core_ids=list(range(8)))
    R = res.results
    B_, S_ = 4, 2048
    yp = np.zeros((B_, S_, D), np.float32); ys = np.zeros((128, TS, D), np.float32)
    p_h = np.zeros((1, B_, DL), np.float32); p_c = np.zeros((1, B_, 3, DL), np.float32)
    p_re = np.zeros((1, B_, G, SP_), np.float32); p_im = np.zeros((1, B_, G, SP_), np.float32)
    s_h = np.zeros((1, 128, DL), np.float32); s_c = np.zeros((1, 128, 3, DL), np.float32)
    s_re = np.zeros((1, 128, G, SP_), np.float32); s_im = np.zeros((1, 128, G, SP_), np.float32)
    for c in range(8):
        s, hf = c // 2, c % 2
        y = R[c]["yT"].reshape(D, NT).T
        yp[s, hf * TP:(hf + 1) * TP] = y[:TP]
        ys[NSQ * c:NSQ * (c + 1)] = y[TP:].reshape(NSQ, TS, D)
        lh = R[c]["lruh"].transpose(2, 1, 0).reshape(17, DL)
        cv = R[c]["convo"].transpose(2, 3, 1, 0).reshape(17, 3, DL)
        sr = R[c]["ssmr"].transpose(2, 1, 0); si = R[c]["ssmi"].transpose(2, 1, 0)
        if hf == 1:
            p_h[0, s] = lh[0]; p_c[0, s] = cv[0]; p_re[0, s] = sr[0]; p_im[0, s] = si[0]
        s_h[0, NSQ * c:NSQ * (c + 1)] = lh[1:]; s_c[0, NSQ * c:NSQ * (c + 1)] = cv[1:]
        s_re[0, NSQ * c:NSQ * (c + 1)] = sr[1:]; s_im[0, NSQ * c:NSQ * (c + 1)] = si[1:]
    return (yp, ys, p_h, p_c, p_re, p_im, s_h, s_c, s_re, s_im)
```

```python
import math
import numpy as np
from contextlib import ExitStack
import concourse.bass as bass
import concourse.mybir as mybir
from concourse.bass_utils import run_bass_kernel_spmd

F32 = mybir.dt.float32
BF16 = mybir.dt.bfloat16
AF = mybir.ActivationFunctionType
ALU = mybir.AluOpType

D = 4096; DTL = 32; DFF = 11008; FFT = 86; DL = 2048; LT = 16
NT = 1152; TP = 1024; NSQ = 16; TS = 8; TB = 576; G = 128; SP_ = 64; CG = 16; NK = 144
EPS = 1e-6
WSLOT = 4096
ENGS = ("pe", "act", "dve", "pool", "sp")

C_G1PRE, C_G1POST, C_GMPRE, C_GMPOST, C_G2PRE, C_G2POST = 0, 32, 64, 96, 128, 160
C_CONVW = 192; C_CONVB = 256; C_BRG = 272; C_BIG = 288; C_LAM = 304; C_SSMD = 320; C_BGLU = 336; C_FLAG = 352
NCST = 353


class Tok:
    __slots__ = ("sem", "val", "eng")

    def __init__(self, sem, val, eng=None):
        self.sem = sem; self.val = val; self.eng = eng


class Buf:
    __slots__ = ("ap", "w", "r")

    def __init__(self, ap=None):
        self.ap = ap; self.w = None; self.r = []


class Ring:
    def __init__(self, aps):
        self.bufs = [Buf(a) for a in aps]; self.i = 0

    def get(self):
        b = self.bufs[self.i]; k = self.i
        self.i = (self.i + 1) % len(self.bufs)
        return k, b


class Prog:
    SEM_ROLL = 30000

    def __init__(self, nc, es):
        self.nc = nc; self.es = es
        self.streams = {e: [] for e in ENGS}
        self.cur_sem = {e: None for e in ENGS}
        self.cur_cnt = {e: 0 for e in ENGS}
        self.waited = {e: {} for e in ENGS}
        self.nsem = 0
        self.dma_sems = {}
        self.bar = []

    def new_sem(self, name):
        self.nsem += 1
        return self.es.enter_context(self.nc.semaphore(f"{name}_{self.nsem}"))

    def _waits(self, eng, deps):
        ws = []; w = self.waited[eng]
        for d in deps:
            if d is None:
                continue
            if isinstance(d, (list, tuple)):
                ws.extend(self._waits(eng, d)); continue
            if eng == "pe" and d.eng == "pe":
                continue
            k = id(d.sem)
            if w.get(k, -1) >= d.val:
                continue
            w[k] = d.val
            ws.append((d.sem, d.val))
        return ws

    def op(self, eng, fn, deps=(), signal=True):
        ws = self._waits(eng, deps)
        tok = None; inc = None
        if signal:
            if self.cur_sem[eng] is None or self.cur_cnt[eng] >= self.SEM_ROLL:
                self.cur_sem[eng] = self.new_sem(f"s_{eng}"); self.cur_cnt[eng] = 0
            self.cur_cnt[eng] += 1
            tok = Tok(self.cur_sem[eng], self.cur_cnt[eng], eng); inc = (self.cur_sem[eng], 1)
        self.streams[eng].append((fn, ws, inc))
        return tok

    def dma(self, eng, fn, key, deps=()):
        ws = self._waits(eng, deps)
        if key not in self.dma_sems:
            self.dma_sems[key] = [self.new_sem("d"), 0]
        ent = self.dma_sems[key]; ent[1] += 16
        self.streams[eng].append((fn, ws, (ent[0], 16)))
        return Tok(ent[0], ent[1])

    def wait_only(self, eng, deps):
        ws = self._waits(eng, deps)
        if ws:
            self.streams[eng].append((None, ws, None))

    def barrier(self, engs=("pe", "act", "dve", "sp")):
        toks = []
        for e in ENGS:
            if self.cur_sem[e] is not None:
                toks.append(Tok(self.cur_sem[e], self.cur_cnt[e]))
        for k, (s, v) in self.dma_sems.items():
            toks.append(Tok(s, v))
        for e in engs:
            self.wait_only(e, toks)
        self.bar = toks
        return toks

    def build(self, block):
        def run(e, items):
            for fn, ws, inc in items:
                for (s, v) in ws:
                    e.wait_ge(s, v)
                if fn is None:
                    continue
                ins = fn(e)
                if inc is not None:
                    ins.then_inc(inc[0], inc[1])

        @block.sync
        def _(e):
            run(e, self.streams["sp"])

        @block.scalar
        def _(e):
            run(e, self.streams["act"])

        @block.vector
        def _(e):
            run(e, self.streams["dve"])

        @block.gpsimd
        def _(e):
            run(e, self.streams["pool"])

        @block.tensor
        def _(e):
            run(e, self.streams["pe"])


class K:
    def __init__(self, nc, es):
        self.nc = nc; self.es = es; self.P = Prog(nc, es)
        self.nkey = 0
        self.lastdma = {}

    def OP(self, eng, fn, reads=(), writes=(), extra=(), signal=True):
        deps = list(extra)
        for b in reads:
            deps.append(b.w)
        for b in writes:
            deps.append(b.w); deps.extend(b.r)
        t = self.P.op(eng, fn, deps=deps, signal=signal)
        if t is not None:
            for b in reads:
                b.r.append(t)
            for b in writes:
                b.w = t; b.r = []
        return t

    def DMA(self, q, out, in_, key, reads=(), writes=(), extra=()):
        deps = list(extra)
        for b in reads:
            deps.append(b.w)
        for b in writes:
            deps.append(b.w); deps.extend(b.r)
        deps.append(self.lastdma.get(key))
        t = self.P.dma(q, lambda e: e.dma_start(out=out, in_=in_), key, deps=deps)
        self.lastdma[key] = t
        for b in reads:
            b.r.append(t)
        for b in writes:
            b.w = t; b.r = []
        return t

    def sb(self, st, name, shape, dt):
        self.nkey += 1
        return st.enter_context(self.nc.sbuf_tensor(f"{name}_s{self.nkey}", shape, dt))

    def ring(self, st, name, n, shape, dt):
        t = self.sb(st, name, [shape[0], n] + list(shape[1:]), dt)
        r = Ring([t[:, i] for i in range(n)])
        r.name = name
        return r

    def mm(self, out_buf, out_ap, pairs, reads=(), start=True, sgc=False, stop=True):
        n = len(pairs)
        deps = [out_buf.w] + list(out_buf.r)
        for b in reads:
            deps.append(b.w)
        tok = None
        for q, (l, r) in enumerate(pairs):
            tok = self.P.op("pe", (lambda e, l=l, r=r, q=q: e.matmul(out_ap, lhsT=l, rhs=r, start=(start and q == 0), stop=(stop and q == n - 1), skip_group_check=sgc)),
                            deps=deps if q == 0 else (), signal=(q == n - 1))
        for b in reads:
            b.r.append(tok)
        out_buf.w = tok; out_buf.r = []
        return tok


def chunks(n, m):
    k = (n + m - 1) // m
    base = n // k
    assert base * k == n
    return [(i * base, base) for i in range(k)]


def build_program():
    nc = bass.Bass("TRN2", target_bir_lowering=False)
    es = ExitStack()
    with es:
        kk = K(nc, es); P = kk.P; OP = kk.OP; DMA = kk.DMA

        def din(name, shape, dt=F32):
            return nc.dram_tensor(name, list(shape), dt, kind="ExternalInput").ap()

        def dout(name, shape, dt=F32):
            return nc.dram_tensor(name, list(shape), dt, kind="ExternalOutput").ap()

        def dscr(name, shape, dt=F32):
            return nc.dram_tensor(name, list(shape), dt, kind="Internal").ap()

        xT = din("xT", [DTL, 128, NT]); cst_d = din("cst", [128, NCST])
        wg1 = din("wg1", [FFT, 128, 4096]); wu1 = din("wu1", [FFT, 128, 4096]); wd1 = din("wd1", [DTL, 128, DFF])
        wg2 = din("wg2", [FFT, 128, 4096]); wu2 = din("wu2", [FFT, 128, 4096]); wd2 = din("wd2", [DTL, 128, DFF])
        win = din("win", [112, 128, 4096]); wrg_d = din("wrg", [128, LT * 128]); wig_d = din("wig", [128, LT * 128])
        wglu = din("wglu", [LT, 128, 2048]); woa = din("woa", [DTL, 128, 2048]); wob = din("wob", [DTL, 128, 2048])
        wo = din("wo", [DTL, 128, 4096])
        h0s_d = din("h0s", [128, LT, NSQ]); conv0s_d = din("conv0s", [128, LT, NSQ, 3])
        ssm0r_d = din("ssm0r", [SP_, G, NSQ]); ssm0i_d = din("ssm0i", [SP_, G, NSQ])
        are_d = din("are", [SP_, G]); aim_d = din("aim", [SP_, G]); ldt_d = din("ldt", [SP_, G])
        bre_d = din("bre", [SP_, G, CG]); bim_d = din("bim", [SP_, G, CG]); cre_d = din("cre", [SP_, G, CG]); cim_d = din("cim", [SP_, G, CG])
        sel_d = din("sel", [128, 64 * 128]); selT_d = din("selT", [128, 64 * 128])
        mask_d = din("mask", [128, 128]); ident_d = din("ident", [128, 128]); dcol_d = din("dcol", [128, G])

        yT = dout("yT", [DTL, 128, NT]); lruh_o = dout("lruh", [128, LT, 17]); convo_o = dout("convo", [128, LT, 17, 3])
        ssmr_o = dout("ssmr", [SP_, G, 17]); ssmi_o = dout("ssmi", [SP_, G, 17])

        X1T = dscr("X1T", [DTL, 128, NT]); X2T = dscr("X2T", [DTL, 128, NT]); FSC = dscr("FSC", [DTL, 128, TB])
        XA = dscr("XA", [LT, 128, NT]); GA = dscr("GA", [LT, 128, NT]); XB = dscr("XB", [LT, 128, NT], BF16)
        YA = dscr("YA", [LT, 128, NT], BF16); YB = dscr("YB", [LT, 128, NT], BF16)
        BCd = dscr("BCd", [G, 128, 128], BF16); DCd = dscr("DCd", [G, 128, 128], BF16)
        CCRd = dscr("CCRd", [SP_, G, 128]); CCId = dscr("CCId", [SP_, G, 128])
        EXI = dscr("EXI", [128, 208]); EXO = dscr("EXO", [256, 208])
        def dbufs(n):
            return [Buf() for _ in range(n)]
        bX1 = dbufs(DTL); bX2 = dbufs(DTL); bFS = dbufs(DTL); bXA = dbufs(LT); bGA = dbufs(LT); bXB = dbufs(LT)
        bYA = dbufs(LT); bYB = dbufs(LT); bBC = dbufs(G); bDC = dbufs(G); bCC = dbufs(8); bEXI = Buf(); bEXO = Buf()
        bXin = dbufs(DTL); bYout = dbufs(DTL)

        cst = kk.sb(es, "cst", [128, NCST], F32); bcst = Buf()
        drv = kk.sb(es, "drv", [128, 3 * 32 + 16], F32); bdrv = Buf()
        ones = kk.sb(es, "ones", [128, 128], BF16); bones = Buf()
        rstd1g = kk.sb(es, "rstd1g", [128, NT], F32); brstd1g = Buf()
        NWS = 5
        wring = kk.ring(es, "wr", NWS, [128, WSLOT], BF16)
        psum = es.enter_context(nc.psum_tensor("ps", [128, 8, 512], F32))
        pbank = [Buf(psum[:, i, :]) for i in range(8)]
        prr = [0]

        def pget():
            b = pbank[prr[0]]; prr[0] = (prr[0] + 1) % 6
            return b

        def wload(src, n):
            k, b = wring.get()
            DMA("pool", b.ap[:, :n], src, f"w{k}", writes=[b])
            return b

        DMA("sp", cst[:, :], cst_d, "c0", writes=[bcst])
        OP("dve", lambda e: e.memset(ones[:, :], 1.0), writes=[bones])
        OP("dve", lambda e: e.tensor_scalar(out=drv[:, 0:32], in0=cst[:, C_G1POST:C_G1POST + 32], scalar1=0.5, scalar2=None, op0=ALU.mult), reads=[bcst], writes=[bdrv])
        OP("dve", lambda e: e.tensor_scalar(out=drv[:, 32:64], in0=cst[:, C_G2POST:C_G2POST + 32], scalar1=0.5, scalar2=None, op0=ALU.mult), reads=[bcst], writes=[bdrv])
        OP("act", lambda e: e.activation(out=drv[:, 96:112], in_=cst[:, C_LAM:C_LAM + 16], func=AF.Exp, scale=-1.0), reads=[bcst], writes=[bdrv])
        OP("act", lambda e: e.activation(out=drv[:, 96:112], in_=drv[:, 96:112], func=AF.Ln, bias=1.0), writes=[bdrv])
        OP("dve", lambda e: e.tensor_scalar(out=drv[:, 96:112], in0=drv[:, 96:112], scalar1=-8.0, scalar2=None, op0=ALU.mult), writes=[bdrv])
        GP05_1 = drv[:, 0:32]; GP05_2 = drv[:, 32:64]; C8 = drv[:, 96:112]

        HALVES = chunks(TB, 512)
        THIRDS = chunks(NT, 512)

        def norm_rstd(st, src, bsrc, t0, n, cks, rstd, brstd, xring, sqring):
            ssb = [pbank[6], pbank[7], pbank[5]][:len(cks)]
            for i in range(DTL):
                k, xb_ = xring.get()
                DMA("sp", xb_.ap[:, :n], src[i][:, t0:t0 + n], f"{xring.name}{k}", reads=[bsrc[i]], writes=[xb_])
                k2, sq = sqring.get()
                OP("act", lambda e, xb_=xb_, sq=sq: e.activation(out=sq.ap[:, :n], in_=xb_.ap[:, :n], func=AF.Square), reads=[xb_], writes=[sq])
                for ci, (o, m) in enumerate(cks):
                    deps = [sq.w]
                    if i == 0:
                        deps += [ssb[ci].w] + ssb[ci].r
                    t = P.op("pe", lambda e, ci=ci, o=o, m=m, sq=sq, i=i: e.matmul(ssb[ci].ap[:, :m], lhsT=ones[:, :], rhs=sq.ap[:, o:o + m], start=(i == 0), stop=(i == DTL - 1)), deps=deps + [bones.w])
                    sq.r.append(t)
                    if i == DTL - 1:
                        ssb[ci].w = t; ssb[ci].r = []
            for ci, (o, m) in enumerate(cks):
                OP("act", lambda e, ci=ci, o=o, m=m: e.activation(out=rstd[:, o:o + m], in_=ssb[ci].ap[:, :m], func=AF.Sqrt, scale=1.0 / D, bias=EPS), reads=[ssb[ci]], writes=[brstd])
            OP("dve", lambda e: e.reciprocal(out=rstd[:, :n], in_=rstd[:, :n]), writes=[brstd])

        def apply_norm(src, bsrc, t0, n, gcol, rstd, brstd, dstT, bdst, xring):
            for i in range(DTL):
                k, xb_ = xring.get()
                DMA("sp", xb_.ap[:, :n], src[i][:, t0:t0 + n], f"{xring.name}{k}", reads=[bsrc[i]], writes=[xb_])
                OP("dve", lambda e, xb_=xb_, i=i: e.scalar_tensor_tensor(out=dstT[:, i, :n], in0=xb_.ap[:, :n], scalar=cst[:, gcol + i:gcol + i + 1], in1=rstd[:, :n], op0=ALU.mult, op1=ALU.mult),
                   reads=[xb_, brstd, bcst], writes=[bdst[i]])

        def resid(srcx, bsrcx, t0, n, gsc, rstd, brstd, dst, bdst, xring, fring, keep=None, cks=None):
            ssb = [pbank[6], pbank[7]]
            for j in range(DTL):
                k, fb = fring.get()
                DMA("sp", fb.ap[:, :n], FSC[j][:, :n], f"{fring.name}{k}", reads=[bFS[j]], writes=[fb])
                k2, xb_ = xring.get()
                DMA("sp", xb_.ap[:, :n], srcx[j][:, t0:t0 + n], f"{xring.name}{k2}", reads=[bsrcx[j]], writes=[xb_])
                OP("dve", lambda e, fb=fb, j=j: e.scalar_tensor_tensor(out=fb.ap[:, :n], in0=fb.ap[:, :n], scalar=gsc[:, j:j + 1], in1=rstd[:, :n], op0=ALU.mult, op1=ALU.mult),
                   reads=[brstd, bdrv, bcst], writes=[fb])
                if keep is None:
                    OP("dve", lambda e, fb=fb, xb_=xb_: e.tensor_tensor(out=xb_.ap[:, :n], in0=fb.ap[:, :n], in1=xb_.ap[:, :n], op=ALU.add), reads=[fb], writes=[xb_])
                    DMA("sp", dst[j][:, t0:t0 + n], xb_.ap[:, :n], f"{xring.name}s{k2}", reads=[xb_], writes=[bdst[j]])
                else:
                    x1buf, bx1, sqring = keep
                    OP("dve", lambda e, fb=fb, xb_=xb_, j=j: e.tensor_tensor(out=x1buf[:, j, :n], in0=fb.ap[:, :n], in1=xb_.ap[:, :n], op=ALU.add), reads=[fb, xb_], writes=[bx1[j]])
                    DMA("sp", dst[j][:, t0:t0 + n], x1buf[:, j, :n], f"x1s{j % 4}", reads=[bx1[j]], writes=[bdst[j]])
                    k3, sq = sqring.get()
                    OP("act", lambda e, sq=sq, j=j: e.activation(out=sq.ap[:, :n], in_=x1buf[:, j, :n], func=AF.Square), reads=[bx1[j]], writes=[sq])
                    for ci, (o, m) in enumerate(cks):
                        deps = [sq.w, bones.w]
                        if j == 0:
                            deps += [ssb[ci].w] + ssb[ci].r
                        t = P.op("pe", lambda e, ci=ci, o=o, m=m, sq=sq, j=j: e.matmul(ssb[ci].ap[:, :m], lhsT=ones[:, :], rhs=sq.ap[:, o:o + m], start=(j == 0), stop=(j == DTL - 1)), deps=deps)
                        sq.r.append(t)
                        if j == DTL - 1:
                            ssb[ci].w = t; ssb[ci].r = []

        def proj_to_fsc(st, nk, wsrc_fn, rhsT, brhs, n, cks, sqring, fring):
            ssb = [pbank[6], pbank[7]]
            for j in range(DTL):
                pieces = wsrc_fn(j)
                k, fb = fring.get()
                pbs = [pget() for _ in cks]
                for pi, (wb, kc0, nkc) in enumerate(pieces):
                    for ci, (o, m) in enumerate(cks):
                        pb = pbs[ci]
                        kk.mm(pb, pb.ap[:, :m], [(wb.ap[:, q * 128:(q + 1) * 128], rhsT[:, kc0 + q, o:o + m]) for q in range(nkc)],
                              reads=[wb] + (list(brhs) if pi == 0 else []), start=(pi == 0), stop=(pi == len(pieces) - 1), sgc=True)
                for ci, (o, m) in enumerate(cks):
                    pb = pbs[ci]
                    OP("act", lambda e, pb=pb, fb=fb, o=o, m=m: e.activation(out=fb.ap[:, o:o + m], in_=pb.ap[:, :m], func=AF.Copy), reads=[pb], writes=[fb])
                k2, sq = sqring.get()
                OP("act", lambda e, fb=fb, sq=sq: e.activation(out=sq.ap[:, :n], in_=fb.ap[:, :n], func=AF.Square), reads=[fb], writes=[sq])
                for ci, (o, m) in enumerate(cks):
                    deps = [sq.w, bones.w]
                    if j == 0:
                        deps += [ssb[ci].w] + ssb[ci].r
                    t = P.op("pe", lambda e, ci=ci, o=o, m=m, sq=sq, j=j: e.matmul(ssb[ci].ap[:, :m], lhsT=ones[:, :], rhs=sq.ap[:, o:o + m], start=(j == 0), stop=(j == DTL - 1)), deps=deps)
                    sq.r.append(t)
                    if j == DTL - 1:
                        ssb[ci].w = t; ssb[ci].r = []
                DMA("sp", FSC[j][:, :n], fb.ap[:, :n], f"{fring.name}s{k}", reads=[fb], writes=[bFS[j]])

        def finish_rstd(cks, rstd, brstd, n):
            ssb = [pbank[6], pbank[7]]
            for ci, (o, m) in enumerate(cks):
                OP("act", lambda e, ci=ci, o=o, m=m: e.activation(out=rstd[:, o:o + m], in_=ssb[ci].ap[:, :m], func=AF.Sqrt, scale=1.0 / D, bias=EPS), reads=[ssb[ci]], writes=[brstd])
            OP("dve", lambda e: e.reciprocal(out=rstd[:, :n], in_=rstd[:, :n]), writes=[brstd])

        def ffn_stage(src, bsrc, dst, bdst, t0, gprecol, gp05, Wg, Wu, Wd, tail_fn=None):
            with ExitStack() as st:
                n = TB
                hT = kk.sb(st, "hT", [128, DTL, TB], BF16); bhT = dbufs(DTL)
                hid = kk.sb(st, "hid", [128, max(FFT, 64), TB], BF16); bhid = dbufs(FFT)
                rstd = kk.sb(st, "rstd", [128, TB], F32); brstd = Buf()
                xring = kk.ring(st, "xr", 3, [128, TB], F32)
                fring = kk.ring(st, "fr", 2, [128, TB], F32)
                sqring = kk.ring(st, "sq", 2, [128, TB], BF16)
                sgring = kk.ring(st, "sg", 2, [128, 288], F32)
                norm_rstd(st, src, bsrc, t0, n, HALVES, rstd, brstd, xring, sqring)
                apply_norm(src, bsrc, t0, n, gprecol, rstd, brstd, hT, bhT, xring)
                for f in range(FFT):
                    sg = wload(Wg[f], 4096); su = wload(Wu[f], 4096)
                    for (o, m) in HALVES:
                        pg = pget(); pu = pget()
                        kk.mm(pg, pg.ap[:, :m], [(sg.ap[:, q * 128:(q + 1) * 128], hT[:, q, o:o + m]) for q in range(DTL)], reads=[sg] + bhT)
                        kk.mm(pu, pu.ap[:, :m], [(su.ap[:, q * 128:(q + 1) * 128], hT[:, q, o:o + m]) for q in range(DTL)], reads=[su] + bhT)
                        k, sb_ = sgring.get()
                        OP("act", lambda e, pg=pg, sb_=sb_, m=m: e.activation(out=sb_.ap[:, :m], in_=pg.ap[:, :m], func=AF.Silu), reads=[pg], writes=[sb_])
                        OP("dve", lambda e, pu=pu, sb_=sb_, f=f, o=o, m=m: e.tensor_tensor(out=hid[:, f, o:o + m], in0=sb_.ap[:, :m], in1=pu.ap[:, :m], op=ALU.mult),
                           reads=[sb_, pu], writes=[bhid[f]])
                def wsrc(j):
                    out = []
                    for (kc0, nkc) in [(k0, min(32, FFT - k0)) for k0 in range(0, FFT, 32)]:
                        b = wload(Wd[j][:, kc0 * 128:(kc0 + nkc) * 128], nkc * 128)
                        out.append((b, kc0, nkc))
                    return out
                proj_to_fsc(st, FFT, wsrc, hid, bhid, n, HALVES, sqring, fring)
                finish_rstd(HALVES, rstd, brstd, n)
                if tail_fn is None:
                    resid(src, bsrc, t0, n, gp05, rstd, brstd, dst, bdst, xring, fring)
                else:
                    x1buf = hid[:, :, :].rearrange("p f n -> p (f n)").bitcast(F32)[:, 0:DTL * TB].rearrange("p (i n) -> p i n", n=TB)
                    bx1 = dbufs(DTL)
                    resid(src, bsrc, t0, n, gp05, rstd, brstd, dst, bdst, xring, fring, keep=(x1buf, bx1, sqring), cks=HALVES)
                    tail_fn(st, t0, hT, bhT, rstd, brstd, xring, fring, sqring, x1buf, bx1)
                P.barrier()

        def win_tail(st, t0, uT, buT, rstd, brstd, xring, fring, sqring, x1buf, bx1):
            n = TB
            finish_rstd(HALVES, rstd, brstd, n)
            OP("dve", lambda e: e.tensor_copy(out=rstd1g[:, t0:t0 + n], in_=rstd[:, :n]), reads=[brstd], writes=[brstd1g])
            for i in range(DTL):
                OP("dve", lambda e, i=i: e.scalar_tensor_tensor(out=uT[:, i, :n], in0=x1buf[:, i, :n], scalar=cst[:, C_GMPRE + i:C_GMPRE + i + 1], in1=rstd[:, :n], op0=ALU.mult, op1=ALU.mult),
                   reads=[bx1[i], brstd, bcst], writes=[buT[i]])
            for o_ in range(48):
                wb = wload(win[o_], 4096)
                kind = o_ // 16; ti = o_ % 16
                k, fb = fring.get()
                for (o, m) in HALVES:
                    pb = pget()
                    kk.mm(pb, pb.ap[:, :m], [(wb.ap[:, q * 128:(q + 1) * 128], uT[:, q, o:o + m]) for q in range(DTL)], reads=[wb] + buT)
                    if kind == 1:
                        OP("act", lambda e, pb=pb, fb=fb, o=o, m=m: e.activation(out=fb.ap[:, o:o + m], in_=pb.ap[:, :m], func=AF.Gelu), reads=[pb], writes=[fb])
                    elif kind == 0:
                        OP("act", lambda e, pb=pb, fb=fb, o=o, m=m: e.activation(out=fb.ap[:, o:o + m], in_=pb.ap[:, :m], func=AF.Copy), reads=[pb], writes=[fb])
                    else:
                        fbb = fb.ap.bitcast(BF16)
                        OP("act", lambda e, pb=pb, fbb=fbb, o=o, m=m: e.activation(out=fbb[:, o:o + m], in_=pb.ap[:, :m], func=AF.Copy), reads=[pb], writes=[fb])
                if kind == 0:
                    DMA("sp", XA[ti][:, t0:t0 + n], fb.ap[:, :n], f"{fring.name}s{k}", reads=[fb], writes=[bXA[ti]])
                elif kind == 1:
                    DMA("sp", GA[ti][:, t0:t0 + n], fb.ap[:, :n], f"{fring.name}s{k}", reads=[fb], writes=[bGA[ti]])
                else:
                    DMA("sp", XB[ti][:, t0:t0 + n], fb.ap.bitcast(BF16)[:, :n], f"{fring.name}s{k}", reads=[fb], writes=[bXB[ti]])

        lamr = kk.sb(es, "lamr", [SP_, G], F32); lami = kk.sb(es, "lami", [SP_, G], F32); blam = Buf()
        def s5_setup():
            with ExitStack() as st:
                GB = 16
                pg = lambda nm: kk.sb(st, nm, [SP_, G], F32)
                are = pg("are"); aim = pg("aim"); ldt = pg("ldt"); t1 = pg("t1"); t2 = pg("t2"); t3 = pg("t3")
                cc_ = pg("cc_"); ss_ = pg("ss_"); coefr = pg("coefr"); coefi = pg("coefi"); invr = pg("invr"); invi = pg("invi")
                pwr = kk.sb(st, "pwr", [SP_, 16, G], F32); pwi = kk.sb(st, "pwi", [SP_, 16, G], F32)
                ident = kk.sb(st, "ident", [128, 128], F32); mask = kk.sb(st, "mask", [128, 128], F32); dcol = kk.sb(st, "dcol", [128, G], F32)
                B = Buf()
                for (t, d, key) in ((are, are_d, "s0"), (aim, aim_d, "s0"), (ldt, ldt_d, "s0"), (ident, ident_d, "s0"), (mask, mask_d, "s0"), (dcol, dcol_d, "s0")):
                    DMA("sp", t[:, :], d, key, writes=[B])
                V = lambda fn: OP("dve", fn, writes=[B])
                A = lambda fn: OP("act", fn, writes=[B])
                tt = lambda o, a, b, op: V(lambda e: e.tensor_tensor(out=o, in0=a, in1=b, op=op))
                A(lambda e: e.activation(out=ldt[:, :], in_=ldt[:, :], func=AF.Exp))
                tt(t1[:, :], are[:, :], ldt[:, :], ALU.mult)
                tt(t2[:, :], aim[:, :], ldt[:, :], ALU.mult)
                A(lambda e: e.activation(out=t1[:, :], in_=t1[:, :], func=AF.Exp))
                A(lambda e: e.activation(out=ss_[:, :], in_=t2[:, :], func=AF.Sin, scale=1.0 / 32))
                A(lambda e: e.activation(out=cc_[:, :], in_=t2[:, :], func=AF.Sin, scale=1.0 / 32, bias=math.pi / 2))
                for _ in range(5):
                    tt(t2[:, :], cc_[:, :], ss_[:, :], ALU.mult)
                    tt(cc_[:, :], cc_[:, :], cc_[:, :], ALU.mult)
                    tt(ss_[:, :], ss_[:, :], ss_[:, :], ALU.mult)
                    tt(cc_[:, :], cc_[:, :], ss_[:, :], ALU.subtract)
                    V(lambda e: e.tensor_scalar(out=ss_[:, :], in0=t2[:, :], scalar1=2.0, scalar2=None, op0=ALU.mult))
                ar = pwr[:, 8, :]; ai = pwi[:, 8, :]
                tt(ar, t1[:, :], cc_[:, :], ALU.mult); tt(ai, t1[:, :], ss_[:, :], ALU.mult)
                V(lambda e: e.memset(pwr[:, 7, :], 1.0)); V(lambda e: e.memset(pwi[:, 7, :], 0.0))

                def cmul(o_r, o_i, a_r, a_i, b_r, b_i, ta, tb, neg_im=False):
                    tt(ta, a_r, b_r, ALU.mult); tt(tb, a_i, b_i, ALU.mult); tt(o_r, ta, tb, ALU.subtract)
                    tt(ta, a_r, b_i, ALU.mult); tt(tb, a_i, b_r, ALU.mult)
                    if neg_im:
                        V(lambda e: e.scalar_tensor_tensor(out=o_i, in0=ta, scalar=-1.0, in1=tb, op0=ALU.mult, op1=ALU.subtract))
                    else:
                        tt(o_i, ta, tb, ALU.add)
                for j in range(2, 9):
                    cmul(pwr[:, j + 7, :], pwi[:, j + 7, :], pwr[:, j + 6, :], pwi[:, j + 6, :], ar, ai, t2[:, :], t3[:, :])
                tt(t2[:, :], ar, ar, ALU.mult); tt(t3[:, :], ai, ai, ALU.mult); tt(t2[:, :], t2[:, :], t3[:, :], ALU.add)
                V(lambda e: e.reciprocal(out=t2[:, :], in_=t2[:, :]))
                tt(invr[:, :], ar, t2[:, :], ALU.mult)
                V(lambda e: e.scalar_tensor_tensor(out=invi[:, :], in0=ai, scalar=-1.0, in1=t2[:, :], op0=ALU.mult, op1=ALU.mult))
                V(lambda e: e.tensor_copy(out=pwr[:, 6, :], in_=invr[:, :])); V(lambda e: e.tensor_copy(out=pwi[:, 6, :], in_=invi[:, :]))
                for j in range(2, 8):
                    cmul(pwr[:, 7 - j, :], pwi[:, 7 - j, :], pwr[:, 8 - j, :], pwi[:, 8 - j, :], invr[:, :], invi[:, :], t2[:, :], t3[:, :])
                tt(t2[:, :], are[:, :], are[:, :], ALU.mult); tt(t3[:, :], aim[:, :], aim[:, :], ALU.mult); tt(t2[:, :], t2[:, :], t3[:, :], ALU.add)
                V(lambda e: e.reciprocal(out=t2[:, :], in_=t2[:, :]))
                V(lambda e: e.tensor_scalar(out=t1[:, :], in0=ar, scalar1=-1.0, scalar2=None, op0=ALU.add))
                tt(t3[:, :], t1[:, :], are[:, :], ALU.mult); tt(cc_[:, :], ai, aim[:, :], ALU.mult); tt(t3[:, :], t3[:, :], cc_[:, :], ALU.add)
                tt(coefr[:, :], t3[:, :], t2[:, :], ALU.mult)
                tt(t3[:, :], ai, are[:, :], ALU.mult); tt(cc_[:, :], t1[:, :], aim[:, :], ALU.mult); tt(t3[:, :], t3[:, :], cc_[:, :], ALU.subtract)
                tt(coefi[:, :], t3[:, :], t2[:, :], ALU.mult)
                V(lambda e: e.tensor_copy(out=lamr[:, :], in_=pwr[:, 15, :])); V(lambda e: e.tensor_copy(out=lami[:, :], in_=pwi[:, 15, :]))
                blam.w = B.w

                gsh = [SP_, GB, CG]
                cur = [B]
                V = lambda fn: OP("dve", fn, reads=[B], writes=[cur[0]])
                tt = lambda o, a, b, op: V(lambda e: e.tensor_tensor(out=o, in0=a, in1=b, op=op))

                def cmul(o_r, o_i, a_r, a_i, b_r, b_i, ta, tb, neg_im=False):
                    tt(ta, a_r, b_r, ALU.mult); tt(tb, a_i, b_i, ALU.mult); tt(o_r, ta, tb, ALU.subtract)
                    tt(ta, a_r, b_i, ALU.mult); tt(tb, a_i, b_r, ALU.mult)
                    if neg_im:
                        V(lambda e: e.scalar_tensor_tensor(out=o_i, in0=ta, scalar=-1.0, in1=tb, op0=ALU.mult, op1=ALU.subtract))
                    else:
                        tt(o_i, ta, tb, ALU.add)
                bsets = []
                for si in range(2):
                    bt = lambda nm: kk.sb(st, nm, gsh, F32)
                    T_ = dict(br_=bt("br_"), bi_=bt("bi_"), cr_=bt("cr_"), ci_=bt("ci_"), bbr=bt("bbr"), bbi=bt("bbi"), ta=bt("ta"), tb=bt("tb"),
                              Wr=kk.sb(st, "Wr", [SP_, GB, 8, CG], F32), Wi=kk.sb(st, "Wi", [SP_, GB, 8, CG], F32),
                              Xr=kk.sb(st, "Xr", [SP_, GB, 8, CG], F32), Xi=kk.sb(st, "Xi", [SP_, GB, 8, CG], F32),
                              Yr=kk.sb(st, "Yr", [SP_, GB, 9, CG], F32), Yi=kk.sb(st, "Yi", [SP_, GB, 9, CG], F32), B=Buf())
                    bsets.append(T_)
                ostg = kk.ring(st, "ostg", 3, [128, 256], BF16)
                tmpm = kk.ring(st, "tmpm", 3, [128, 128], F32)
                def unpack(gb):
                    T_ = bsets[gb % 2]
                    return T_, T_["B"]

                def batch_elem(gb):
                    g0 = gb * GB
                    T_, Bk = unpack(gb)
                    br_ = T_["br_"]; bi_ = T_["bi_"]; cr_ = T_["cr_"]; ci_ = T_["ci_"]; bbr = T_["bbr"]; bbi = T_["bbi"]; ta = T_["ta"]; tb = T_["tb"]
                    Wr = T_["Wr"]; Wi = T_["Wi"]; Xr = T_["Xr"]; Xi = T_["Xi"]; Yr = T_["Yr"]; Yi = T_["Yi"]
                    for (t, d, key) in ((br_, bre_d, "s6"), (bi_, bim_d, "s6"), (cr_, cre_d, "s6"), (ci_, cim_d, "s6")):
                        DMA("sp", t[:, :, :], d[:, g0:g0 + GB, :], f"{key}_{gb % 2}", writes=[Bk])
                    bc = lambda ap2: ap2.unsqueeze(2).to_broadcast(gsh)
                    cur[0] = Bk
                    cmul(bbr[:, :, :], bbi[:, :, :], bc(coefr[:, g0:g0 + GB]), bc(coefi[:, g0:g0 + GB]), br_[:, :, :], bi_[:, :, :], ta[:, :, :], tb[:, :, :])
                    yield
                    for s_ in range(8):
                        cur[0] = Bk
                        cmul(Wr[:, :, s_, :], Wi[:, :, s_, :], bc(pwr[:, 14 - s_, g0:g0 + GB]), bc(pwi[:, 14 - s_, g0:g0 + GB]), bbr[:, :, :], bbi[:, :, :], ta[:, :, :], tb[:, :, :])
                        yield
                        cur[0] = Bk
                        cmul(Xr[:, :, s_, :], Xi[:, :, s_, :], bc(pwr[:, 7 - s_, g0:g0 + GB]), bc(pwi[:, 7 - s_, g0:g0 + GB]), bbr[:, :, :], bbi[:, :, :], ta[:, :, :], tb[:, :, :])
                        yield
                    for t_ in range(9):
                        cur[0] = Bk
                        cmul(Yr[:, :, t_, :], Yi[:, :, t_, :], cr_[:, :, :], ci_[:, :, :], bc(pwr[:, 7 + t_, g0:g0 + GB]), bc(pwi[:, 7 + t_, g0:g0 + GB]), ta[:, :, :], tb[:, :, :], neg_im=True)
                        yield
                    DMA("sp", CCRd[:, g0:g0 + GB, :].rearrange("p g (t c) -> p g t c", c=CG), Yr[:, :, 1:9, :], f"s10_{gb % 2}", reads=[Bk], writes=[bCC[gb]])
                    DMA("sp", CCId[:, g0:g0 + GB, :].rearrange("p g (t c) -> p g t c", c=CG), Yi[:, :, 1:9, :], f"s10_{gb % 2}", reads=[Bk], writes=[bCC[gb]])

                def batch_groups(gb):
                    g0 = gb * GB
                    T_, Bk = unpack(gb)
                    Wr = T_["Wr"]; Wi = T_["Wi"]; Xr = T_["Xr"]; Xi = T_["Xi"]; Yr = T_["Yr"]; Yi = T_["Yi"]
                    for gl in range(GB):
                        g = g0 + gl
                        k, ob = ostg.get()
                        for half, Wsrc in enumerate((Wr, Wi)):
                            pb = pget()
                            OP("pe", lambda e, pb=pb, Wsrc=Wsrc, gl=gl: e.transpose(pb.ap[:, 0:SP_], Wsrc[:, gl, :, :].rearrange("p s c -> p (s c)"), ident[0:SP_, 0:SP_]), reads=[Bk, B], writes=[pb])
                            OP("act", lambda e, pb=pb, ob=ob, half=half: e.activation(out=ob.ap[:, half * 64:(half + 1) * 64], in_=pb.ap[:, 0:SP_], func=AF.Copy), reads=[pb], writes=[ob])
                        pb = pget()
                        kk.mm(pb, pb.ap[:, 0:128], [(Xr[:, gl, :, :].rearrange("p s c -> p (s c)"), Yr[:, gl, 0:8, :].rearrange("p t c -> p (t c)")),
                                                    (Xi[:, gl, :, :].rearrange("p s c -> p (s c)"), Yi[:, gl, 0:8, :].rearrange("p t c -> p (t c)"))], reads=[Bk])
                        k2, tm = tmpm.get()
                        OP("dve", lambda e, pb=pb, tm=tm: e.tensor_tensor(out=tm.ap[:, :], in0=pb.ap[:, 0:128], in1=mask[:, :], op=ALU.mult), reads=[pb, B], writes=[tm])
                        OP("dve", lambda e, tm=tm, ob=ob, g=g: e.scalar_tensor_tensor(out=ob.ap[:, 128:256], in0=ident[:, :], scalar=dcol[:, g:g + 1], in1=tm.ap[:, :], op0=ALU.mult, op1=ALU.add),
                           reads=[tm, B], writes=[ob])
                        DMA("sp", BCd[g], ob.ap[:, 0:128], f"ostg{k}", reads=[ob], writes=[bBC[g]])
                        DMA("sp", DCd[g], ob.ap[:, 128:256], f"ostg{k}", reads=[ob], writes=[bDC[g]])
                        yield

                NBAT = G // GB
                for _ in batch_elem(0):
                    pass
                for gb in range(NBAT):
                    gens = [batch_groups(gb)] + ([batch_elem(gb + 1)] if gb + 1 < NBAT else [])
                    while gens:
                        for g_ in list(gens):
                            try:
                                next(g_)
                            except StopIteration:
                                gens.remove(g_)
                P.barrier()

        def stage2():
            with ExitStack() as st:
                U = kk.sb(st, "U", [128, G, NK], BF16); bU = dbufs(G)
                hsr = kk.sb(st, "hsr", [SP_, G, NSQ], F32); hsi = kk.sb(st, "hsi", [SP_, G, NSQ], F32); bhs = Buf()
                exg = kk.sb(st, "exg", [128, 208], F32); exg2 = kk.sb(st, "exg2", [SP_, G], F32); bexg = Buf()
                wrg = kk.sb(st, "wrg", [128, LT * 128], BF16); wig = kk.sb(st, "wig", [128, LT * 128], BF16); bwg = Buf()
                h0s = kk.sb(st, "h0s", [128, LT, NSQ], F32); bh0 = Buf()
                DMA("pool", wrg[:, :], wrg_d, "c1", writes=[bwg], extra=P.bar)
                DMA("pool", wig[:, :], wig_d, "c1", writes=[bwg], extra=P.bar)
                DMA("sp", hsr[:, :, :], ssm0r_d, "hs0", writes=[bhs]); DMA("sp", hsi[:, :, :], ssm0i_d, "hs0", writes=[bhs])
                DMA("sp", h0s[:, :, :], h0s_d, "hs0", writes=[bh0])
                with ExitStack() as s2:
                    sel = kk.sb(s2, "sel", [128, 64, 128], BF16); bsel = Buf()
                    DMA("pool", sel[:, :, :].rearrange("p a b -> p (a b)"), sel_d, "sel", writes=[bsel], extra=P.bar)
                    xbring = kk.ring(s2, "xbr", 2, [128, NT], BF16)
                    for T in range(LT):
                        k, xb_ = xbring.get()
                        DMA("sp", xb_.ap[:, :], XB[T], f"xbr{k}", reads=[bXB[T]], writes=[xb_])
                        xv = xb_.ap.rearrange("p (k s) -> p k s", s=8)
                        for gl in range(8):
                            g = T * 8 + gl
                            pb = pget()
                            kk.mm(pb, pb.ap[:, :NK], [(sel[:, gl * 8 + s, :], xv[:, :, s]) for s in range(8)], reads=[xb_, bsel])
                            OP("act", lambda e, pb=pb, g=g: e.activation(out=U[:, g, :], in_=pb.ap[:, :NK], func=AF.Copy), reads=[pb], writes=[bU[g]])
                    P.barrier()

                def lru_round(final):
                    with ExitStack() as s2:
                        lout = kk.sb(s2, "lout", [128, LT, 17], F32); blout = Buf()
                        sets = []
                        for si in range(3):
                            lt = lambda nm, w=NT, dt=F32, si=si: kk.sb(s2, f"{nm}{si}", [128, w], dt)
                            S = dict(xap=lt("xap", 1203), xc=lt("xc"), xcb=lt("xcb", NT, BF16), rr=lt("rr"), ii=lt("ii"), mm_=lt("mm_"),
                                     yab=lt("yab", NT, BF16), tsm=kk.sb(s2, f"tsm{si}", [128, 16], F32), bL=Buf(), si=si)
                            sets.append(S)

                        def head(h, S):
                            xap = S["xap"]; xc = S["xc"]; xcb = S["xcb"]; rr = S["rr"]; ii = S["ii"]; mm_ = S["mm_"]; yab = S["yab"]; tsm = S["tsm"]; bL = S["bL"]; si = S["si"]
                            aa = rr; hh = xc; gg = mm_
                            xaps = xap[:, 1027:1203].rearrange("p (b t) -> p b t", t=11)
                            L_ = lambda eng, fn, extra=(): OP(eng, fn, writes=[bL], extra=extra)
                            DMA("sp", xap[:, 3:1027], XA[h][:, 0:TP], f"xa0{si}", reads=[bXA[h]], writes=[bL])
                            DMA("sp", xaps[:, :, 3:11], XA[h][:, TP:NT].rearrange("p (b t) -> p b t", t=8), f"xa0{si}", reads=[bXA[h]], writes=[bL])
                            DMA("sp", xaps[:, :, 0:3], conv0s_d[:, h, :, :], f"xa0{si}", writes=[bL])
                            yield
                            if not final:
                                L_("dve", lambda e: e.memset(xap[:, 0:3], 0.0))
                            else:
                                L_("dve", lambda e: e.tensor_scalar(out=xap[:, 0:3], in0=exg[:, 4 * h:4 * h + 3], scalar1=cst[:, C_FLAG:C_FLAG + 1], scalar2=None, op0=ALU.mult), extra=[bexg.w])
                            cw = lambda k_: cst[:, C_CONVW + 4 * h + k_:C_CONVW + 4 * h + k_ + 1]
                            cb = cst[:, C_CONVB + h:C_CONVB + h + 1]
                            xcs = xc[:, TP:NT].rearrange("p (b t) -> p b t", t=8)
                            L_("dve", lambda e: e.tensor_scalar(out=xc[:, 0:TP], in0=xap[:, 3:1027], scalar1=cw(3), scalar2=cb, op0=ALU.mult, op1=ALU.add))
                            L_("dve", lambda e: e.tensor_scalar(out=xcs, in0=xaps[:, :, 3:11], scalar1=cw(3), scalar2=cb, op0=ALU.mult, op1=ALU.add))
                            yield
                            for k_ in range(3):
                                L_("dve", lambda e, k_=k_: e.scalar_tensor_tensor(out=xc[:, 0:TP], in0=xap[:, k_:k_ + TP], scalar=cw(k_), in1=xc[:, 0:TP], op0=ALU.mult, op1=ALU.add))
                                L_("dve", lambda e, k_=k_: e.scalar_tensor_tensor(out=xcs, in0=xaps[:, :, k_:k_ + 8], scalar=cw(k_), in1=xcs, op0=ALU.mult, op1=ALU.add))
                                yield
                            if final:
                                DMA("sp", convo_o[:, h, 0, :], xap[:, 1024:1027], f"co0{si}", reads=[bL])
                                DMA("sp", convo_o[:, h, 1:17, :], xaps[:, :, 8:11], f"co0{si}", reads=[bL])
                            else:
                                DMA("sp", EXI[:, 4 * h:4 * h + 3], xap[:, 1024:1027], f"co0{si}", reads=[bL], writes=[bEXI])
                            L_("act", lambda e: e.activation(out=xcb[:, :], in_=xc[:, :], func=AF.Copy))
                            yield
                            for (o, m) in THIRDS:
                                pr = pget(); pi_ = pget()
                                kk.mm(pr, pr.ap[:, :m], [(wrg[:, h * 128:(h + 1) * 128], xcb[:, o:o + m])], reads=[bL, bwg])
                                kk.mm(pi_, pi_.ap[:, :m], [(wig[:, h * 128:(h + 1) * 128], xcb[:, o:o + m])], reads=[bL, bwg])
                                OP("act", lambda e, pr=pr, o=o, m=m: e.activation(out=rr[:, o:o + m], in_=pr.ap[:, :m], func=AF.Sigmoid, bias=cst[:, C_BRG + h:C_BRG + h + 1]), reads=[pr], writes=[bL])
                                OP("act", lambda e, pi_=pi_, o=o, m=m: e.activation(out=ii[:, o:o + m], in_=pi_.ap[:, :m], func=AF.Sigmoid, bias=cst[:, C_BIG + h:C_BIG + h + 1]), reads=[pi_], writes=[bL])
                                yield
                            L_("act", lambda e: e.activation(out=aa[:, :], in_=rr[:, :], func=AF.Exp, scale=C8[:, h:h + 1]))
                            yield
                            L_("dve", lambda e: e.tensor_tensor(out=mm_[:, :], in0=aa[:, :], in1=aa[:, :], op=ALU.mult))
                            yield
                            L_("act", lambda e: e.activation(out=mm_[:, :], in_=mm_[:, :], func=AF.Sqrt, scale=-1.0, bias=1.0))
                            L_("dve", lambda e: e.tensor_tensor(out=ii[:, :], in0=ii[:, :], in1=xc[:, :], op=ALU.mult))
                            yield
                            L_("dve", lambda e: e.tensor_tensor(out=ii[:, :], in0=ii[:, :], in1=mm_[:, :], op=ALU.mult))
                            as_ = aa[:, TP:NT].rearrange("p (b t) -> p b t", t=8)[:, :, 0]
                            bs_ = ii[:, TP:NT].rearrange("p (b t) -> p b t", t=8)[:, :, 0]
                            L_("dve", lambda e: e.tensor_tensor(out=tsm[:, :], in0=as_, in1=h0s[:, h, :], op=ALU.mult), extra=[bh0.w])
                            L_("dve", lambda e: e.tensor_tensor(out=bs_, in0=bs_, in1=tsm[:, :], op=ALU.add))
                            yield
                            if final:
                                L_("dve", lambda e: e.tensor_scalar(out=tsm[:, 0:1], in0=exg[:, 64 + h:65 + h], scalar1=cst[:, C_FLAG:C_FLAG + 1], scalar2=None, op0=ALU.mult), extra=[bexg.w])
                                L_("dve", lambda e: e.scalar_tensor_tensor(out=ii[:, 0:1], in0=aa[:, 0:1], scalar=tsm[:, 0:1], in1=ii[:, 0:1], op0=ALU.mult, op1=ALU.add))
                            L_("dve", lambda e: e.memset(aa[:, 0:1], 0.0))
                            L_("dve", lambda e: e.memset(as_, 0.0))
                            yield
                            L_("dve", lambda e: e.tensor_tensor_scan(out=hh[:, :], data0=aa[:, :], data1=ii[:, :], initial=0.0, op0=ALU.mult, op1=ALU.add))
                            yield
                            if not final:
                                OP("dve", lambda e: e.tensor_copy(out=lout[:, 0, h:h + 1], in_=hh[:, TP - 1:TP]), reads=[bL], writes=[blout])
                            else:
                                OP("dve", lambda e: e.tensor_copy(out=lout[:, h, 0:1], in_=hh[:, TP - 1:TP]), reads=[bL], writes=[blout])
                                OP("dve", lambda e: e.tensor_copy(out=lout[:, h, 1:17], in_=hh[:, TP:NT].rearrange("p (b t) -> p b t", t=8)[:, :, 7]), reads=[bL], writes=[blout])
                                DMA("sp", gg[:, :], GA[h], f"ga{si}", reads=[bGA[h]], writes=[bL])
                                yield
                                L_("dve", lambda e: e.tensor_tensor(out=yab[:, :], in0=hh[:, :], in1=gg[:, :], op=ALU.mult))
                                DMA("sp", YA[h], yab[:, :], f"ya{si}", reads=[bL], writes=[bYA[h]])
                            yield

                        pending = list(range(LT)); active = []
                        free_sets = list(sets)
                        while pending or active:
                            while pending and free_sets:
                                S = free_sets.pop(0)
                                active.append((head(pending.pop(0), S), S))
                            nxt = []
                            for (g_, S) in active:
                                try:
                                    next(g_); nxt.append((g_, S))
                                except StopIteration:
                                    free_sets.append(S)
                            active = nxt
                        if final:
                            DMA("sp", lruh_o, lout[:, :, :], "lo", reads=[blout])
                        else:
                            DMA("sp", EXI[:, 64:80], lout[:, 0, 0:16], "ex1", reads=[blout], writes=[bEXI])
                        P.barrier()

                def s5_pass(final):
                    with ExitStack() as s2:
                        Ar = kk.sb(s2, "Ar", [SP_, 64, 145], F32); Ai = kk.sb(s2, "Ai", [SP_, 64, 145], F32); bA = Buf()
                        mring = kk.ring(s2, "mr", 3, [128, 4, 128], BF16)
                        crr = kk.ring(s2, "crr", 2, [SP_, 4, 128], F32); cri = kk.ring(s2, "cri", 2, [SP_, 4, 128], F32)
                        t4 = [kk.sb(s2, f"t4_{i}", [SP_, 64], F32) for i in range(4)]
                        t5 = kk.sb(s2, "t5", [SP_, 64, NSQ], F32)
                        S_ = lambda fn, extra=(): OP("dve", fn, writes=[bA], extra=extra)
                        for hf in range(2):
                            gs = [hf * 64 + q for q in range(64)]
                            lr = lamr[:, hf * 64:(hf + 1) * 64]; li = lami[:, hf * 64:(hf + 1) * 64]
                            for q, g in enumerate(gs):
                                if q % 4 == 0:
                                    k, mb = mring.get()
                                    DMA("sp", mb.ap[:, :, :], BCd[g:g + 4].rearrange("g p c -> p g c"), f"mr{k}", reads=[bBC[g + i_] for i_ in range(4)], writes=[mb])
                                pr = pget(); pi_ = pget()
                                kk.mm(pr, pr.ap[0:SP_, :NK], [(mb.ap[:, q % 4, 0:64], U[:, g, :])], reads=[mb, bU[g]])
                                kk.mm(pi_, pi_.ap[0:SP_, :NK], [(mb.ap[:, q % 4, 64:128], U[:, g, :])], reads=[mb, bU[g]])
                                OP("act", lambda e, pr=pr, q=q: e.activation(out=Ar[:, q, 1:145], in_=pr.ap[0:SP_, :NK], func=AF.Copy), reads=[pr], writes=[bA])
                                OP("act", lambda e, pi_=pi_, q=q: e.activation(out=Ai[:, q, 1:145], in_=pi_.ap[0:SP_, :NK], func=AF.Copy), reads=[pi_], writes=[bA])
                            if not final:
                                S_(lambda e: e.memset(Ar[:, :, 0], 0.0)); S_(lambda e: e.memset(Ai[:, :, 0], 0.0))
                            else:
                                fl = cst[0:SP_, C_FLAG:C_FLAG + 1]
                                S_(lambda e, hf=hf, fl=fl: e.tensor_scalar(out=Ar[:, :, 0], in0=exg[0:SP_, 80 + hf * 64:80 + (hf + 1) * 64], scalar1=fl, scalar2=None, op0=ALU.mult), extra=[bexg.w])
                                S_(lambda e, hf=hf, fl=fl: e.tensor_scalar(out=Ai[:, :, 0], in0=exg2[:, hf * 64:(hf + 1) * 64], scalar1=fl, scalar2=None, op0=ALU.mult), extra=[bexg.w])
                            bAr = Buf(); bAi = Buf(); btm = [Buf() for _ in range(4)]
                            bAr.w = bA.w; bAi.w = bA.w; bAr.r = list(bA.r); bAi.r = list(bA.r)
                            for k_ in range(128):
                                xr_ = Ar[:, :, k_]; xi_ = Ai[:, :, k_]; nr_ = Ar[:, :, k_ + 1]; ni_ = Ai[:, :, k_ + 1]
                                OP("dve", lambda e, xr_=xr_, lr=lr: e.tensor_tensor(out=t4[0][:, :], in0=lr, in1=xr_, op=ALU.mult), reads=[bAr], writes=[btm[0]])
                                OP("dve", lambda e, xi_=xi_, li=li: e.tensor_tensor(out=t4[1][:, :], in0=li, in1=xi_, op=ALU.mult), reads=[bAi], writes=[btm[1]])
                                OP("dve", lambda e, xi_=xi_, lr=lr: e.tensor_tensor(out=t4[2][:, :], in0=lr, in1=xi_, op=ALU.mult), reads=[bAi], writes=[btm[2]])
                                OP("dve", lambda e, xr_=xr_, li=li: e.tensor_tensor(out=t4[3][:, :], in0=li, in1=xr_, op=ALU.mult), reads=[bAr], writes=[btm[3]])
                                OP("dve", lambda e, nr_=nr_: e.tensor_tensor(out=nr_, in0=nr_, in1=t4[0][:, :], op=ALU.add), reads=[btm[0]], writes=[bAr])
                                OP("dve", lambda e, nr_=nr_: e.tensor_tensor(out=nr_, in0=nr_, in1=t4[1][:, :], op=ALU.subtract), reads=[btm[1]], writes=[bAr])
                                OP("dve", lambda e, ni_=ni_: e.tensor_tensor(out=ni_, in0=ni_, in1=t4[2][:, :], op=ALU.add), reads=[btm[2]], writes=[bAi])
                                OP("dve", lambda e, ni_=ni_: e.tensor_tensor(out=ni_, in0=ni_, in1=t4[3][:, :], op=ALU.add), reads=[btm[3]], writes=[bAi])
                            bA.w = bAi.w; bA.r = []
                            if not final:
                                S_(lambda e: e.tensor_copy(out=t4[0][:, :], in_=Ar[:, :, 128])); S_(lambda e: e.tensor_copy(out=t4[1][:, :], in_=Ai[:, :, 128]))
                                DMA("sp", EXI[0:SP_, 80 + hf * 64:80 + (hf + 1) * 64], t4[0][:, :], "ex2", reads=[bA], writes=[bEXI])
                                DMA("sp", EXI[SP_:128, 80 + hf * 64:80 + (hf + 1) * 64], t4[1][:, :], "ex2", reads=[bA], writes=[bEXI])
                                bA.w = bEXI.w
                                continue
                            lrb = lr.unsqueeze(2).to_broadcast([SP_, 64, NSQ]); lib = li.unsqueeze(2).to_broadcast([SP_, 64, NSQ])
                            hr_ = hsr[:, hf * 64:(hf + 1) * 64, :]; hi_ = hsi[:, hf * 64:(hf + 1) * 64, :]
                            sr_ = Ar[:, :, 129:145]; si_ = Ai[:, :, 129:145]
                            for (a_, b_, dst_, op_) in ((lrb, hr_, sr_, ALU.add), (lib, hi_, sr_, ALU.subtract), (lrb, hi_, si_, ALU.add), (lib, hr_, si_, ALU.add)):
                                S_(lambda e, a_=a_, b_=b_: e.tensor_tensor(out=t5[:, :, :], in0=a_, in1=b_, op=ALU.mult), extra=[bhs.w])
                                S_(lambda e, dst_=dst_, op_=op_: e.tensor_tensor(out=dst_, in0=dst_, in1=t5[:, :, :], op=op_))
                            DMA("sp", ssmr_o[:, hf * 64:(hf + 1) * 64, :], Ar[:, :, 128:145], "so0", reads=[bA])
                            DMA("sp", ssmi_o[:, hf * 64:(hf + 1) * 64, :], Ai[:, :, 128:145], "so0", reads=[bA])
                            for q, g in enumerate(gs):
                                if q % 4 == 0:
                                    k, cr_b = crr.get(); k1, ci_b = cri.get(); k2, mb = mring.get()
                                    DMA("sp", cr_b.ap[:, :, :], CCRd[:, g:g + 4, :], f"crr{k}", reads=[bCC[g // 16]], writes=[cr_b])
                                    DMA("sp", ci_b.ap[:, :, :], CCId[:, g:g + 4, :], f"cri{k1}", reads=[bCC[g // 16]], writes=[ci_b])
                                    DMA("sp", mb.ap[:, :, :], DCd[g:g + 4].rearrange("g p c -> p g c"), f"mr{k2}", reads=[bDC[g + i_] for i_ in range(4)], writes=[mb])
                                qq = q % 4
                                pb = pget()
                                kk.mm(pb, pb.ap[:, 0:128], [(mb.ap[:, qq, :], U[:, g, 0:128])], reads=[mb, bU[g]])
                                kk.mm(pb, pb.ap[:, 0:128], [(cr_b.ap[:, qq, :], Ar[:, q, 0:128]), (ci_b.ap[:, qq, :], Ai[:, q, 0:128])], reads=[cr_b, ci_b, bA], start=False, sgc=True)
                                pb2 = pget()
                                kk.mm(pb2, pb2.ap[:, 0:NSQ], [(mb.ap[:, qq, :], U[:, g, 128:144])], reads=[mb, bU[g]])
                                kk.mm(pb2, pb2.ap[:, 0:NSQ], [(cr_b.ap[:, qq, :], hsr[:, g, :]), (ci_b.ap[:, qq, :], hsi[:, g, :])], reads=[cr_b, ci_b, bhs], start=False, sgc=True)
                                OP("act", lambda e, pb=pb, g=g: e.activation(out=U[:, g, 0:128], in_=pb.ap[:, 0:128], func=AF.Copy), reads=[pb], writes=[bU[g]])
                                OP("act", lambda e, pb2=pb2, g=g: e.activation(out=U[:, g, 128:144], in_=pb2.ap[:, 0:NSQ], func=AF.Copy), reads=[pb2], writes=[bU[g]])
                        P.barrier()

                lru_round(False)
                s5_pass(False)
                def cc(e):
                    return e.collective_compute("AllGather", ALU.bypass, replica_groups=[[2 * i, 2 * i + 1] for i in range(NCORES // 2)], ins=[EXI], outs=[EXO])
                OP("pool", cc, reads=[bEXI], writes=[bEXO])
                DMA("sp", exg[:, :], EXO[0:128, :], "exg", reads=[bEXO], writes=[bexg])
                DMA("sp", exg2[:, :], EXO[SP_:128, 80:208], "exg", reads=[bEXO], writes=[bexg])
                lru_round(True)
                s5_pass(True)
                with ExitStack() as s2:
                    selT = kk.sb(s2, "selT", [128, 64, 128], BF16); bselT = Buf()
                    DMA("pool", selT[:, :, :].rearrange("p a b -> p (a b)"), selT_d, "selT", writes=[bselT], extra=P.bar)
                    gT = kk.sb(s2, "gT", [128, LT, NT], BF16); bgT = dbufs(LT)
                    for T in range(LT):
                        banks = [pget(), pget(), pget()]
                        for t_ in range(8):
                            pb = banks[t_ // 3]; off = (t_ % 3) * NK
                            kk.mm(pb, pb.ap[:, off:off + NK], [(selT[:, gl * 8 + t_, :], U[:, T * 8 + gl, :]) for gl in range(8)], reads=[bselT] + [bU[T * 8 + gl] for gl in range(8)])
                        gv = gT[:, T, :].rearrange("p (k t) -> p t k", t=8)
                        for bi in range(3):
                            nt_ = 3 if bi < 2 else 2
                            pb = banks[bi]
                            OP("act", lambda e, pb=pb, bi=bi, nt_=nt_, gv=gv: e.activation(out=gv[:, bi * 3:bi * 3 + nt_, :], in_=pb.ap[:, 0:nt_ * NK].rearrange("p (t k) -> p t k", k=NK), func=AF.Gelu),
                               reads=[pb], writes=[bgT[T]])
                    sgr = kk.ring(s2, "sgr", 2, [128, 384], F32)
                    ybr = kk.ring(s2, "ybr", 2, [128, NT], BF16)
                    for o_ in range(LT):
                        wb = wload(wglu[o_], 2048)
                        k, yb_ = ybr.get()
                        for (o, m) in THIRDS:
                            pb = pget()
                            kk.mm(pb, pb.ap[:, :m], [(wb.ap[:, q * 128:(q + 1) * 128], gT[:, q, o:o + m]) for q in range(LT)], reads=[wb] + bgT)
                            k2, sb_ = sgr.get()
                            OP("act", lambda e, pb=pb, sb_=sb_, m=m, o_=o_: e.activation(out=sb_.ap[:, :m], in_=pb.ap[:, :m], func=AF.Sigmoid, bias=cst[:, C_BGLU + o_:C_BGLU + o_ + 1]), reads=[pb], writes=[sb_])
                            OP("dve", lambda e, sb_=sb_, yb_=yb_, o=o, m=m, o_=o_: e.tensor_tensor(out=yb_.ap[:, o:o + m], in0=sb_.ap[:, :m], in1=gT[:, o_, o:o + m], op=ALU.mult), reads=[sb_, bgT[o_]], writes=[yb_])
                        DMA("sp", YB[o_], yb_.ap[:, :], f"ybr{k}", reads=[yb_], writes=[bYB[o_]])
                    P.barrier()

        def stage3(t0):
            with ExitStack() as st:
                n = TB
                uT = kk.sb(st, "uT", [128, DTL, TB], BF16); buT = dbufs(DTL)
                mT = kk.sb(st, "mT", [128, DTL, TB], BF16); bmT = dbufs(DTL)
                yaT = kk.sb(st, "yaT", [128, LT, TB], BF16); ybT = kk.sb(st, "ybT", [128, LT, TB], BF16); bya = Buf(); byb = Buf()
                rstd = kk.sb(st, "rstd", [128, TB], F32); brstd = Buf()
                xring = kk.ring(st, "xr", 3, [128, TB], F32)
                fring = kk.ring(st, "fr", 2, [128, TB], F32)
                sqring = kk.ring(st, "sq", 2, [128, TB], BF16)
                sgr = kk.ring(st, "sg3", 4, [128, 288], F32)
                for T in range(LT):
                    DMA("sp", yaT[:, T, :], YA[T][:, t0:t0 + n], "ya3", reads=[bYA[T]], writes=[bya])
                    DMA("sp", ybT[:, T, :], YB[T][:, t0:t0 + n], "yb3", reads=[bYB[T]], writes=[byb])
                apply_norm(X1T, bX1, t0, n, C_GMPRE, rstd1g[:, t0:t0 + n], brstd1g, uT, buT, xring)
                for j in range(DTL):
                    wga = wload(win[48 + j], 4096); wgb = wload(win[80 + j], 4096); wa = wload(woa[j], 2048); wb_ = wload(wob[j], 2048)
                    for (o, m) in HALVES:
                        pga = pget(); pgb = pget(); ppa = pget(); ppb = pget()
                        kk.mm(pga, pga.ap[:, :m], [(wga.ap[:, q * 128:(q + 1) * 128], uT[:, q, o:o + m]) for q in range(DTL)], reads=[wga] + buT)
                        kk.mm(pgb, pgb.ap[:, :m], [(wgb.ap[:, q * 128:(q + 1) * 128], uT[:, q, o:o + m]) for q in range(DTL)], reads=[wgb] + buT)
                        kk.mm(ppa, ppa.ap[:, :m], [(wa.ap[:, q * 128:(q + 1) * 128], yaT[:, q, o:o + m]) for q in range(LT)], reads=[wa, bya])
                        kk.mm(ppb, ppb.ap[:, :m], [(wb_.ap[:, q * 128:(q + 1) * 128], ybT[:, q, o:o + m]) for q in range(LT)], reads=[wb_, byb])
                        k1, s1 = sgr.get(); k2, s2_ = sgr.get()
                        OP("act", lambda e, pga=pga, s1=s1, m=m: e.activation(out=s1.ap[:, :m], in_=pga.ap[:, :m], func=AF.Sigmoid), reads=[pga], writes=[s1])
                        OP("act", lambda e, pgb=pgb, s2_=s2_, m=m: e.activation(out=s2_.ap[:, :m], in_=pgb.ap[:, :m], func=AF.Sigmoid), reads=[pgb], writes=[s2_])
                        OP("dve", lambda e, s1=s1, ppa=ppa, m=m: e.tensor_tensor(out=s1.ap[:, :m], in0=s1.ap[:, :m], in1=ppa.ap[:, :m], op=ALU.mult), reads=[ppa], writes=[s1])
                        OP("dve", lambda e, s2_=s2_, ppb=ppb, m=m: e.tensor_tensor(out=s2_.ap[:, :m], in0=s2_.ap[:, :m], in1=ppb.ap[:, :m], op=ALU.mult), reads=[ppb], writes=[s2_])
                        OP("dve", lambda e, s1=s1, s2_=s2_, j=j, o=o, m=m: e.tensor_tensor(out=mT[:, j, o:o + m], in0=s1.ap[:, :m], in1=s2_.ap[:, :m], op=ALU.add), reads=[s1, s2_], writes=[bmT[j]])

                def wsrc(j):
                    b = wload(wo[j], 4096)
                    return [(b, 0, 32)]
                proj_to_fsc(st, DTL, wsrc, mT, bmT, n, HALVES, sqring, fring)
                finish_rstd(HALVES, rstd, brstd, n)
                resid(X1T, bX1, t0, n, cst[:, C_GMPOST:C_GMPOST + 32], rstd, brstd, X2T, bX2, xring, fring)
                P.barrier()

        def dbg_dump():
            P.barrier()
            taps = dict(BCd=BCd, DCd=DCd, CCRd=CCRd, CCId=CCId, X1T=X1T, XA=XA, GA=GA, XB=XB, YA=YA, YB=YB, X2T=X2T, EXO=EXO)
            for nm in DBG_TAPS:
                src = taps[nm]
                o = nc.dram_tensor("dbg_" + nm, list(src.shape), src.dtype, kind="ExternalOutput").ap()
                P.dma("sp", lambda e, o=o, src=src: e.dma_start(out=o, in_=src), "dbg")

        P.barrier()
        if UPTO >= 1:
            s5_setup()
        if UPTO >= 2:
            for blk in range(2):
                ffn_stage(xT, bXin, X1T, bX1, blk * TB, C_G1PRE, GP05_1, wg1, wu1, wd1, tail_fn=win_tail)
        if UPTO >= 3:
            stage2()
        if UPTO >= 4:
            for blk in range(2):
                stage3(blk * TB)
                ffn_stage(X2T, bX2, yT, bYout, blk * TB, C_G2PRE, GP05_2, wg2, wu2, wd2)
        if DEBUG:
            dbg_dump()
        P.barrier(engs=("sp",))
        block = es.enter_context(nc.Block())
        P.build(block)
    return nc


UPTO = 4
DEBUG = False
NCORES = 8
DBG_TAPS = []


def tile_w(W, K, N):
    return np.ascontiguousarray(W.reshape(K // 128, 128, N // 128, 128).transpose(2, 1, 0, 3)).reshape(N // 128, 128, K)


def vec_tiles(v, nt):
    return np.ascontiguousarray(np.asarray(v).reshape(nt, 128).T)


def make_shared(inp):
    f = lambda a: np.asarray(a, dtype=np.float32)
    sh = {}
    sh["wg1"] = tile_w(f(inp["ffn1_w_gate"])[0], D, DFF); sh["wu1"] = tile_w(f(inp["ffn1_w_up"])[0], D, DFF); sh["wd1"] = tile_w(f(inp["ffn1_w_down"])[0], DFF, D)
    sh["wg2"] = tile_w(f(inp["ffn2_w_gate"])[0], D, DFF); sh["wu2"] = tile_w(f(inp["ffn2_w_up"])[0], D, DFF); sh["wd2"] = tile_w(f(inp["ffn2_w_down"])[0], DFF, D)
    sh["win"] = tile_w(f(inp["w_in"])[0], D, 14336)
    sh["wrg"] = np.ascontiguousarray(f(inp["w_rg"])[0].transpose(1, 0, 2)).reshape(128, LT * 128)
    sh["wig"] = np.ascontiguousarray(f(inp["w_ig"])[0].transpose(1, 0, 2)).reshape(128, LT * 128)
    sh["wglu"] = tile_w(f(inp["w_glu"])[0], DL, DL); sh["woa"] = tile_w(f(inp["w_out_a"])[0], DL, D); sh["wob"] = tile_w(f(inp["w_out_b"])[0], DL, D)
    sh["wo"] = tile_w(f(inp["w_o"])[0], D, D)
    sh["are"] = np.ascontiguousarray(f(inp["ssm_a_re"])[0].T); sh["aim"] = np.ascontiguousarray(f(inp["ssm_a_im"])[0].T)
    sh["ldt"] = np.ascontiguousarray(np.broadcast_to(f(inp["ssm_log_dt"])[0][None, :], (SP_, G)))
    sh["bre"] = np.ascontiguousarray(f(inp["ssm_b_re"])[0].transpose(1, 0, 2)); sh["bim"] = np.ascontiguousarray(f(inp["ssm_b_im"])[0].transpose(1, 0, 2))
    sh["cre"] = np.ascontiguousarray(f(inp["ssm_c_re"])[0].transpose(2, 0, 1)); sh["cim"] = np.ascontiguousarray(f(inp["ssm_c_im"])[0].transpose(2, 0, 1))
    sel = np.zeros((128, 64, 128), np.float32)
    for gl in range(8):
        for s in range(8):
            for c in range(16):
                sel[gl * 16 + c, gl * 8 + s, s * 16 + c] = 1.0
    sh["sel"] = sel.reshape(128, 64 * 128)
    sh["selT"] = np.ascontiguousarray(sel.transpose(2, 1, 0)).reshape(128, 64 * 128)
    s_idx = np.arange(128) // 16
    sh["mask"] = (s_idx[:, None] <= s_idx[None, :]).astype(np.float32)
    sh["ident"] = np.eye(128, dtype=np.float32)
    d = f(inp["ssm_d"])[0].reshape(G, CG)
    sh["dcol"] = np.ascontiguousarray(np.tile(d.T, (8, 1)))
    cst = np.zeros((128, NCST), np.float32)
    for col, nm in ((C_G1PRE, "ffn1_pre_g"), (C_G1POST, "ffn1_post_g"), (C_GMPRE, "mix_pre_g"), (C_GMPOST, "mix_post_g"), (C_G2PRE, "ffn2_pre_g"), (C_G2POST, "ffn2_post_g")):
        cst[:, col:col + 32] = vec_tiles(f(inp[nm])[0], 32)
    cst[:, C_CONVW:C_CONVW + 64] = f(inp["conv_w"])[0].reshape(4, LT, 128).transpose(2, 1, 0).reshape(128, 64)
    for col, nm in ((C_CONVB, "conv_b"), (C_BRG, "b_rg"), (C_BIG, "b_ig"), (C_LAM, "lru_lambda"), (C_SSMD, "ssm_d"), (C_BGLU, "b_glu")):
        cst[:, col:col + 16] = vec_tiles(f(inp[nm])[0], 16)
    sh["cst"] = cst
    return sh


def make_core_inputs(inp, sh, c):
    f = lambda a: np.asarray(a, dtype=np.float32)
    s, hf = c // 2, c % 2
    xp = f(inp["x_prompt"])[s, hf * TP:(hf + 1) * TP, :]
    xs = f(inp["x_sample"])[NSQ * c:NSQ * (c + 1)].reshape(NSQ * TS, D)
    x = np.concatenate([xp, xs], axis=0)
    m = dict(sh)
    m["xT"] = np.ascontiguousarray(x.T).reshape(DTL, 128, NT)
    cst = sh["cst"].copy(); cst[:, C_FLAG] = float(hf); m["cst"] = cst
    sl = slice(NSQ * c, NSQ * (c + 1))
    m["h0s"] = np.ascontiguousarray(f(inp["state_lru_h"])[0, sl].reshape(NSQ, LT, 128).transpose(2, 1, 0))
    m["conv0s"] = np.ascontiguousarray(f(inp["state_conv"])[0, sl].reshape(NSQ, 3, LT, 128).transpose(3, 2, 0, 1))
    m["ssm0r"] = np.ascontiguousarray(f(inp["state_ssm_re"])[0, sl].transpose(2, 1, 0))
    m["ssm0i"] = np.ascontiguousarray(f(inp["state_ssm_im"])[0, sl].transpose(2, 1, 0))
    return m


_NC_CACHE = {}


def kernel(**inputs):
    sh = make_shared(inputs)
    in_maps = [make_core_inputs(inputs, sh, c) for c in range(8)]
    if "nc" not in _NC_CACHE:
        _NC_CACHE["nc"] = build_program()
    nc = _NC_CACHE["nc"]
    res = run_bass_kernel_spmd(nc, in_maps, core_ids=list(range(8)))
    R = res.results
    B_, S_ = 4, 2048
    yp = np.zeros((B_, S_, D), np.float32); ys = np.zeros((128, TS, D), np.float32)
    p_h = np.zeros((1, B_, DL), np.float32); p_c = np.zeros((1, B_, 3, DL), np.float32)
    p_re = np.zeros((1, B_, G, SP_), np.float32); p_im = np.zeros((1, B_, G, SP_), np.float32)
    s_h = np.zeros((1, 128, DL), np.float32); s_c = np.zeros((1, 128, 3, DL), np.float32)
    s_re = np.zeros((1, 128, G, SP_), np.float32); s_im = np.zeros((1, 128, G, SP_), np.float32)
    for c in range(8):
        s, hf = c // 2, c % 2
        y = R[c]["yT"].reshape(D, NT).T
        yp[s, hf * TP:(hf + 1) * TP] = y[:TP]
        ys[NSQ * c:NSQ * (c + 1)] = y[TP:].reshape(NSQ, TS, D)
        lh = R[c]["lruh"].transpose(2, 1, 0).reshape(17, DL)
        cv = R[c]["convo"].transpose(2, 3, 1, 0).reshape(17, 3, DL)
        sr = R[c]["ssmr"].transpose(2, 1, 0); si = R[c]["ssmi"].transpose(2, 1, 0)
        if hf == 1:
            p_h[0, s] = lh[0]; p_c[0, s] = cv[0]; p_re[0, s] = sr[0]; p_im[0, s] = si[0]
        s_h[0, NSQ * c:NSQ * (c + 1)] = lh[1:]; s_c[0, NSQ * c:NSQ * (c + 1)] = cv[1:]
        s_re[0, NSQ * c:NSQ * (c + 1)] = sr[1:]; s_im[0, NSQ * c:NSQ * (c + 1)] = si[1:]
    return (yp, ys, p_h, p_c, p_re, p_im, s_h, s_c, s_re, s_im)
```

```python
import math
import numpy as np
from contextlib import ExitStack
import concourse.bass as bass
import concourse.mybir as mybir
from concourse.bass_utils import run_bass_kernel_spmd

F32 = mybir.dt.float32
BF16 = mybir.dt.bfloat16
AF = mybir.ActivationFunctionType
ALU = mybir.AluOpType

D = 4096; DTL = 32; DFF = 11008; FFT = 86; DL = 2048; LT = 16
NT = 1152; TP = 1024; NSQ = 16; TS = 8; TB = 576; G = 128; SP_ = 64; CG = 16; NK = 144
EPS = 1e-6
WSLOT = 4096
ENGS = ("pe", "act", "dve", "pool", "sp")

C_G1PRE, C_G1POST, C_GMPRE, C_GMPOST, C_G2PRE, C_G2POST = 0, 32, 64, 96, 128, 160
C_CONVW = 192; C_CONVB = 256; C_BRG = 272; C_BIG = 288; C_LAM = 304; C_SSMD = 320; C_BGLU = 336; C_FLAG = 352
NCST = 353


class Tok:
    __slots__ = ("sem", "val", "eng")

    def __init__(self, sem, val, eng=None):
        self.sem = sem; self.val = val; self.eng = eng


class Buf:
    __slots__ = ("ap", "w", "r")

    def __init__(self, ap=None):
        self.ap = ap; self.w = None; self.r = []


class Ring:
    def __init__(self, aps):
        self.bufs = [Buf(a) for a in aps]; self.i = 0

    def get(self):
        b = self.bufs[self.i]; k = self.i
        self.i = (self.i + 1) % len(self.bufs)
        return k, b


class Prog:
    SEM_ROLL = 30000

    def __init__(self, nc, es):
        self.nc = nc; self.es = es
        self.streams = {e: [] for e in ENGS}
        self.cur_sem = {e: None for e in ENGS}
        self.cur_cnt = {e: 0 for e in ENGS}
        self.waited = {e: {} for e in ENGS}
        self.nsem = 0
        self.dma_sems = {}
        self.bar = []

    def new_sem(self, name):
        self.nsem += 1
        return self.es.enter_context(self.nc.semaphore(f"{name}_{self.nsem}"))

    def _waits(self, eng, deps):
        ws = []; w = self.waited[eng]
        for d in deps:
            if d is None:
                continue
            if isinstance(d, (list, tuple)):
                ws.extend(self._waits(eng, d)); continue
            if eng == "pe" and d.eng == "pe":
                continue
            k = id(d.sem)
            if w.get(k, -1) >= d.val:
                continue
            w[k] = d.val
            ws.append((d.sem, d.val))
        return ws

    def op(self, eng, fn, deps=(), signal=True):
        ws = self._waits(eng, deps)
        tok = None; inc = None
        if signal:
            if self.cur_sem[eng] is None or self.cur_cnt[eng] >= self.SEM_ROLL:
                self.cur_sem[eng] = self.new_sem(f"s_{eng}"); self.cur_cnt[eng] = 0
            self.cur_cnt[eng] += 1
            tok = Tok(self.cur_sem[eng], self.cur_cnt[eng], eng); inc = (self.cur_sem[eng], 1)
        self.streams[eng].append((fn, ws, inc))
        return tok

    def dma(self, eng, fn, key, deps=()):
        ws = self._waits(eng, deps)
        if key not in self.dma_sems:
            self.dma_sems[key] = [self.new_sem("d"), 0]
        ent = self.dma_sems[key]; ent[1] += 16
        self.streams[eng].append((fn, ws, (ent[0], 16)))
        return Tok(ent[0], ent[1])

    def wait_only(self, eng, deps):
        ws = self._waits(eng, deps)
        if ws:
            self.streams[eng].append((None, ws, None))

    def barrier(self, engs=("pe", "act", "dve", "sp")):
        toks = []
        for e in ENGS:
            if self.cur_sem[e] is not None:
                toks.append(Tok(self.cur_sem[e], self.cur_cnt[e]))
        for k, (s, v) in self.dma_sems.items():
            toks.append(Tok(s, v))
        for e in engs:
            self.wait_only(e, toks)
        self.bar = toks
        return toks

    def build(self, block):
        def run(e, items):
            for fn, ws, inc in items:
                for (s, v) in ws:
                    e.wait_ge(s, v)
                if fn is None:
                    continue
                ins = fn(e)
                if inc is not None:
                    ins.then_inc(inc[0], inc[1])

        @block.sync
        def _(e):
            run(e, self.streams["sp"])

        @block.scalar
        def _(e):
            run(e, self.streams["act"])

        @block.vector
        def _(e):
            run(e, self.streams["dve"])

        @block.gpsimd
        def _(e):
            run(e, self.streams["pool"])

        @block.tensor
        def _(e):
            run(e, self.streams["pe"])


class K:
    def __init__(self, nc, es):
        self.nc = nc; self.es = es; self.P = Prog(nc, es)
        self.nkey = 0
        self.lastdma = {}

    def OP(self, eng, fn, reads=(), writes=(), extra=(), signal=True):
        deps = list(extra)
        for b in reads:
            deps.append(b.w)
        for b in writes:
            deps.append(b.w); deps.extend(b.r)
        t = self.P.op(eng, fn, deps=deps, signal=signal)
        if t is not None:
            for b in reads:
                b.r.append(t)
            for b in writes:
                b.w = t; b.r = []
        return t

    def DMA(self, q, out, in_, key, reads=(), writes=(), extra=()):
        deps = list(extra)
        for b in reads:
            deps.append(b.w)
        for b in writes:
            deps.append(b.w); deps.extend(b.r)
        deps.append(self.lastdma.get(key))
        t = self.P.dma(q, lambda e: e.dma_start(out=out, in_=in_), key, deps=deps)
        self.lastdma[key] = t
        for b in reads:
            b.r.append(t)
        for b in writes:
            b.w = t; b.r = []
        return t

    def sb(self, st, name, shape, dt):
        self.nkey += 1
        return st.enter_context(self.nc.sbuf_tensor(f"{name}_s{self.nkey}", shape, dt))

    def ring(self, st, name, n, shape, dt):
        t = self.sb(st, name, [shape[0], n] + list(shape[1:]), dt)
        r = Ring([t[:, i] for i in range(n)])
        r.name = name
        return r

    def mm(self, out_buf, out_ap, pairs, reads=(), start=True, sgc=False, stop=True):
        n = len(pairs)
        deps = [out_buf.w] + list(out_buf.r)
        for b in reads:
            deps.append(b.w)
        tok = None
        for q, (l, r) in enumerate(pairs):
            tok = self.P.op("pe", (lambda e, l=l, r=r, q=q: e.matmul(out_ap, lhsT=l, rhs=r, start=(start and q == 0), stop=(stop and q == n - 1), skip_group_check=sgc)),
                            deps=deps if q == 0 else (), signal=(q == n - 1))
        for b in reads:
            b.r.append(tok)
        out_buf.w = tok; out_buf.r = []
        return tok


def chunks(n, m):
    k = (n + m - 1) // m
    base = n // k
    assert base * k == n
    return [(i * base, base) for i in range(k)]


def build_program():
    nc = bass.Bass("TRN2", target_bir_lowering=False)
    es = ExitStack()
    with es:
        kk = K(nc, es); P = kk.P; OP = kk.OP; DMA = kk.DMA

        def din(name, shape, dt=F32):
            return nc.dram_tensor(name, list(shape), dt, kind="ExternalInput").ap()

        def dout(name, shape, dt=F32):
            return nc.dram_tensor(name, list(shape), dt, kind="ExternalOutput").ap()

        def dscr(name, shape, dt=F32):
            return nc.dram_tensor(name, list(shape), dt, kind="Internal").ap()

        xT = din("xT", [DTL, 128, NT]); cst_d = din("cst", [128, NCST])
        wg1 = din("wg1", [FFT, 128, 4096]); wu1 = din("wu1", [FFT, 128, 4096]); wd1 = din("wd1", [DTL, 128, DFF])
        wg2 = din("wg2", [FFT, 128, 4096]); wu2 = din("wu2", [FFT, 128, 4096]); wd2 = din("wd2", [DTL, 128, DFF])
        win = din("win", [112, 128, 4096]); wrg_d = din("wrg", [128, LT * 128]); wig_d = din("wig", [128, LT * 128])
        wglu = din("wglu", [LT, 128, 2048]); woa = din("woa", [DTL, 128, 2048]); wob = din("wob", [DTL, 128, 2048])
        wo = din("wo", [DTL, 128, 4096])
        h0s_d = din("h0s", [128, LT, NSQ]); conv0s_d = din("conv0s", [128, LT, NSQ, 3])
        ssm0r_d = din("ssm0r", [SP_, G, NSQ]); ssm0i_d = din("ssm0i", [SP_, G, NSQ])
        are_d = din("are", [SP_, G]); aim_d = din("aim", [SP_, G]); ldt_d = din("ldt", [SP_, G])
        bre_d = din("bre", [SP_, G, CG]); bim_d = din("bim", [SP_, G, CG]); cre_d = din("cre", [SP_, G, CG]); cim_d = din("cim", [SP_, G, CG])
        sel_d = din("sel", [128, 64 * 128]); selT_d = din("selT", [128, 64 * 128])
        mask_d = din("mask", [128, 128]); ident_d = din("ident", [128, 128]); dcol_d = din("dcol", [128, G])

        yT = dout("yT", [DTL, 128, NT]); lruh_o = dout("lruh", [128, LT, 17]); convo_o = dout("convo", [128, LT, 17, 3])
        ssmr_o = dout("ssmr", [SP_, G, 17]); ssmi_o = dout("ssmi", [SP_, G, 17])

        X1T = dscr("X1T", [DTL, 128, NT]); X2T = dscr("X2T", [DTL, 128, NT]); FSC = dscr("FSC", [DTL, 128, TB])
        XA = dscr("XA", [LT, 128, NT]); GA = dscr("GA", [LT, 128, NT]); XB = dscr("XB", [LT, 128, NT], BF16)
        YA = dscr("YA", [LT, 128, NT], BF16); YB = dscr("YB", [LT, 128, NT], BF16)
        BCd = dscr("BCd", [G, 128, 128], BF16); DCd = dscr("DCd", [G, 128, 128], BF16)
        CCRd = dscr("CCRd", [SP_, G, 128]); CCId = dscr("CCId", [SP_, G, 128])
        EXI = dscr("EXI", [128, 208]); EXO = dscr("EXO", [256, 208])
        def dbufs(n):
            return [Buf() for _ in range(n)]
        bX1 = dbufs(DTL); bX2 = dbufs(DTL); bFS = dbufs(DTL); bXA = dbufs(LT); bGA = dbufs(LT); bXB = dbufs(LT)
        bYA = dbufs(LT); bYB = dbufs(LT); bBC = dbufs(G); bDC = dbufs(G); bCC = dbufs(8); bEXI = Buf(); bEXO = Buf()
        bXin = dbufs(DTL); bYout = dbufs(DTL)

        cst = kk.sb(es, "cst", [128, NCST], F32); bcst = Buf()
        drv = kk.sb(es, "drv", [128, 3 * 32 + 16], F32); bdrv = Buf()
        ones = kk.sb(es, "ones", [128, 128], BF16); bones = Buf()
        rstd1g = kk.sb(es, "rstd1g", [128, NT], F32); brstd1g = Buf()
        NWS = 5
        wring = kk.ring(es, "wr", NWS, [128, WSLOT], BF16)
        psum = es.enter_context(nc.psum_tensor("ps", [128, 8, 512], F32))
        pbank = [Buf(psum[:, i, :]) for i in range(8)]
        prr = [0]

        def pget():
            b = pbank[prr[0]]; prr[0] = (prr[0] + 1) % 6
            return b

        def wload(src, n):
            k, b = wring.get()
            DMA("pool", b.ap[:, :n], src, f"w{k}", writes=[b])
            return b

        DMA("sp", cst[:, :], cst_d, "c0", writes=[bcst])
        OP("dve", lambda e: e.memset(ones[:, :], 1.0), writes=[bones])
        OP("dve", lambda e: e.tensor_scalar(out=drv[:, 0:32], in0=cst[:, C_G1POST:C_G1POST + 32], scalar1=0.5, scalar2=None, op0=ALU.mult), reads=[bcst], writes=[bdrv])
        OP("dve", lambda e: e.tensor_scalar(out=drv[:, 32:64], in0=cst[:, C_G2POST:C_G2POST + 32], scalar1=0.5, scalar2=None, op0=ALU.mult), reads=[bcst], writes=[bdrv])
        OP("act", lambda e: e.activation(out=drv[:, 96:112], in_=cst[:, C_LAM:C_LAM + 16], func=AF.Exp, scale=-1.0), reads=[bcst], writes=[bdrv])
        OP("act", lambda e: e.activation(out=drv[:, 96:112], in_=drv[:, 96:112], func=AF.Ln, bias=1.0), writes=[bdrv])
        OP("dve", lambda e: e.tensor_scalar(out=drv[:, 96:112], in0=drv[:, 96:112], scalar1=-8.0, scalar2=None, op0=ALU.mult), writes=[bdrv])
        GP05_1 = drv[:, 0:32]; GP05_2 = drv[:, 32:64]; C8 = drv[:, 96:112]

        HALVES = chunks(TB, 512)
        THIRDS = chunks(NT, 512)

        def norm_rstd(st, src, bsrc, t0, n, cks, rstd, brstd, xring, sqring):
            ssb = [pbank[6], pbank[7], pbank[5]][:len(cks)]
            for i in range(DTL):
                k, xb_ = xring.get()
                DMA("sp", xb_.ap[:, :n], src[i][:, t0:t0 + n], f"{xring.name}{k}", reads=[bsrc[i]], writes=[xb_])
                k2, sq = sqring.get()
                OP("act", lambda e, xb_=xb_, sq=sq: e.activation(out=sq.ap[:, :n], in_=xb_.ap[:, :n], func=AF.Square), reads=[xb_], writes=[sq])
                for ci, (o, m) in enumerate(cks):
                    deps = [sq.w]
                    if i == 0:
                        deps += [ssb[ci].w] + ssb[ci].r
                    t = P.op("pe", lambda e, ci=ci, o=o, m=m, sq=sq, i=i: e.matmul(ssb[ci].ap[:, :m], lhsT=ones[:, :], rhs=sq.ap[:, o:o + m], start=(i == 0), stop=(i == DTL - 1)), deps=deps + [bones.w])
                    sq.r.append(t)
                    if i == DTL - 1:
                        ssb[ci].w = t; ssb[ci].r = []
            for ci, (o, m) in enumerate(cks):
                OP("act", lambda e, ci=ci, o=o, m=m: e.activation(out=rstd[:, o:o + m], in_=ssb[ci].ap[:, :m], func=AF.Sqrt, scale=1.0 / D, bias=EPS), reads=[ssb[ci]], writes=[brstd])
            OP("dve", lambda e: e.reciprocal(out=rstd[:, :n], in_=rstd[:, :n]), writes=[brstd])

        def apply_norm(src, bsrc, t0, n, gcol, rstd, brstd, dstT, bdst, xring):
            for i in range(DTL):
                k, xb_ = xring.get()
                DMA("sp", xb_.ap[:, :n], src[i][:, t0:t0 + n], f"{xring.name}{k}", reads=[bsrc[i]], writes=[xb_])
                OP("dve", lambda e, xb_=xb_, i=i: e.scalar_tensor_tensor(out=dstT[:, i, :n], in0=xb_.ap[:, :n], scalar=cst[:, gcol + i:gcol + i + 1], in1=rstd[:, :n], op0=ALU.mult, op1=ALU.mult),
                   reads=[xb_, brstd, bcst], writes=[bdst[i]])

        def resid(srcx, bsrcx, t0, n, gsc, rstd, brstd, dst, bdst, xring, fring, keep=None, cks=None):
            ssb = [pbank[6], pbank[7]]
            for j in range(DTL):
                k, fb = fring.get()
                DMA("sp", fb.ap[:, :n], FSC[j][:, :n], f"{fring.name}{k}", reads=[bFS[j]], writes=[fb])
                k2, xb_ = xring.get()
                DMA("sp", xb_.ap[:, :n], srcx[j][:, t0:t0 + n], f"{xring.name}{k2}", reads=[bsrcx[j]], writes=[xb_])
                OP("dve", lambda e, fb=fb, j=j: e.scalar_tensor_tensor(out=fb.ap[:, :n], in0=fb.ap[:, :n], scalar=gsc[:, j:j + 1], in1=rstd[:, :n], op0=ALU.mult, op1=ALU.mult),
                   reads=[brstd, bdrv, bcst], writes=[fb])
                if keep is None:
                    OP("dve", lambda e, fb=fb, xb_=xb_: e.tensor_tensor(out=xb_.ap[:, :n], in0=fb.ap[:, :n], in1=xb_.ap[:, :n], op=ALU.add), reads=[fb], writes=[xb_])
                    DMA("sp", dst[j][:, t0:t0 + n], xb_.ap[:, :n], f"{xring.name}s{k2}", reads=[xb_], writes=[bdst[j]])
                else:
                    x1buf, bx1, sqring = keep
                    OP("dve", lambda e, fb=fb, xb_=xb_, j=j: e.tensor_tensor(out=x1buf[:, j, :n], in0=fb.ap[:, :n], in1=xb_.ap[:, :n], op=ALU.add), reads=[fb, xb_], writes=[bx1[j]])
                    DMA("sp", dst[j][:, t0:t0 + n], x1buf[:, j, :n], f"x1s{j % 4}", reads=[bx1[j]], writes=[bdst[j]])
                    k3, sq = sqring.get()
                    OP("act", lambda e, sq=sq, j=j: e.activation(out=sq.ap[:, :n], in_=x1buf[:, j, :n], func=AF.Square), reads=[bx1[j]], writes=[sq])
                    for ci, (o, m) in enumerate(cks):
                        deps = [sq.w, bones.w]
                        if j == 0:
                            deps += [ssb[ci].w] + ssb[ci].r
                        t = P.op("pe", lambda e, ci=ci, o=o, m=m, sq=sq, j=j: e.matmul(ssb[ci].ap[:, :m], lhsT=ones[:, :], rhs=sq.ap[:, o:o + m], start=(j == 0), stop=(j == DTL - 1)), deps=deps)
                        sq.r.append(t)
                        if j == DTL - 1:
                            ssb[ci].w = t; ssb[ci].r = []

        def proj_to_fsc(st, nk, wsrc_fn, rhsT, brhs, n, cks, sqring, fring):
            ssb = [pbank[6], pbank[7]]
            for j in range(DTL):
                pieces = wsrc_fn(j)
                k, fb = fring.get()
                pbs = [pget() for _ in cks]
                for pi, (wb, kc0, nkc) in enumerate(pieces):
                    for ci, (o, m) in enumerate(cks):
                        pb = pbs[ci]
                        kk.mm(pb, pb.ap[:, :m], [(wb.ap[:, q * 128:(q + 1) * 128], rhsT[:, kc0 + q, o:o + m]) for q in range(nkc)],
                              reads=[wb] + (list(brhs) if pi == 0 else []), start=(pi == 0), stop=(pi == len(pieces) - 1), sgc=True)
                for ci, (o, m) in enumerate(cks):
                    pb = pbs[ci]
                    OP("act", lambda e, pb=pb, fb=fb, o=o, m=m: e.activation(out=fb.ap[:, o:o + m], in_=pb.ap[:, :m], func=AF.Copy), reads=[pb], writes=[fb])
                k2, sq = sqring.get()
                OP("act", lambda e, fb=fb, sq=sq: e.activation(out=sq.ap[:, :n], in_=fb.ap[:, :n], func=AF.Square), reads=[fb], writes=[sq])
                for ci, (o, m) in enumerate(cks):
                    deps = [sq.w, bones.w]
                    if j == 0:
                        deps += [ssb[ci].w] + ssb[ci].r
                    t = P.op("pe", lambda e, ci=ci, o=o, m=m, sq=sq, j=j: e.matmul(ssb[ci].ap[:, :m], lhsT=ones[:, :], rhs=sq.ap[:, o:o + m], start=(j == 0), stop=(j == DTL - 1)), deps=deps)
                    sq.r.append(t)
                    if j == DTL - 1:
                        ssb[ci].w = t; ssb[ci].r = []
                DMA("sp", FSC[j][:, :n], fb.ap[:, :n], f"{fring.name}s{k}", reads=[fb], writes=[bFS[j]])

        def finish_rstd(cks, rstd, brstd, n):
            ssb = [pbank[6], pbank[7]]
            for ci, (o, m) in enumerate(cks):
                OP("act", lambda e, ci=ci, o=o, m=m: e.activation(out=rstd[:, o:o + m], in_=ssb[ci].ap[:, :m], func=AF.Sqrt, scale=1.0 / D, bias=EPS), reads=[ssb[ci]], writes=[brstd])
            OP("dve", lambda e: e.reciprocal(out=rstd[:, :n], in_=rstd[:, :n]), writes=[brstd])

        def ffn_stage(src, bsrc, dst, bdst, t0, gprecol, gp05, Wg, Wu, Wd, tail_fn=None):
            with ExitStack() as st:
                n = TB
                hT = kk.sb(st, "hT", [128, DTL, TB], BF16); bhT = dbufs(DTL)
                hid = kk.sb(st, "hid", [128, max(FFT, 64), TB], BF16); bhid = dbufs(FFT)
                rstd = kk.sb(st, "rstd", [128, TB], F32); brstd = Buf()
                xring = kk.ring(st, "xr", 3, [128, TB], F32)
                fring = kk.ring(st, "fr", 2, [128, TB], F32)
                sqring = kk.ring(st, "sq", 2, [128, TB], BF16)
                sgring = kk.ring(st, "sg", 2, [128, 288], F32)
                norm_rstd(st, src, bsrc, t0, n, HALVES, rstd, brstd, xring, sqring)
                apply_norm(src, bsrc, t0, n, gprecol, rstd, brstd, hT, bhT, xring)
                for f in range(FFT):
                    sg = wload(Wg[f], 4096); su = wload(Wu[f], 4096)
                    for (o, m) in HALVES:
                        pg = pget(); pu = pget()
                        kk.mm(pg, pg.ap[:, :m], [(sg.ap[:, q * 128:(q + 1) * 128], hT[:, q, o:o + m]) for q in range(DTL)], reads=[sg] + bhT)
                        kk.mm(pu, pu.ap[:, :m], [(su.ap[:, q * 128:(q + 1) * 128], hT[:, q, o:o + m]) for q in range(DTL)], reads=[su] + bhT)
                        k, sb_ = sgring.get()
                        OP("act", lambda e, pg=pg, sb_=sb_, m=m: e.activation(out=sb_.ap[:, :m], in_=pg.ap[:, :m], func=AF.Silu), reads=[pg], writes=[sb_])
                        OP("dve", lambda e, pu=pu, sb_=sb_, f=f, o=o, m=m: e.tensor_tensor(out=hid[:, f, o:o + m], in0=sb_.ap[:, :m], in1=pu.ap[:, :m], op=ALU.mult),
                           reads=[sb_, pu], writes=[bhid[f]])
                def wsrc(j):
                    out = []
                    for (kc0, nkc) in [(k0, min(32, FFT - k0)) for k0 in range(0, FFT, 32)]:
                        b = wload(Wd[j][:, kc0 * 128:(kc0 + nkc) * 128], nkc * 128)
                        out.append((b, kc0, nkc))
                    return out
                proj_to_fsc(st, FFT, wsrc, hid, bhid, n, HALVES, sqring, fring)
                finish_rstd(HALVES, rstd, brstd, n)
                if tail_fn is None:
                    resid(src, bsrc, t0, n, gp05, rstd, brstd, dst, bdst, xring, fring)
                else:
                    x1buf = hid[:, :, :].rearrange("p f n -> p (f n)").bitcast(F32)[:, 0:DTL * TB].rearrange("p (i n) -> p i n", n=TB)
                    bx1 = dbufs(DTL)
                    resid(src, bsrc, t0, n, gp05, rstd, brstd, dst, bdst, xring, fring, keep=(x1buf, bx1, sqring), cks=HALVES)
                    tail_fn(st, t0, hT, bhT, rstd, brstd, xring, fring, sqring, x1buf, bx1)
                P.barrier()

        def win_tail(st, t0, uT, buT, rstd, brstd, xring, fring, sqring, x1buf, bx1):
            n = TB
            finish_rstd(HALVES, rstd, brstd, n)
            OP("dve", lambda e: e.tensor_copy(out=rstd1g[:, t0:t0 + n], in_=rstd[:, :n]), reads=[brstd], writes=[brstd1g])
            for i in range(DTL):
                OP("dve", lambda e, i=i: e.scalar_tensor_tensor(out=uT[:, i, :n], in0=x1buf[:, i, :n], scalar=cst[:, C_GMPRE + i:C_GMPRE + i + 1], in1=rstd[:, :n], op0=ALU.mult, op1=ALU.mult),
                   reads=[bx1[i], brstd, bcst], writes=[buT[i]])
            for o_ in range(48):
                wb = wload(win[o_], 4096)
                kind = o_ // 16; ti = o_ % 16
                k, fb = fring.get()
                for (o, m) in HALVES:
                    pb = pget()
                    kk.mm(pb, pb.ap[:, :m], [(wb.ap[:, q * 128:(q + 1) * 128], uT[:, q, o:o + m]) for q in range(DTL)], reads=[wb] + buT)
                    if kind == 1:
                        OP("act", lambda e, pb=pb, fb=fb, o=o, m=m: e.activation(out=fb.ap[:, o:o + m], in_=pb.ap[:, :m], func=AF.Gelu), reads=[pb], writes=[fb])
                    elif kind == 0:
                        OP("act", lambda e, pb=pb, fb=fb, o=o, m=m: e.activation(out=fb.ap[:, o:o + m], in_=pb.ap[:, :m], func=AF.Copy), reads=[pb], writes=[fb])
                    else:
                        fbb = fb.ap.bitcast(BF16)
                        OP("act", lambda e, pb=pb, fbb=fbb, o=o, m=m: e.activation(out=fbb[:, o:o + m], in_=pb.ap[:, :m], func=AF.Copy), reads=[pb], writes=[fb])
                if kind == 0:
                    DMA("sp", XA[ti][:, t0:t0 + n], fb.ap[:, :n], f"{fring.name}s{k}", reads=[fb], writes=[bXA[ti]])
                elif kind == 1:
                    DMA("sp", GA[ti][:, t0:t0 + n], fb.ap[:, :n], f"{fring.name}s{k}", reads=[fb], writes=[bGA[ti]])
                else:
                    DMA("sp", XB[ti][:, t0:t0 + n], fb.ap.bitcast(BF16)[:, :n], f"{fring.name}s{k}", reads=[fb], writes=[bXB[ti]])

        lamr = kk.sb(es, "lamr", [SP_, G], F32); lami = kk.sb(es, "lami", [SP_, G], F32); blam = Buf()
        def s5_setup():
            with ExitStack() as st:
                GB = 16
                pg = lambda nm: kk.sb(st, nm, [SP_, G], F32)
                are = pg("are"); aim = pg("aim"); ldt = pg("ldt"); t1 = pg("t1"); t2 = pg("t2"); t3 = pg("t3")
                cc_ = pg("cc_"); ss_ = pg("ss_"); coefr = pg("coefr"); coefi = pg("coefi"); invr = pg("invr"); invi = pg("invi")
                pwr = kk.sb(st, "pwr", [SP_, 16, G], F32); pwi = kk.sb(st, "pwi", [SP_, 16, G], F32)
                ident = kk.sb(st, "ident", [128, 128], F32); mask = kk.sb(st, "mask", [128, 128], F32); dcol = kk.sb(st, "dcol", [128, G], F32)
                B = Buf()
                for (t, d, key) in ((are, are_d, "s0"), (aim, aim_d, "s0"), (ldt, ldt_d, "s0"), (ident, ident_d, "s0"), (mask, mask_d, "s0"), (dcol, dcol_d, "s0")):
                    DMA("sp", t[:, :], d, key, writes=[B])
                V = lambda fn: OP("dve", fn, writes=[B])
                A = lambda fn: OP("act", fn, writes=[B])
                tt = lambda o, a, b, op: V(lambda e: e.tensor_tensor(out=o, in0=a, in1=b, op=op))
                A(lambda e: e.activation(out=ldt[:, :], in_=ldt[:, :], func=AF.Exp))
                tt(t1[:, :], are[:, :], ldt[:, :], ALU.mult)
                tt(t2[:, :], aim[:, :], ldt[:, :], ALU.mult)
                A(lambda e: e.activation(out=t1[:, :], in_=t1[:, :], func=AF.Exp))
                A(lambda e: e.activation(out=ss_[:, :], in_=t2[:, :], func=AF.Sin, scale=1.0 / 32))
                A(lambda e: e.activation(out=cc_[:, :], in_=t2[:, :], func=AF.Sin, scale=1.0 / 32, bias=math.pi / 2))
                for _ in range(5):
                    tt(t2[:, :], cc_[:, :], ss_[:, :], ALU.mult)
                    tt(cc_[:, :], cc_[:, :], cc_[:, :], ALU.mult)
                    tt(ss_[:, :], ss_[:, :], ss_[:, :], ALU.mult)
                    tt(cc_[:, :], cc_[:, :], ss_[:, :], ALU.subtract)
                    V(lambda e: e.tensor_scalar(out=ss_[:, :], in0=t2[:, :], scalar1=2.0, scalar2=None, op0=ALU.mult))
                ar = pwr[:, 8, :]; ai = pwi[:, 8, :]
                tt(ar, t1[:, :], cc_[:, :], ALU.mult); tt(ai, t1[:, :], ss_[:, :], ALU.mult)
                V(lambda e: e.memset(pwr[:, 7, :], 1.0)); V(lambda e: e.memset(pwi[:, 7, :], 0.0))

                def cmul(o_r, o_i, a_r, a_i, b_r, b_i, ta, tb, neg_im=False):
                    tt(ta, a_r, b_r, ALU.mult); tt(tb, a_i, b_i, ALU.mult); tt(o_r, ta, tb, ALU.subtract)
                    tt(ta, a_r, b_i, ALU.mult); tt(tb, a_i, b_r, ALU.mult)
                    if neg_im:
                        V(lambda e: e.scalar_tensor_tensor(out=o_i, in0=ta, scalar=-1.0, in1=tb, op0=ALU.mult, op1=ALU.subtract))
                    else:
                        tt(o_i, ta, tb, ALU.add)
                for j in range(2, 9):
                    cmul(pwr[:, j + 7, :], pwi[:, j + 7, :], pwr[:, j + 6, :], pwi[:, j + 6, :], ar, ai, t2[:, :], t3[:, :])
                tt(t2[:, :], ar, ar, ALU.mult); tt(t3[:, :], ai, ai, ALU.mult); tt(t2[:, :], t2[:, :], t3[:, :], ALU.add)
                V(lambda e: e.reciprocal(out=t2[:, :], in_=t2[:, :]))
                tt(invr[:, :], ar, t2[:, :], ALU.mult)
                V(lambda e: e.scalar_tensor_tensor(out=invi[:, :], in0=ai, scalar=-1.0, in1=t2[:, :], op0=ALU.mult, op1=ALU.mult))
                V(lambda e: e.tensor_copy(out=pwr[:, 6, :], in_=invr[:, :])); V(lambda e: e.tensor_copy(out=pwi[:, 6, :], in_=invi[:, :]))
                for j in range(2, 8):
                    cmul(pwr[:, 7 - j, :], pwi[:, 7 - j, :], pwr[:, 8 - j, :], pwi[:, 8 - j, :], invr[:, :], invi[:, :], t2[:, :], t3[:, :])
                tt(t2[:, :], are[:, :], are[:, :], ALU.mult); tt(t3[:, :], aim[:, :], aim[:, :], ALU.mult); tt(t2[:, :], t2[:, :], t3[:, :], ALU.add)
                V(lambda e: e.reciprocal(out=t2[:, :], in_=t2[:, :]))
                V(lambda e: e.tensor_scalar(out=t1[:, :], in0=ar, scalar1=-1.0, scalar2=None, op0=ALU.add))
                tt(t3[:, :], t1[:, :], are[:, :], ALU.mult); tt(cc_[:, :], ai, aim[:, :], ALU.mult); tt(t3[:, :], t3[:, :], cc_[:, :], ALU.add)
                tt(coefr[:, :], t3[:, :], t2[:, :], ALU.mult)
                tt(t3[:, :], ai, are[:, :], ALU.mult); tt(cc_[:, :], t1[:, :], aim[:, :], ALU.mult); tt(t3[:, :], t3[:, :], cc_[:, :], ALU.subtract)
                tt(coefi[:, :], t3[:, :], t2[:, :], ALU.mult)
                V(lambda e: e.tensor_copy(out=lamr[:, :], in_=pwr[:, 15, :])); V(lambda e: e.tensor_copy(out=lami[:, :], in_=pwi[:, 15, :]))
                blam.w = B.w

                gsh = [SP_, GB, CG]
                cur = [B]
                V = lambda fn: OP("dve", fn, reads=[B], writes=[cur[0]])
                tt = lambda o, a, b, op: V(lambda e: e.tensor_tensor(out=o, in0=a, in1=b, op=op))

                def cmul(o_r, o_i, a_r, a_i, b_r, b_i, ta, tb, neg_im=False):
                    tt(ta, a_r, b_r, ALU.mult); tt(tb, a_i, b_i, ALU.mult); tt(o_r, ta, tb, ALU.subtract)
                    tt(ta, a_r, b_i, ALU.mult); tt(tb, a_i, b_r, ALU.mult)
                    if neg_im:
                        V(lambda e: e.scalar_tensor_tensor(out=o_i, in0=ta, scalar=-1.0, in1=tb, op0=ALU.mult, op1=ALU.subtract))
                    else:
                        tt(o_i, ta, tb, ALU.add)
                bsets = []
                for si in range(2):
                    bt = lambda nm: kk.sb(st, nm, gsh, F32)
                    T_ = dict(br_=bt("br_"), bi_=bt("bi_"), cr_=bt("cr_"), ci_=bt("ci_"), bbr=bt("bbr"), bbi=bt("bbi"), ta=bt("ta"), tb=bt("tb"),
                              Wr=kk.sb(st, "Wr", [SP_, GB, 8, CG], F32), Wi=kk.sb(st, "Wi", [SP_, GB, 8, CG], F32),
                              Xr=kk.sb(st, "Xr", [SP_, GB, 8, CG], F32), Xi=kk.sb(st, "Xi", [SP_, GB, 8, CG], F32),
                              Yr=kk.sb(st, "Yr", [SP_, GB, 9, CG], F32), Yi=kk.sb(st, "Yi", [SP_, GB, 9, CG], F32), B=Buf())
                    bsets.append(T_)
                ostg = kk.ring(st, "ostg", 3, [128, 256], BF16)
                tmpm = kk.ring(st, "tmpm", 3, [128, 128], F32)
                def unpack(gb):
                    T_ = bsets[gb % 2]
                    return T_, T_["B"]

                def batch_elem(gb):
                    g0 = gb * GB
                    T_, Bk = unpack(gb)
                    br_ = T_["br_"]; bi_ = T_["bi_"]; cr_ = T_["cr_"]; ci_ = T_["ci_"]; bbr = T_["bbr"]; bbi = T_["bbi"]; ta = T_["ta"]; tb = T_["tb"]
                    Wr = T_["Wr"]; Wi = T_["Wi"]; Xr = T_["Xr"]; Xi = T_["Xi"]; Yr = T_["Yr"]; Yi = T_["Yi"]
                    for (t, d, key) in ((br_, bre_d, "s6"), (bi_, bim_d, "s6"), (cr_, cre_d, "s6"), (ci_, cim_d, "s6")):
                        DMA("sp", t[:, :, :], d[:, g0:g0 + GB, :], f"{key}_{gb % 2}", writes=[Bk])
                    bc = lambda ap2: ap2.unsqueeze(2).to_broadcast(gsh)
                    cur[0] = Bk
                    cmul(bbr[:, :, :], bbi[:, :, :], bc(coefr[:, g0:g0 + GB]), bc(coefi[:, g0:g0 + GB]), br_[:, :, :], bi_[:, :, :], ta[:, :, :], tb[:, :, :])
                    yield
                    for s_ in range(8):
                        cur[0] = Bk
                        cmul(Wr[:, :, s_, :], Wi[:, :, s_, :], bc(pwr[:, 14 - s_, g0:g0 + GB]), bc(pwi[:, 14 - s_, g0:g0 + GB]), bbr[:, :, :], bbi[:, :, :], ta[:, :, :], tb[:, :, :])
                        yield
                        cur[0] = Bk
                        cmul(Xr[:, :, s_, :], Xi[:, :, s_, :], bc(pwr[:, 7 - s_, g0:g0 + GB]), bc(pwi[:, 7 - s_, g0:g0 + GB]), bbr[:, :, :], bbi[:, :, :], ta[:, :, :], tb[:, :, :])
                        yield
                    for t_ in range(9):
                        cur[0] = Bk
                        cmul(Yr[:, :, t_, :], Yi[:, :, t_, :], cr_[:, :, :], ci_[:, :, :], bc(pwr[:, 7 + t_, g0:g0 + GB]), bc(pwi[:, 7 + t_, g0:g0 + GB]), ta[:, :, :], tb[:, :, :], neg_im=True)
                        yield
                    DMA("sp", CCRd[:, g0:g0 + GB, :].rearrange("p g (t c) -> p g t c", c=CG), Yr[:, :, 1:9, :], f"s10_{gb % 2}", reads=[Bk], writes=[bCC[gb]])
                    DMA("sp", CCId[:, g0:g0 + GB, :].rearrange("p g (t c) -> p g t c", c=CG), Yi[:, :, 1:9, :], f"s10_{gb % 2}", reads=[Bk], writes=[bCC[gb]])

                def batch_groups(gb):
                    g0 = gb * GB
                    T_, Bk = unpack(gb)
                    Wr = T_["Wr"]; Wi = T_["Wi"]; Xr = T_["Xr"]; Xi = T_["Xi"]; Yr = T_["Yr"]; Yi = T_["Yi"]
                    for gl in range(GB):
                        g = g0 + gl
                        k, ob = ostg.get()
                        for half, Wsrc in enumerate((Wr, Wi)):
                            pb = pget()
                            OP("pe", lambda e, pb=pb, Wsrc=Wsrc, gl=gl: e.transpose(pb.ap[:, 0:SP_], Wsrc[:, gl, :, :].rearrange("p s c -> p (s c)"), ident[0:SP_, 0:SP_]), reads=[Bk, B], writes=[pb])
                            OP("act", lambda e, pb=pb, ob=ob, half=half: e.activation(out=ob.ap[:, half * 64:(half + 1) * 64], in_=pb.ap[:, 0:SP_], func=AF.Copy), reads=[pb], writes=[ob])
                        pb = pget()
                        kk.mm(pb, pb.ap[:, 0:128], [(Xr[:, gl, :, :].rearrange("p s c -> p (s c)"), Yr[:, gl, 0:8, :].rearrange("p t c -> p (t c)")),
                                                    (Xi[:, gl, :, :].rearrange("p s c -> p (s c)"), Yi[:, gl, 0:8, :].rearrange("p t c -> p (t c)"))], reads=[Bk])
                        k2, tm = tmpm.get()
                        OP("dve", lambda e, pb=pb, tm=tm: e.tensor_tensor(out=tm.ap[:, :], in0=pb.ap[:, 0:128], in1=mask[:, :], op=ALU.mult), reads=[pb, B], writes=[tm])
                        OP("dve", lambda e, tm=tm, ob=ob, g=g: e.scalar_tensor_tensor(out=ob.ap[:, 128:256], in0=ident[:, :], scalar=dcol[:, g:g + 1], in1=tm.ap[:, :], op0=ALU.mult, op1=ALU.add),
                           reads=[tm, B], writes=[ob])
                        DMA("sp", BCd[g], ob.ap[:, 0:128], f"ostg{k}", reads=[ob], writes=[bBC[g]])
                        DMA("sp", DCd[g], ob.ap[:, 128:256], f"ostg{k}", reads=[ob], writes=[bDC[g]])
                        yield

                NBAT = G // GB
                for _ in batch_elem(0):
                    pass
                for gb in range(NBAT):
                    gens = [batch_groups(gb)] + ([batch_elem(gb + 1)] if gb + 1 < NBAT else [])
                    while gens:
                        for g_ in list(gens):
                            try:
                                next(g_)
                            except StopIteration:
                                gens.remove(g_)
                P.barrier()

        def stage2():
            with ExitStack() as st:
                U = kk.sb(st, "U", [128, G, NK], BF16); bU = dbufs(G)
                hsr = kk.sb(st, "hsr", [SP_, G, NSQ], F32); hsi = kk.sb(st, "hsi", [SP_, G, NSQ], F32); bhs = Buf()
                exg = kk.sb(st, "exg", [128, 208], F32); exg2 = kk.sb(st, "exg2", [SP_, G], F32); bexg = Buf()
                wrg = kk.sb(st, "wrg", [128, LT * 128], BF16); wig = kk.sb(st, "wig", [128, LT * 128], BF16); bwg = Buf()
                h0s = kk.sb(st, "h0s", [128, LT, NSQ], F32); bh0 = Buf()
                DMA("pool", wrg[:, :], wrg_d, "c1", writes=[bwg], extra=P.bar)
                DMA("pool", wig[:, :], wig_d, "c1", writes=[bwg], extra=P.bar)
                DMA("sp", hsr[:, :, :], ssm0r_d, "hs0", writes=[bhs]); DMA("sp", hsi[:, :, :], ssm0i_d, "hs0", writes=[bhs])
                DMA("sp", h0s[:, :, :], h0s_d, "hs0", writes=[bh0])
                c0all = kk.sb(st, "c0all", [128, LT, NSQ, 3], F32)
                DMA("sp", c0all[:, :, :, :], conv0s_d, "hs0", writes=[bh0])
                with ExitStack() as s2:
                    sel = kk.sb(s2, "sel", [128, 64, 128], BF16); bsel = Buf()
                    DMA("pool", sel[:, :, :].rearrange("p a b -> p (a b)"), sel_d, "sel", writes=[bsel], extra=P.bar)
                    xbring = kk.ring(s2, "xbr", 2, [128, NT], BF16)
                    for T in range(LT):
                        k, xb_ = xbring.get()
                        DMA("sp", xb_.ap[:, :], XB[T], f"xbr{k}", reads=[bXB[T]], writes=[xb_])
                        xv = xb_.ap.rearrange("p (k s) -> p k s", s=8)
                        for gl in range(8):
                            g = T * 8 + gl
                            pb = pget()
                            kk.mm(pb, pb.ap[:, :NK], [(sel[:, gl * 8 + s, :], xv[:, :, s]) for s in range(8)], reads=[xb_, bsel])
                            OP("act", lambda e, pb=pb, g=g: e.activation(out=U[:, g, :], in_=pb.ap[:, :NK], func=AF.Copy), reads=[pb], writes=[bU[g]])
                    P.barrier()

                def lru_round(final):
                    with ExitStack() as s2:
                        lout = kk.sb(s2, "lout", [128, LT, 17], F32); blout = Buf()
                        cvo = kk.sb(s2, "cvo", [128, LT, 17, 3], F32); bcvo = Buf()
                        tails = kk.sb(s2, "tails", [128, 64], F32)
                        sets = []
                        for si in range(3):
                            lt = lambda nm, w=NT, dt=F32, si=si: kk.sb(s2, f"{nm}{si}", [128, w], dt)
                            S = dict(xap=lt("xap", 1203), xc=lt("xc"), xcb=lt("xcb", NT, BF16), rr=lt("rr"), ii=lt("ii"), mm_=lt("mm_"),
                                     yab=lt("yab", NT, BF16), tsm=kk.sb(s2, f"tsm{si}", [128, 16], F32), bL=Buf(), si=si)
                            sets.append(S)

                        def head(h, S):
                            xap = S["xap"]; xc = S["xc"]; xcb = S["xcb"]; rr = S["rr"]; ii = S["ii"]; mm_ = S["mm_"]; yab = S["yab"]; tsm = S["tsm"]; bL = S["bL"]; si = S["si"]
                            aa = rr; hh = xc; gg = mm_
                            xaps = xap[:, 1027:1203].rearrange("p (b t) -> p b t", t=11)
                            L_ = lambda eng, fn, extra=(): OP(eng, fn, writes=[bL], extra=extra)
                            xraw = ii
                            DMA("sp", xraw[:, :], XA[h], f"xa0{si}", reads=[bXA[h]], writes=[bL])
                            yield
                            L_("dve", lambda e: e.tensor_copy(out=xap[:, 3:1027], in_=xraw[:, 0:TP]))
                            L_("dve", lambda e: e.tensor_copy(out=xaps[:, :, 3:11], in_=xraw[:, TP:NT].rearrange("p (b t) -> p b t", t=8)))
                            L_("dve", lambda e: e.tensor_copy(out=xaps[:, :, 0:3], in_=c0all[:, h, :, :]), extra=[bh0.w])
                            yield
                            if not final:
                                L_("dve", lambda e: e.memset(xap[:, 0:3], 0.0))
                            else:
                                L_("dve", lambda e: e.tensor_scalar(out=xap[:, 0:3], in0=exg[:, 4 * h:4 * h + 3], scalar1=cst[:, C_FLAG:C_FLAG + 1], scalar2=None, op0=ALU.mult), extra=[bexg.w])
                            cw = lambda k_: cst[:, C_CONVW + 4 * h + k_:C_CONVW + 4 * h + k_ + 1]
                            cb = cst[:, C_CONVB + h:C_CONVB + h + 1]
                            xcs = xc[:, TP:NT].rearrange("p (b t) -> p b t", t=8)
                            L_("dve", lambda e: e.tensor_scalar(out=xc[:, 0:TP], in0=xap[:, 3:1027], scalar1=cw(3), scalar2=cb, op0=ALU.mult, op1=ALU.add))
                            L_("dve", lambda e: e.tensor_scalar(out=xcs, in0=xaps[:, :, 3:11], scalar1=cw(3), scalar2=cb, op0=ALU.mult, op1=ALU.add))
                            yield
                            for k_ in range(3):
                                L_("dve", lambda e, k_=k_: e.scalar_tensor_tensor(out=xc[:, 0:TP], in0=xap[:, k_:k_ + TP], scalar=cw(k_), in1=xc[:, 0:TP], op0=ALU.mult, op1=ALU.add))
                                L_("dve", lambda e, k_=k_: e.scalar_tensor_tensor(out=xcs, in0=xaps[:, :, k_:k_ + 8], scalar=cw(k_), in1=xcs, op0=ALU.mult, op1=ALU.add))
                                yield
                            if final:
                                OP("dve", lambda e: e.tensor_copy(out=cvo[:, h, 0, :], in_=xap[:, 1024:1027]), reads=[bL], writes=[bcvo])
                                OP("dve", lambda e: e.tensor_copy(out=cvo[:, h, 1:17, :], in_=xaps[:, :, 8:11]), reads=[bL], writes=[bcvo])
                            else:
                                OP("dve", lambda e: e.tensor_copy(out=tails[:, 4 * h:4 * h + 3], in_=xap[:, 1024:1027]), reads=[bL], writes=[bcvo])
                            L_("act", lambda e: e.activation(out=xcb[:, :], in_=xc[:, :], func=AF.Copy))
                            yield
                            for (o, m) in THIRDS:
                                pr = pget(); pi_ = pget()
                                kk.mm(pr, pr.ap[:, :m], [(wrg[:, h * 128:(h + 1) * 128], xcb[:, o:o + m])], reads=[bL, bwg])
                                kk.mm(pi_, pi_.ap[:, :m], [(wig[:, h * 128:(h + 1) * 128], xcb[:, o:o + m])], reads=[bL, bwg])
                                OP("act", lambda e, pr=pr, o=o, m=m: e.activation(out=rr[:, o:o + m], in_=pr.ap[:, :m], func=AF.Sigmoid, bias=cst[:, C_BRG + h:C_BRG + h + 1]), reads=[pr], writes=[bL])
                                OP("act", lambda e, pi_=pi_, o=o, m=m: e.activation(out=ii[:, o:o + m], in_=pi_.ap[:, :m], func=AF.Sigmoid, bias=cst[:, C_BIG + h:C_BIG + h + 1]), reads=[pi_], writes=[bL])
                                yield
                            L_("act", lambda e: e.activation(out=aa[:, :], in_=rr[:, :], func=AF.Exp, scale=C8[:, h:h + 1]))
                            yield
                            L_("dve", lambda e: e.tensor_tensor(out=mm_[:, :], in0=aa[:, :], in1=aa[:, :], op=ALU.mult))
                            yield
                            L_("act", lambda e: e.activation(out=mm_[:, :], in_=mm_[:, :], func=AF.Sqrt, scale=-1.0, bias=1.0))
                            L_("dve", lambda e: e.tensor_tensor(out=ii[:, :], in0=ii[:, :], in1=xc[:, :], op=ALU.mult))
                            yield
                            L_("dve", lambda e: e.tensor_tensor(out=ii[:, :], in0=ii[:, :], in1=mm_[:, :], op=ALU.mult))
                            as_ = aa[:, TP:NT].rearrange("p (b t) -> p b t", t=8)[:, :, 0]
                            bs_ = ii[:, TP:NT].rearrange("p (b t) -> p b t", t=8)[:, :, 0]
                            L_("dve", lambda e: e.tensor_tensor(out=tsm[:, :], in0=as_, in1=h0s[:, h, :], op=ALU.mult), extra=[bh0.w])
                            L_("dve", lambda e: e.tensor_tensor(out=bs_, in0=bs_, in1=tsm[:, :], op=ALU.add))
                            yield
                            if final:
                                L_("dve", lambda e: e.tensor_scalar(out=tsm[:, 0:1], in0=exg[:, 64 + h:65 + h], scalar1=cst[:, C_FLAG:C_FLAG + 1], scalar2=None, op0=ALU.mult), extra=[bexg.w])
                                L_("dve", lambda e: e.scalar_tensor_tensor(out=ii[:, 0:1], in0=aa[:, 0:1], scalar=tsm[:, 0:1], in1=ii[:, 0:1], op0=ALU.mult, op1=ALU.add))
                            L_("dve", lambda e: e.memset(aa[:, 0:1], 0.0))
                            L_("dve", lambda e: e.memset(as_, 0.0))
                            yield
                            L_("dve", lambda e: e.tensor_tensor_scan(out=hh[:, :], data0=aa[:, :], data1=ii[:, :], initial=0.0, op0=ALU.mult, op1=ALU.add))
                            yield
                            if not final:
                                OP("dve", lambda e: e.tensor_copy(out=lout[:, 0, h:h + 1], in_=hh[:, TP - 1:TP]), reads=[bL], writes=[blout])
                            else:
                                OP("dve", lambda e: e.tensor_copy(out=lout[:, h, 0:1], in_=hh[:, TP - 1:TP]), reads=[bL], writes=[blout])
                                OP("dve", lambda e: e.tensor_copy(out=lout[:, h, 1:17], in_=hh[:, TP:NT].rearrange("p (b t) -> p b t", t=8)[:, :, 7]), reads=[bL], writes=[blout])
                                DMA("sp", gg[:, :], GA[h], f"ga{si}", reads=[bGA[h]], writes=[bL])
                                yield
                                L_("dve", lambda e: e.tensor_tensor(out=yab[:, :], in0=hh[:, :], in1=gg[:, :], op=ALU.mult))
                                DMA("sp", YA[h], yab[:, :], f"ya{si}", reads=[bL], writes=[bYA[h]])
                            yield

                        pending = list(range(LT)); active = []
                        free_sets = list(sets)
                        while pending or active:
                            while pending and free_sets:
                                S = free_sets.pop(0)
                                active.append((head(pending.pop(0), S), S))
                            nxt = []
                            for (g_, S) in active:
                                try:
                                    next(g_); nxt.append((g_, S))
                                except StopIteration:
                                    free_sets.append(S)
                            active = nxt
                        if final:
                            DMA("sp", lruh_o, lout[:, :, :], "lo", reads=[blout])
                            DMA("sp", convo_o, cvo[:, :, :, :], "lo", reads=[bcvo])
                        else:
                            DMA("sp", EXI[:, 64:80], lout[:, 0, 0:16], "ex1", reads=[blout], writes=[bEXI])
                            DMA("sp", EXI[:, 0:64], tails[:, :], "ex1", reads=[bcvo], writes=[bEXI])
                        P.barrier()

                def s5_pass(final):
                    with ExitStack() as s2:
                        Ar = kk.sb(s2, "Ar", [SP_, 64, 145], F32); Ai = kk.sb(s2, "Ai", [SP_, 64, 145], F32); bA = Buf()
                        mring = kk.ring(s2, "mr", 3, [128, 4, 128], BF16)
                        crr = kk.ring(s2, "crr", 2, [SP_, 4, 128], F32); cri = kk.ring(s2, "cri", 2, [SP_, 4, 128], F32)
                        t4 = [kk.sb(s2, f"t4_{i}", [SP_, 64], F32) for i in range(4)]
                        t5 = kk.sb(s2, "t5", [SP_, 64, NSQ], F32)
                        sot = kk.sb(s2, "sot", [SP_, 64, 17], F32); bsot = Buf()
                        S_ = lambda fn, extra=(): OP("dve", fn, writes=[bA], extra=extra)
                        for hf in range(2):
                            gs = [hf * 64 + q for q in range(64)]
                            lr = lamr[:, hf * 64:(hf + 1) * 64]; li = lami[:, hf * 64:(hf + 1) * 64]
                            for q, g in enumerate(gs):
                                if q % 4 == 0:
                                    k, mb = mring.get()
                                    DMA("sp", mb.ap[:, :, :], BCd[g:g + 4].rearrange("g p c -> p g c"), f"mr{k}", reads=[bBC[g + i_] for i_ in range(4)], writes=[mb])
                                pr = pget(); pi_ = pget()
                                kk.mm(pr, pr.ap[0:SP_, :NK], [(mb.ap[:, q % 4, 0:64], U[:, g, :])], reads=[mb, bU[g]])
                                kk.mm(pi_, pi_.ap[0:SP_, :NK], [(mb.ap[:, q % 4, 64:128], U[:, g, :])], reads=[mb, bU[g]])
                                OP("act", lambda e, pr=pr, q=q: e.activation(out=Ar[:, q, 1:145], in_=pr.ap[0:SP_, :NK], func=AF.Copy), reads=[pr], writes=[bA])
                                OP("act", lambda e, pi_=pi_, q=q: e.activation(out=Ai[:, q, 1:145], in_=pi_.ap[0:SP_, :NK], func=AF.Copy), reads=[pi_], writes=[bA])
                            if not final:
                                S_(lambda e: e.memset(Ar[:, :, 0], 0.0)); S_(lambda e: e.memset(Ai[:, :, 0], 0.0))
                            else:
                                fl = cst[0:SP_, C_FLAG:C_FLAG + 1]
                                S_(lambda e, hf=hf, fl=fl: e.tensor_scalar(out=Ar[:, :, 0], in0=exg[0:SP_, 80 + hf * 64:80 + (hf + 1) * 64], scalar1=fl, scalar2=None, op0=ALU.mult), extra=[bexg.w])
                                S_(lambda e, hf=hf, fl=fl: e.tensor_scalar(out=Ai[:, :, 0], in0=exg2[:, hf * 64:(hf + 1) * 64], scalar1=fl, scalar2=None, op0=ALU.mult), extra=[bexg.w])
                            bAr = Buf(); bAi = Buf(); btm = [Buf() for _ in range(4)]
                            bAr.w = bA.w; bAi.w = bA.w; bAr.r = list(bA.r); bAi.r = list(bA.r)
                            for k_ in range(128):
                                xr_ = Ar[:, :, k_]; xi_ = Ai[:, :, k_]; nr_ = Ar[:, :, k_ + 1]; ni_ = Ai[:, :, k_ + 1]
                                OP("dve", lambda e, xr_=xr_, lr=lr: e.tensor_tensor(out=t4[0][:, :], in0=lr, in1=xr_, op=ALU.mult), reads=[bAr], writes=[btm[0]])
                                OP("dve", lambda e, xi_=xi_, li=li: e.tensor_tensor(out=t4[1][:, :], in0=li, in1=xi_, op=ALU.mult), reads=[bAi], writes=[btm[1]])
                                OP("dve", lambda e, xi_=xi_, lr=lr: e.tensor_tensor(out=t4[2][:, :], in0=lr, in1=xi_, op=ALU.mult), reads=[bAi], writes=[btm[2]])
                                OP("dve", lambda e, xr_=xr_, li=li: e.tensor_tensor(out=t4[3][:, :], in0=li, in1=xr_, op=ALU.mult), reads=[bAr], writes=[btm[3]])
                                OP("dve", lambda e, nr_=nr_: e.tensor_tensor(out=nr_, in0=nr_, in1=t4[0][:, :], op=ALU.add), reads=[btm[0]], writes=[bAr])
                                OP("dve", lambda e, nr_=nr_: e.tensor_tensor(out=nr_, in0=nr_, in1=t4[1][:, :], op=ALU.subtract), reads=[btm[1]], writes=[bAr])
                                OP("dve", lambda e, ni_=ni_: e.tensor_tensor(out=ni_, in0=ni_, in1=t4[2][:, :], op=ALU.add), reads=[btm[2]], writes=[bAi])
                                OP("dve", lambda e, ni_=ni_: e.tensor_tensor(out=ni_, in0=ni_, in1=t4[3][:, :], op=ALU.add), reads=[btm[3]], writes=[bAi])
                            bA.w = bAi.w; bA.r = []
                            if not final:
                                S_(lambda e: e.tensor_copy(out=t4[0][:, :], in_=Ar[:, :, 128])); S_(lambda e: e.tensor_copy(out=t4[1][:, :], in_=Ai[:, :, 128]))
                                DMA("sp", EXI[0:SP_, 80 + hf * 64:80 + (hf + 1) * 64], t4[0][:, :], "ex2", reads=[bA], writes=[bEXI])
                                DMA("sp", EXI[SP_:128, 80 + hf * 64:80 + (hf + 1) * 64], t4[1][:, :], "ex2", reads=[bA], writes=[bEXI])
                                bA.w = bEXI.w
                                continue
                            lrb = lr.unsqueeze(2).to_broadcast([SP_, 64, NSQ]); lib = li.unsqueeze(2).to_broadcast([SP_, 64, NSQ])
                            hr_ = hsr[:, hf * 64:(hf + 1) * 64, :]; hi_ = hsi[:, hf * 64:(hf + 1) * 64, :]
                            sr_ = Ar[:, :, 129:145]; si_ = Ai[:, :, 129:145]
                            for (a_, b_, dst_, op_) in ((lrb, hr_, sr_, ALU.add), (lib, hi_, sr_, ALU.subtract), (lrb, hi_, si_, ALU.add), (lib, hr_, si_, ALU.add)):
                                S_(lambda e, a_=a_, b_=b_: e.tensor_tensor(out=t5[:, :, :], in0=a_, in1=b_, op=ALU.mult), extra=[bhs.w])
                                S_(lambda e, dst_=dst_, op_=op_: e.tensor_tensor(out=dst_, in0=dst_, in1=t5[:, :, :], op=op_))
                            for (src_, dst_o) in ((Ar, ssmr_o), (Ai, ssmi_o)):
                                OP("dve", lambda e, src_=src_: e.tensor_copy(out=sot[:, :, :], in_=src_[:, :, 128:145]), reads=[bA], writes=[bsot])
                                DMA("sp", dst_o[:, hf * 64:(hf + 1) * 64, :], sot[:, :, :], "so0", reads=[bsot])
                            for q, g in enumerate(gs):
                                if q % 4 == 0:
                                    k, cr_b = crr.get(); k1, ci_b = cri.get(); k2, mb = mring.get()
                                    DMA("sp", cr_b.ap[:, :, :], CCRd[:, g:g + 4, :], f"crr{k}", reads=[bCC[g // 16]], writes=[cr_b])
                                    DMA("sp", ci_b.ap[:, :, :], CCId[:, g:g + 4, :], f"cri{k1}", reads=[bCC[g // 16]], writes=[ci_b])
                                    DMA("sp", mb.ap[:, :, :], DCd[g:g + 4].rearrange("g p c -> p g c"), f"mr{k2}", reads=[bDC[g + i_] for i_ in range(4)], writes=[mb])
                                qq = q % 4
                                pb = pget()
                                kk.mm(pb, pb.ap[:, 0:128], [(mb.ap[:, qq, :], U[:, g, 0:128])], reads=[mb, bU[g]])
                                kk.mm(pb, pb.ap[:, 0:128], [(cr_b.ap[:, qq, :], Ar[:, q, 0:128]), (ci_b.ap[:, qq, :], Ai[:, q, 0:128])], reads=[cr_b, ci_b, bA], start=False, sgc=True)
                                pb2 = pget()
                                kk.mm(pb2, pb2.ap[:, 0:NSQ], [(mb.ap[:, qq, :], U[:, g, 128:144])], reads=[mb, bU[g]])
                                kk.mm(pb2, pb2.ap[:, 0:NSQ], [(cr_b.ap[:, qq, :], hsr[:, g, :]), (ci_b.ap[:, qq, :], hsi[:, g, :])], reads=[cr_b, ci_b, bhs], start=False, sgc=True)
                                OP("act", lambda e, pb=pb, g=g: e.activation(out=U[:, g, 0:128], in_=pb.ap[:, 0:128], func=AF.Copy), reads=[pb], writes=[bU[g]])
                                OP("act", lambda e, pb2=pb2, g=g: e.activation(out=U[:, g, 128:144], in_=pb2.ap[:, 0:NSQ], func=AF.Copy), reads=[pb2], writes=[bU[g]])
                        P.barrier()

                lru_round(False)
                s5_pass(False)
                def cc(e):
                    return e.collective_compute("AllGather", ALU.bypass, replica_groups=[[2 * i, 2 * i + 1] for i in range(NCORES // 2)], ins=[EXI], outs=[EXO])
                OP("pool", cc, reads=[bEXI], writes=[bEXO])
                DMA("sp", exg[:, :], EXO[0:128, :], "exg", reads=[bEXO], writes=[bexg])
                DMA("sp", exg2[:, :], EXO[SP_:128, 80:208], "exg", reads=[bEXO], writes=[bexg])
                lru_round(True)
                s5_pass(True)
                with ExitStack() as s2:
                    selT = kk.sb(s2, "selT", [128, 64, 128], BF16); bselT = Buf()
                    DMA("pool", selT[:, :, :].rearrange("p a b -> p (a b)"), selT_d, "selT", writes=[bselT], extra=P.bar)
                    gT = kk.sb(s2, "gT", [128, LT, NT], BF16); bgT = dbufs(LT)
                    for T in range(LT):
                        banks = [pget(), pget(), pget()]
                        for t_ in range(8):
                            pb = banks[t_ // 3]; off = (t_ % 3) * NK
                            kk.mm(pb, pb.ap[:, off:off + NK], [(selT[:, gl * 8 + t_, :], U[:, T * 8 + gl, :]) for gl in range(8)], reads=[bselT] + [bU[T * 8 + gl] for gl in range(8)])
                        gv = gT[:, T, :].rearrange("p (k t) -> p t k", t=8)
                        for bi in range(3):
                            nt_ = 3 if bi < 2 else 2
                            pb = banks[bi]
                            OP("act", lambda e, pb=pb, bi=bi, nt_=nt_, gv=gv: e.activation(out=gv[:, bi * 3:bi * 3 + nt_, :], in_=pb.ap[:, 0:nt_ * NK].rearrange("p (t k) -> p t k", k=NK), func=AF.Gelu),
                               reads=[pb], writes=[bgT[T]])
                    sgr = kk.ring(s2, "sgr", 2, [128, 384], F32)
                    ybr = kk.ring(s2, "ybr", 2, [128, NT], BF16)
                    for o_ in range(LT):
                        wb = wload(wglu[o_], 2048)
                        k, yb_ = ybr.get()
                        for (o, m) in THIRDS:
                            pb = pget()
                            kk.mm(pb, pb.ap[:, :m], [(wb.ap[:, q * 128:(q + 1) * 128], gT[:, q, o:o + m]) for q in range(LT)], reads=[wb] + bgT)
                            k2, sb_ = sgr.get()
                            OP("act", lambda e, pb=pb, sb_=sb_, m=m, o_=o_: e.activation(out=sb_.ap[:, :m], in_=pb.ap[:, :m], func=AF.Sigmoid, bias=cst[:, C_BGLU + o_:C_BGLU + o_ + 1]), reads=[pb], writes=[sb_])
                            OP("dve", lambda e, sb_=sb_, yb_=yb_, o=o, m=m, o_=o_: e.tensor_tensor(out=yb_.ap[:, o:o + m], in0=sb_.ap[:, :m], in1=gT[:, o_, o:o + m], op=ALU.mult), reads=[sb_, bgT[o_]], writes=[yb_])
                        DMA("sp", YB[o_], yb_.ap[:, :], f"ybr{k}", reads=[yb_], writes=[bYB[o_]])
                    P.barrier()

        def stage3(t0):
            with ExitStack() as st:
                n = TB
                uT = kk.sb(st, "uT", [128, DTL, TB], BF16); buT = dbufs(DTL)
                mT = kk.sb(st, "mT", [128, DTL, TB], BF16); bmT = dbufs(DTL)
                yaT = kk.sb(st, "yaT", [128, LT, TB], BF16); ybT = kk.sb(st, "ybT", [128, LT, TB], BF16); bya = Buf(); byb = Buf()
                rstd = kk.sb(st, "rstd", [128, TB], F32); brstd = Buf()
                xring = kk.ring(st, "xr", 3, [128, TB], F32)
                fring = kk.ring(st, "fr", 2, [128, TB], F32)
                sqring = kk.ring(st, "sq", 2, [128, TB], BF16)
                sgr = kk.ring(st, "sg3", 4, [128, 288], F32)
                for T in range(LT):
                    DMA("sp", yaT[:, T, :], YA[T][:, t0:t0 + n], "ya3", reads=[bYA[T]], writes=[bya])
                    DMA("sp", ybT[:, T, :], YB[T][:, t0:t0 + n], "yb3", reads=[bYB[T]], writes=[byb])
                apply_norm(X1T, bX1, t0, n, C_GMPRE, rstd1g[:, t0:t0 + n], brstd1g, uT, buT, xring)
                for j in range(DTL):
                    wga = wload(win[48 + j], 4096); wgb = wload(win[80 + j], 4096); wa = wload(woa[j], 2048); wb_ = wload(wob[j], 2048)
                    for (o, m) in HALVES:
                        pga = pget(); pgb = pget(); ppa = pget(); ppb = pget()
                        kk.mm(pga, pga.ap[:, :m], [(wga.ap[:, q * 128:(q + 1) * 128], uT[:, q, o:o + m]) for q in range(DTL)], reads=[wga] + buT)
                        kk.mm(pgb, pgb.ap[:, :m], [(wgb.ap[:, q * 128:(q + 1) * 128], uT[:, q, o:o + m]) for q in range(DTL)], reads=[wgb] + buT)
                        kk.mm(ppa, ppa.ap[:, :m], [(wa.ap[:, q * 128:(q + 1) * 128], yaT[:, q, o:o + m]) for q in range(LT)], reads=[wa, bya])
                        kk.mm(ppb, ppb.ap[:, :m], [(wb_.ap[:, q * 128:(q + 1) * 128], ybT[:, q, o:o + m]) for q in range(LT)], reads=[wb_, byb])
                        k1, s1 = sgr.get(); k2, s2_ = sgr.get()
                        OP("act", lambda e, pga=pga, s1=s1, m=m: e.activation(out=s1.ap[:, :m], in_=pga.ap[:, :m], func=AF.Sigmoid), reads=[pga], writes=[s1])
                        OP("act", lambda e, pgb=pgb, s2_=s2_, m=m: e.activation(out=s2_.ap[:, :m], in_=pgb.ap[:, :m], func=AF.Sigmoid), reads=[pgb], writes=[s2_])
                        OP("dve", lambda e, s1=s1, ppa=ppa, m=m: e.tensor_tensor(out=s1.ap[:, :m], in0=s1.ap[:, :m], in1=ppa.ap[:, :m], op=ALU.mult), reads=[ppa], writes=[s1])
                        OP("dve", lambda e, s2_=s2_, ppb=ppb, m=m: e.tensor_tensor(out=s2_.ap[:, :m], in0=s2_.ap[:, :m], in1=ppb.ap[:, :m], op=ALU.mult), reads=[ppb], writes=[s2_])
                        OP("dve", lambda e, s1=s1, s2_=s2_, j=j, o=o, m=m: e.tensor_tensor(out=mT[:, j, o:o + m], in0=s1.ap[:, :m], in1=s2_.ap[:, :m], op=ALU.add), reads=[s1, s2_], writes=[bmT[j]])

                def wsrc(j):
                    b = wload(wo[j], 4096)
                    return [(b, 0, 32)]
                proj_to_fsc(st, DTL, wsrc, mT, bmT, n, HALVES, sqring, fring)
                finish_rstd(HALVES, rstd, brstd, n)
                resid(X1T, bX1, t0, n, cst[:, C_GMPOST:C_GMPOST + 32], rstd, brstd, X2T, bX2, xring, fring)
                P.barrier()

        def dbg_dump():
            P.barrier()
            taps = dict(BCd=BCd, DCd=DCd, CCRd=CCRd, CCId=CCId, X1T=X1T, XA=XA, GA=GA, XB=XB, YA=YA, YB=YB, X2T=X2T, EXO=EXO)
            for nm in DBG_TAPS:
                src = taps[nm]
                o = nc.dram_tensor("dbg_" + nm, list(src.shape), src.dtype, kind="ExternalOutput").ap()
                P.dma("sp", lambda e, o=o, src=src: e.dma_start(out=o, in_=src), "dbg")

        P.barrier()
        if UPTO >= 1:
            s5_setup()
        if UPTO >= 2:
            for blk in range(2):
                ffn_stage(xT, bXin, X1T, bX1, blk * TB, C_G1PRE, GP05_1, wg1, wu1, wd1, tail_fn=win_tail)
        if UPTO >= 3:
            stage2()
        if UPTO >= 4:
            for blk in range(2):
                stage3(blk * TB)
                ffn_stage(X2T, bX2, yT, bYout, blk * TB, C_G2PRE, GP05_2, wg2, wu2, wd2)
        if DEBUG:
            dbg_dump()
        P.barrier(engs=("sp",))
        block = es.enter_context(nc.Block())
        P.build(block)
    return nc


UPTO = 4
DEBUG = False
NCORES = 8
DBG_TAPS = []


def tile_w(W, K, N):
    return np.ascontiguousarray(W.reshape(K // 128, 128, N // 128, 128).transpose(2, 1, 0, 3)).reshape(N // 128, 128, K)


def vec_tiles(v, nt):
    return np.ascontiguousarray(np.asarray(v).reshape(nt, 128).T)


def make_shared(inp):
    f = lambda a: np.asarray(a, dtype=np.float32)
    sh = {}
    sh["wg1"] = tile_w(f(inp["ffn1_w_gate"])[0], D, DFF); sh["wu1"] = tile_w(f(inp["ffn1_w_up"])[0], D, DFF); sh["wd1"] = tile_w(f(inp["ffn1_w_down"])[0], DFF, D)
    sh["wg2"] = tile_w(f(inp["ffn2_w_gate"])[0], D, DFF); sh["wu2"] = tile_w(f(inp["ffn2_w_up"])[0], D, DFF); sh["wd2"] = tile_w(f(inp["ffn2_w_down"])[0], DFF, D)
    sh["win"] = tile_w(f(inp["w_in"])[0], D, 14336)
    sh["wrg"] = np.ascontiguousarray(f(inp["w_rg"])[0].transpose(1, 0, 2)).reshape(128, LT * 128)
    sh["wig"] = np.ascontiguousarray(f(inp["w_ig"])[0].transpose(1, 0, 2)).reshape(128, LT * 128)
    sh["wglu"] = tile_w(f(inp["w_glu"])[0], DL, DL); sh["woa"] = tile_w(f(inp["w_out_a"])[0], DL, D); sh["wob"] = tile_w(f(inp["w_out_b"])[0], DL, D)
    sh["wo"] = tile_w(f(inp["w_o"])[0], D, D)
    sh["are"] = np.ascontiguousarray(f(inp["ssm_a_re"])[0].T); sh["aim"] = np.ascontiguousarray(f(inp["ssm_a_im"])[0].T)
    sh["ldt"] = np.ascontiguousarray(np.broadcast_to(f(inp["ssm_log_dt"])[0][None, :], (SP_, G)))
    sh["bre"] = np.ascontiguousarray(f(inp["ssm_b_re"])[0].transpose(1, 0, 2)); sh["bim"] = np.ascontiguousarray(f(inp["ssm_b_im"])[0].transpose(1, 0, 2))
    sh["cre"] = np.ascontiguousarray(f(inp["ssm_c_re"])[0].transpose(2, 0, 1)); sh["cim"] = np.ascontiguousarray(f(inp["ssm_c_im"])[0].transpose(2, 0, 1))
    sel = np.zeros((128, 64, 128), np.float32)
    for gl in range(8):
        for s in range(8):
            for c in range(16):
                sel[gl * 16 + c, gl * 8 + s, s * 16 + c] = 1.0
    sh["sel"] = sel.reshape(128, 64 * 128)
    sh["selT"] = np.ascontiguousarray(sel.transpose(2, 1, 0)).reshape(128, 64 * 128)
    s_idx = np.arange(128) // 16
    sh["mask"] = (s_idx[:, None] <= s_idx[None, :]).astype(np.float32)
    sh["ident"] = np.eye(128, dtype=np.float32)
    d = f(inp["ssm_d"])[0].reshape(G, CG)
    sh["dcol"] = np.ascontiguousarray(np.tile(d.T, (8, 1)))
    cst = np.zeros((128, NCST), np.float32)
    for col, nm in ((C_G1PRE, "ffn1_pre_g"), (C_G1POST, "ffn1_post_g"), (C_GMPRE, "mix_pre_g"), (C_GMPOST, "mix_post_g"), (C_G2PRE, "ffn2_pre_g"), (C_G2POST, "ffn2_post_g")):
        cst[:, col:col + 32] = vec_tiles(f(inp[nm])[0], 32)
    cst[:, C_CONVW:C_CONVW + 64] = f(inp["conv_w"])[0].reshape(4, LT, 128).transpose(2, 1, 0).reshape(128, 64)
    for col, nm in ((C_CONVB, "conv_b"), (C_BRG, "b_rg"), (C_BIG, "b_ig"), (C_LAM, "lru_lambda"), (C_SSMD, "ssm_d"), (C_BGLU, "b_glu")):
        cst[:, col:col + 16] = vec_tiles(f(inp[nm])[0], 16)
    sh["cst"] = cst
    return sh


def make_core_inputs(inp, sh, c):
    f = lambda a: np.asarray(a, dtype=np.float32)
    s, hf = c // 2, c % 2
    xp = f(inp["x_prompt"])[s, hf * TP:(hf + 1) * TP, :]
    xs = f(inp["x_sample"])[NSQ * c:NSQ * (c + 1)].reshape(NSQ * TS, D)
    x = np.concatenate([xp, xs], axis=0)
    m = dict(sh)
    m["xT"] = np.ascontiguousarray(x.T).reshape(DTL, 128, NT)
    cst = sh["cst"].copy(); cst[:, C_FLAG] = float(hf); m["cst"] = cst
    sl = slice(NSQ * c, NSQ * (c + 1))
    m["h0s"] = np.ascontiguousarray(f(inp["state_lru_h"])[0, sl].reshape(NSQ, LT, 128).transpose(2, 1, 0))
    m["conv0s"] = np.ascontiguousarray(f(inp["state_conv"])[0, sl].reshape(NSQ, 3, LT, 128).transpose(3, 2, 0, 1))
    m["ssm0r"] = np.ascontiguousarray(f(inp["state_ssm_re"])[0, sl].transpose(2, 1, 0))
    m["ssm0i"] = np.ascontiguousarray(f(inp["state_ssm_im"])[0, sl].transpose(2, 1, 0))
    return m


_NC_CACHE = {}


def kernel(**inputs):
    sh = make_shared(inputs)
    in_maps = [make_core_inputs(inputs, sh, c) for c in range(8)]
    if "nc" not in _NC_CACHE:
        _NC_CACHE["nc"] = build_program()
    nc = _NC_CACHE["nc"]
    res = run_bass_kernel_spmd(nc, in_maps, core_ids=list(range(8)))
    R = res.results
    B_, S_ = 4, 2048
    yp = np.zeros((B_, S_, D), np.float32); ys = np.zeros((128, TS, D), np.float32)
    p_h = np.zeros((1, B_, DL), np.float32); p_c = np.zeros((1, B_, 3, DL), np.float32)
    p_re = np.zeros((1, B_, G, SP_), np.float32); p_im = np.zeros((1, B_, G, SP_), np.float32)
    s_h = np.zeros((1, 128, DL), np.float32); s_c = np.zeros((1, 128, 3, DL), np.float32)
    s_re = np.zeros((1, 128, G, SP_), np.float32); s_im = np.zeros((1, 128, G, SP_), np.float32)
    for c in range(8):
        s, hf = c // 2, c % 2
        y = R[c]["yT"].reshape(D, NT).T
        yp[s, hf * TP:(hf + 1) * TP] = y[:TP]
        ys[NSQ * c:NSQ * (c + 1)] = y[TP:].reshape(NSQ, TS, D)
        lh = R[c]["lruh"].transpose(2, 1, 0).reshape(17, DL)
        cv = R[c]["convo"].transpose(2, 3, 1, 0).reshape(17, 3, DL)
        sr = R[c]["ssmr"].transpose(2, 1, 0); si = R[c]["ssmi"].transpose(2, 1, 0)
        if hf == 1:
            p_h[0, s] = lh[0]; p_c[0, s] = cv[0]; p_re[0, s] = sr[0]; p_im[0, s] = si[0]
        s_h[0, NSQ * c:NSQ * (c + 1)] = lh[1:]; s_c[0, NSQ * c:NSQ * (c + 1)] = cv[1:]
        s_re[0, NSQ * c:NSQ * (c + 1)] = sr[1:]; s_im[0, NSQ * c:NSQ * (c + 1)] = si[1:]
    return (yp, ys, p_h, p_c, p_re, p_im, s_h, s_c, s_re, s_im)
```
